# Optimizing a Trainium2 kernel written in Bass

```python
import math
import jax, jax.numpy as jnp
from jax import lax
import numpy as np

D_MODEL = 2048
BATCH = 4
SEQ = 2048
DEPTH = 2
DEC_BATCH = 8
DEC_SEQ = 64
PAST_LEN = 1024

CHUNK = 64
Q_BLOCK = 128
SB_HEADS = 8
SB_HEAD_DIM = D_MODEL // 16
DIFF_HEADS = 8
DIFF_QK_DIM = D_MODEL // 32
DIFF_V_DIM = 2 * DIFF_QK_DIM
D_FF = 4 * D_MODEL
EPS = 1e-6
LAMBDA_STD = 0.1

SB_WIDTH = SB_HEADS * SB_HEAD_DIM
DIFF_QK_WIDTH = DIFF_HEADS * 2 * DIFF_QK_DIM
DIFF_V_WIDTH = DIFF_HEADS * DIFF_V_DIM
IN_COLS = 3 * SB_WIDTH + 2 * DIFF_QK_WIDTH + DIFF_V_WIDTH + 2 * D_MODEL
SPLITS = (SB_WIDTH, 2 * SB_WIDTH, 3 * SB_WIDTH,
          3 * SB_WIDTH + DIFF_QK_WIDTH,
          3 * SB_WIDTH + 2 * DIFF_QK_WIDTH,
          3 * SB_WIDTH + 2 * DIFF_QK_WIDTH + DIFF_V_WIDTH)

kernel_name = "stickbreak_diffattn_gated_hybrid_stream_step"


def rmsnorm(x, g):
    xf = x.astype(jnp.float32)
    y = xf * lax.rsqrt(jnp.mean(xf * xf, axis=-1, keepdims=True) + EPS)
    return (y * g.astype(jnp.float32)).astype(x.dtype)


def alibi_slopes():
    h = jnp.arange(1, DIFF_HEADS + 1, dtype=jnp.float32)
    return jnp.exp2(-8.0 * h / DIFF_HEADS)


def stick_breaking_block(q, q_pos, k, v, k_pos):
    z = jnp.einsum("bqhd,bkhd->bhqk", q, k).astype(jnp.float32) * (SB_HEAD_DIM ** -0.5)
    visible = k_pos[None, :] < q_pos[:, None]
    log_1m = jnp.where(visible, jax.nn.log_sigmoid(-z), 0.0)
    between = lax.cumsum(log_1m, axis=3, reverse=True) - log_1m
    w = jnp.where(visible, jnp.exp(jax.nn.log_sigmoid(z) + between), 0.0)
    return jnp.einsum("bhqk,bkhd->bqhd", w.astype(v.dtype), v)


def diff_attention_block(q, q_pos, k, v, k_pos, lam, slopes):
    s = jnp.einsum("bqhcd,bkhcd->bchqk", q, k).astype(jnp.float32) * (DIFF_QK_DIM ** -0.5)
    dist = jnp.abs(q_pos[:, None] - k_pos[None, :]).astype(jnp.float32)
    visible = (k_pos[None, :] // CHUNK) <= (q_pos[:, None] // CHUNK)
    bias = jnp.where(visible[None], -slopes[:, None, None] * dist[None], -jnp.inf)
    p = jax.nn.softmax(s + bias, axis=-1)
    w = p[:, 0] - lam * p[:, 1]
    return jnp.einsum("bhqk,bkhd->bqhd", w.astype(v.dtype), v)


def sweep_query_blocks(fn, q, q_pos):
    b, t = q.shape[0], q.shape[1]
    nb = t // Q_BLOCK
    qb = jnp.moveaxis(q.reshape((b, nb, Q_BLOCK) + q.shape[2:]), 1, 0)
    pb = q_pos.reshape(nb, Q_BLOCK)
    out = lax.map(lambda a: fn(a[0], a[1]), (qb, pb))
    out = jnp.moveaxis(out, 0, 1)
    return out.reshape((b, t) + out.shape[3:])


def hybrid_layer(x, q_pos, past, layer, norm1_g, w_in, lambda_q1, lambda_k1, lambda_q2, lambda_k2,
                 diff_norm_g, w_branch_sb, w_branch_diff, w_out, norm2_g, w_up, w_down):
    b, t, _ = x.shape
    h = rmsnorm(x, norm1_g)
    proj = h @ w_in
    q_sb, k_sb, v_sb, q_df, k_df, v_df, gates = jnp.split(proj, SPLITS, axis=-1)
    q_sb = q_sb.reshape(b, t, SB_HEADS, SB_HEAD_DIM)
    k_sb = k_sb.reshape(b, t, SB_HEADS, SB_HEAD_DIM)
    v_sb = v_sb.reshape(b, t, SB_HEADS, SB_HEAD_DIM)
    q_df = q_df.reshape(b, t, DIFF_HEADS, 2, DIFF_QK_DIM)
    k_df = k_df.reshape(b, t, DIFF_HEADS, 2 * DIFF_QK_DIM)
    v_df = v_df.reshape(b, t, DIFF_HEADS, DIFF_V_DIM)
    new_rows = (k_sb, v_sb, k_df, v_df)

    if past is None:
        ks, vs, kd, vd = new_rows
        k_pos = q_pos
    else:
        ks, vs, kd, vd = (jnp.concatenate([p, n], axis=1) for p, n in zip(past, new_rows))
        k_pos = jnp.arange(ks.shape[1], dtype=jnp.int32)
    kd = kd.reshape(kd.shape[:3] + (2, DIFF_QK_DIM))

    lam_init = 0.8 - 0.6 * math.exp(-0.3 * layer)
    lam = (jnp.exp(jnp.sum(lambda_q1.astype(jnp.float32) * lambda_k1.astype(jnp.float32)))
           - jnp.exp(jnp.sum(lambda_q2.astype(jnp.float32) * lambda_k2.astype(jnp.float32)))
           + lam_init)
    slopes = alibi_slopes()

    sb_fn = lambda qb, pb: stick_breaking_block(qb, pb, ks, vs, k_pos)
    df_fn = lambda qb, pb: diff_attention_block(qb, pb, kd, vd, k_pos, lam, slopes)
    if past is None:
        o_sb = sweep_query_blocks(sb_fn, q_sb, q_pos)
        o_df = sweep_query_blocks(df_fn, q_df, q_pos)
    else:
        o_sb = sb_fn(q_sb, q_pos)
        o_df = df_fn(q_df, q_pos)

    o_df = rmsnorm(o_df, diff_norm_g) * (1.0 - lam_init)
    g = jax.nn.sigmoid(gates)
    g_sb, g_df = jnp.split(g, 2, axis=-1)
    merged = (g_sb * (o_sb.reshape(b, t, SB_WIDTH) @ w_branch_sb)
              + g_df * (o_df.reshape(b, t, DIFF_V_WIDTH) @ w_branch_diff))
    x = x + merged @ w_out
    h2 = rmsnorm(x, norm2_g)
    x = x + jnp.square(jax.nn.relu(h2 @ w_up)) @ w_down
    return x, new_rows


def run_trunk(x, q_pos, caches, norm1_g, w_in, lambda_q1, lambda_k1, lambda_q2, lambda_k2,
              diff_norm_g, w_branch_sb, w_branch_diff, w_out, norm2_g, w_up, w_down, final_norm_g):
    rows = []
    for l in range(DEPTH):
        past = None if caches is None else tuple(c[l] for c in caches)
        x, new_rows = hybrid_layer(x, q_pos, past, l, norm1_g[l], w_in[l], lambda_q1[l], lambda_k1[l],
                                   lambda_q2[l], lambda_k2[l], diff_norm_g[l], w_branch_sb[l],
                                   w_branch_diff[l], w_out[l], norm2_g[l], w_up[l], w_down[l])
        rows.append(new_rows)
    stacked = tuple(jnp.stack([r[i] for r in rows], axis=0) for i in range(4))
    return rmsnorm(x, final_norm_g), stacked


def setup_inputs(seed: int = 0) -> dict:
    key = jax.random.key(seed)
    ks = jax.random.split(key, 20)
    f32 = jnp.float32

    def normal(k, shape):
        return jax.random.normal(k, shape, f32)

    def weight(k, shape, fan_in):
        return normal(k, shape) * (fan_in ** -0.5)

    def gain(k, shape):
        return 1.0 + 0.05 * normal(k, shape)

    return {
        "x_prompt": normal(ks[0], (BATCH, SEQ, D_MODEL)),
        "x_sample": normal(ks[1], (DEC_BATCH, DEC_SEQ, D_MODEL)),
        "cache_sb_k": normal(ks[2], (DEPTH, DEC_BATCH, PAST_LEN, SB_HEADS, SB_HEAD_DIM)),
        "cache_sb_v": normal(ks[3], (DEPTH, DEC_BATCH, PAST_LEN, SB_HEADS, SB_HEAD_DIM)),
        "cache_diff_k": normal(ks[4], (DEPTH, DEC_BATCH, PAST_LEN, DIFF_HEADS, 2 * DIFF_QK_DIM)),
        "cache_diff_v": normal(ks[5], (DEPTH, DEC_BATCH, PAST_LEN, DIFF_HEADS, DIFF_V_DIM)),
        "norm1_g": gain(ks[6], (DEPTH, D_MODEL)),
        "w_in": weight(ks[7], (DEPTH, D_MODEL, IN_COLS), D_MODEL),
        "lambda_q1": LAMBDA_STD * normal(ks[8], (DEPTH, DIFF_QK_DIM)),
        "lambda_k1": LAMBDA_STD * normal(ks[9], (DEPTH, DIFF_QK_DIM)),
        "lambda_q2": LAMBDA_STD * normal(ks[10], (DEPTH, DIFF_QK_DIM)),
        "lambda_k2": LAMBDA_STD * normal(ks[11], (DEPTH, DIFF_QK_DIM)),
        "diff_norm_g": gain(ks[12], (DEPTH, DIFF_V_DIM)),
        "w_branch_sb": weight(ks[13], (DEPTH, SB_WIDTH, D_MODEL), SB_WIDTH),
        "w_branch_diff": weight(ks[14], (DEPTH, DIFF_V_WIDTH, D_MODEL), DIFF_V_WIDTH),
        "w_out": weight(ks[15], (DEPTH, D_MODEL, D_MODEL), D_MODEL),
        "norm2_g": gain(ks[16], (DEPTH, D_MODEL)),
        "w_up": weight(ks[17], (DEPTH, D_MODEL, D_FF), D_MODEL),
        "w_down": weight(ks[18], (DEPTH, D_FF, D_MODEL), D_FF),
        "final_norm_g": gain(ks[19], (D_MODEL,)),
    }


def reference(x_prompt, x_sample, cache_sb_k, cache_sb_v, cache_diff_k, cache_diff_v,
              norm1_g, w_in, lambda_q1, lambda_k1, lambda_q2, lambda_k2, diff_norm_g,
              w_branch_sb, w_branch_diff, w_out, norm2_g, w_up, w_down, final_norm_g):
    q_pos_prompt = jnp.arange(x_prompt.shape[1], dtype=jnp.int32)
    q_pos_sample = PAST_LEN + jnp.arange(x_sample.shape[1], dtype=jnp.int32)
    y_prompt, rows_p = run_trunk(x_prompt, q_pos_prompt, None, norm1_g, w_in, lambda_q1, lambda_k1,
                                 lambda_q2, lambda_k2, diff_norm_g, w_branch_sb, w_branch_diff, w_out,
                                 norm2_g, w_up, w_down, final_norm_g)
    y_sample, rows_s = run_trunk(x_sample, q_pos_sample,
                                 (cache_sb_k, cache_sb_v, cache_diff_k, cache_diff_v),
                                 norm1_g, w_in, lambda_q1, lambda_k1, lambda_q2, lambda_k2, diff_norm_g,
                                 w_branch_sb, w_branch_diff, w_out, norm2_g, w_up, w_down, final_norm_g)
    new_sb_k_prompt, new_sb_v_prompt, new_diff_k_prompt, new_diff_v_prompt = rows_p
    new_sb_k_sample, new_sb_v_sample, new_diff_k_sample, new_diff_v_sample = rows_s
    return (y_prompt, y_sample,
            new_sb_k_prompt, new_sb_v_prompt, new_diff_k_prompt, new_diff_v_prompt,
            new_sb_k_sample, new_sb_v_sample, new_diff_k_sample, new_diff_v_sample)
```

```python
import math
import numpy as np
import ml_dtypes
import concourse.bass as bass
import concourse.mybir as mybir
from concourse.bass_utils import run_bass_kernel_spmd

F32 = mybir.dt.float32
BF16 = mybir.dt.bfloat16
I32 = mybir.dt.int32
AF = mybir.ActivationFunctionType
ALU = mybir.AluOpType

D = 2048
NCH = 16
TP = 1024
TS = 64
TO = TP + TS
SEQ = 2048
PAST = 1024
DEPTH = 2
EPS = 1e-6
TT = [(0, 512), (512, 512), (1024, 64)]
NEG = -30000.0
DEBUG = False
GROUPS = [[0, 1], [2, 3], [4, 5], [6, 7]]
SMALLW = False
STOP_AFTER = None


class _Stop(Exception):
    pass

ARENA = 106000


class Sched:
    ENG = ["pe", "act", "dve", "pool", "sp"]
    NDS = 30

    def __init__(self, plan):
        self.plan = plan
        self.ops = []
        self.lastw = {}
        self.rd = {}
        self.last_of = {}
        self.dmas_out = []
        self.last_cc = None

    def op(self, eng, fn, r=(), w=(), kind="c", extra=()):
        if self.plan:
            return -1
        i = len(self.ops)
        if eng in ("act", "dve"):
            pr = [g for g in r if isinstance(g, tuple) and g[0] == "ps"]
            if pr:
                w = list(w) + [g for g in pr if g not in w]
        deps = set(extra)
        for g in r:
            lw = self.lastw.get(g)
            if lw is not None:
                deps.add(lw)
        for g in w:
            lw = self.lastw.get(g)
            if lw is not None:
                deps.add(lw)
            deps.update(self.rd.get(g, ()))
        if kind == "x" and self.last_cc is not None:
            deps.add(self.last_cc)
        deps.discard(i)
        self.ops.append(dict(eng=eng, fn=fn, deps=deps, kind=kind))
        for g in r:
            lst = self.rd.setdefault(g, [])
            if kind == "c":
                for k in range(len(lst)):
                    if self.ops[lst[k]]["kind"] == "c" and self.ops[lst[k]]["eng"] == eng:
                        lst[k] = i
                        break
                else:
                    lst.append(i)
            else:
                lst.append(i)
        for g in w:
            self.lastw[g] = i
            self.rd[g] = []
        if kind == "c":
            self.last_of[eng] = i
        else:
            self.dmas_out.append(i)
            if kind == "x":
                self.last_cc = i
        return i

    def barrier(self):
        if self.plan:
            return
        deps = set(self.last_of.values()) | set(self.dmas_out)
        for e in self.ENG:
            i = len(self.ops)
            self.ops.append(dict(eng=e, fn=None, deps=set(deps), kind="c"))
            self.last_of[e] = i
        self.dmas_out = []
        self.lastw = {}
        self.rd = {}

    def emit(self, nc, handles, esem, dsem, ccsem):
        ops = self.ops
        for o in ops:
            o["tgt"] = False
        for i, o in enumerate(ops):
            for d in o["deps"]:
                od = ops[d]
                if od["kind"] != "c":
                    continue
                if od["eng"] == "pe" and o["eng"] == "pe" and o["kind"] == "c":
                    continue
                od["tgt"] = True
        cnt = {e: 0 for e in self.ENG}
        dcnt = {"sp": 0, "pool": 0}
        cccnt = 0
        for i, o in enumerate(ops):
            e = o["eng"]
            if o["kind"] == "c":
                if o["fn"] is None:
                    o["sig"] = ("E", e, None)
                elif o["tgt"]:
                    cnt[e] += 1
                    o["sig"] = (esem[e], cnt[e])
                else:
                    o["sig"] = None
            elif o["kind"] == "d":
                k = dcnt[e]
                dcnt[e] += 1
                s = dsem[e][k % self.NDS]
                o["sig"] = (s, 16 * (k // self.NDS + 1))
                o["prev"] = (s, 16 * (k // self.NDS)) if k >= self.NDS else None
            else:
                cccnt += 1
                o["sig"] = (ccsem, cccnt)
        per = {e: [] for e in self.ENG}
        for i, o in enumerate(ops):
            per[o["eng"]].append(i)
        waited = {e: {} for e in self.ENG}

        def run(e, h):
            wt = waited[e]
            for i in per[e]:
                o = ops[i]
                need = {}
                for d in o["deps"]:
                    od = ops[d]
                    if od["kind"] == "c":
                        if od["fn"] is None:
                            continue
                        if od["eng"] == "pe" and e == "pe" and o["kind"] == "c":
                            continue
                    sg = od["sig"]
                    if sg is None:
                        raise RuntimeError("dep on non-target op")
                    key = id(sg[0])
                    if key not in need or need[key][1] < sg[1]:
                        need[key] = sg
                if o["kind"] == "d" and o["prev"] is not None:
                    sg = o["prev"]
                    key = id(sg[0])
                    if key not in need or need[key][1] < sg[1]:
                        need[key] = sg
                for key, (s, v) in need.items():
                    if wt.get(key, 0) >= v:
                        continue
                    h.wait_ge(s, v)
                    wt[key] = v
                if o["fn"] is None:
                    continue
                ins = o["fn"](h)
                sg = o["sig"]
                if sg is not None:
                    if o["kind"] == "c":
                        ins.then_inc(sg[0], 1)
                    elif o["kind"] == "d":
                        ins.then_inc(sg[0], 16)
                    else:
                        ins.then_inc(sg[0])
        return run


def build_program():
    nc = bass.Bass("TRN2", target_bir_lowering=False)

    def din(name, shape, dt=F32):
        return nc.dram_tensor(name, list(shape), dt, kind="ExternalInput").ap()

    def dout(name, shape, dt=F32):
        return nc.dram_tensor(name, list(shape), dt, kind="ExternalOutput").ap()

    xin = din("xin", [TO, D])
    if SMALLW:
        w_my = w_in = w_bsb = w_bdf = w_out = w_up = w_down = din("w_small", [DEPTH, 8192, 128])
    else:
        w_my = din("w_my", [DEPTH, D, 3072])
        w_in = din("w_in", [DEPTH, D, 10240])
        w_bsb = din("w_bsb", [DEPTH, 1024, D])
        w_bdf = din("w_bdf", [DEPTH, 1024, D])
        w_out = din("w_out", [DEPTH, D, D])
        w_up = din("w_up", [DEPTH, D, 8192])
        w_down = din("w_down", [DEPTH, 8192, D])
    caches = [din(n, [DEPTH, PAST, 1024]) for n in ("c_sbk", "c_sbv", "c_dfk", "c_dfv")]
    gcols_d = din("gcols", [128, 80])
    dng_d = din("dng", [128, 2])
    lamv_d = din("lamv", [128, 512])
    metaf_d = din("meta_f", [128, 1])
    metai_d = din("meta_i", [1, 2], I32)

    y_d = dout("y", [TO, D])
    pouts = [dout(n, [DEPTH, SEQ, 512]) for n in ("p_sbk", "p_sbv", "p_dfk", "p_dfv")]
    souts = [dout(n, [DEPTH, TS, 1024]) for n in ("s_sbk", "s_sbv", "s_dfk", "s_dfv")]

    hsrc = [[nc.dram_tensor(f"hsrc{l}_{f}", [1024, 1024], BF16) for f in range(2)] for l in range(DEPTH)]
    hall = [[nc.dram_tensor(f"hall{l}_{f}", [2048, 1024], BF16) for f in range(2)] for l in range(DEPTH)]
    osrc = [[nc.dram_tensor(f"osrc{l}_{t}", [1024, 1024], BF16) for t in range(2)] for l in range(DEPTH)]
    oall = [[nc.dram_tensor(f"oall{l}_{t}", [2048, 1024], BF16) for t in range(2)] for l in range(DEPTH)]
    xsp = [nc.dram_tensor(f"xsp{l}", [NCH, 128, TO], F32) for l in range(DEPTH)]

    lam_init = [0.8 - 0.6 * math.exp(-0.3 * l) for l in range(DEPTH)]

    import contextlib
    es = contextlib.ExitStack()
    with es:
        arena_t = es.enter_context(nc.sbuf_tensor("arena", [128, ARENA], BF16))
        psum = [es.enter_context(nc.psum_tensor(f"ps{i}", [128, 512], F32)) for i in range(8)]
        esem = {e: es.enter_context(nc.semaphore(f"s_{e}")) for e in Sched.ENG}
        dsem = {q: [es.enter_context(nc.semaphore(f"d_{q}{i}")) for i in range(Sched.NDS)] for q in ("sp", "pool")}
        ccsem = es.enter_context(nc.semaphore("s_cc"))
        block = es.enter_context(nc.Block())

        def sb(off, shape, dt=BF16):
            n = int(np.prod(shape))
            if dt in (F32, I32):
                ap = arena_t[:, off:off + 2 * n].bitcast(dt)
            else:
                ap = arena_t[:, off:off + n]
            if len(shape) == 2:
                ap = ap.rearrange("p (a b) -> p a b", b=shape[1])
            elif len(shape) == 3:
                ap = ap.rearrange("p (a b c) -> p a b c", b=shape[1], c=shape[2])
            return ap

        class Bump:
            def __init__(self, lo, hi):
                self.lo, self.hi, self.p = lo, hi, lo

            def reset(self):
                self.p = self.lo

            def get(self, shape, dt=BF16):
                n = int(np.prod(shape)) * (2 if dt in (F32, I32) else 1)
                n = (n + 1) // 2 * 2
                off = self.p
                self.p += n
                assert self.p <= self.hi, ("arena overflow", self.p, self.hi)
                return sb(off, shape, dt)

        CONST = Bump(0, 12800)
        RING0 = 12800
        NRING = 3
        SLAB = 8192
        XT0 = RING0 + NRING * SLAB
        XT_N = NCH * TO * 2
        R2_0 = XT0 + XT_N
        R2_N = NCH * TO
        R3_0 = R2_0 + R2_N
        R3_N = ARENA - R3_0
        assert R3_N >= 15872
        RXT = Bump(XT0, XT0 + XT_N)
        R2 = Bump(R2_0, R2_0 + R2_N)
        R3 = Bump(R3_0, R3_0 + R3_N)

        ident_bf = CONST.get([128])
        ident_f = CONST.get([128], F32)
        ones_bf = CONST.get([128])
        ones_f = CONST.get([128], F32)
        negones_bf = CONST.get([128])
        negU_bf = CONST.get([128])
        negtri_bf = CONST.get([128])
        dbase_f = CONST.get([128], F32)
        diag_cur = [CONST.get([128]) for _ in range(2)]
        kaug = CONST.get([2048])
        qbase_hi = CONST.get([2048])
        qaug = [CONST.get([2048]) for _ in range(2)]
        gcols = CONST.get([80], F32)
        dng = CONST.get([2], F32)
        metaf = CONST.get([1], F32)
        slope_my = CONST.get([4], F32)
        nslope_my = CONST.get([4], F32)
        neglam = CONST.get([2], F32)
        gdn = CONST.get([2], F32)
        lamt = CONST.get([8], F32)
        hS = CONST.get([NCH, TS])
        oS = CONST.get([NCH, TS])
        lamv = R3.get([512], F32)

        S = Sched(plan=False)
        PLAN = Sched(plan=True)
        ctx = {}

        class WStream:
            def __init__(self):
                self.order = []
                self.pos = 0
                self.issued = 0
                self.released = 0
                self.planning = True

            def _issue(self, k):
                desc = self.order[k]
                slot = k % NRING
                src_fn, nck, ncol = desc
                dst = sb(RING0 + slot * SLAB, [nck, ncol])
                step = 4 if ncol <= 512 else 1
                for c0 in range(0, nck, step):
                    c1 = min(nck, c0 + step)
                    S.op("pool", (lambda e, c0=c0, c1=c1, dst=dst, src_fn=src_fn: e.dma_start(out=dst[:, c0:c1, :], in_=src_fn(c0, c1))),
                         w=[g for c in range(c0, c1) for g in rg(slot, c, ncol)], kind="d")

            def _pump(self):
                while self.issued < min(len(self.order), self.released + NRING):
                    self._issue(self.issued)
                    self.issued += 1

            def get(self, desc):
                if self.planning:
                    self.order.append(desc)
                    return None, None
                k = self.pos
                self.pos += 1
                self._pump()
                assert self.issued > k, "weight ring deadlock: slab requested before a slot was released"
                slot = k % NRING
                _, nck, ncol = desc
                return sb(RING0 + slot * SLAB, [nck, ncol]), slot

            def release(self, n=1):
                if self.planning:
                    return
                self.released += n
                self._pump()

        def rg(slot, c, ncol=512):
            per = ncol // 512
            return [("ring", slot, c * per + i) for i in range(per)]

        WS = WStream()

        def wsrc(w_ap_l, col0, ncol, row0=0):
            def f(c0, c1):
                return w_ap_l[row0 + c0 * 128: row0 + c1 * 128, col0:col0 + ncol].rearrange("(c p) n -> p c n", p=128)
            return f

        def mm(sch, out, lhsT, rhs, start, stop, r, w):
            sch.op("pe", lambda e: e.matmul(out, lhsT, rhs, start=start, stop=stop, skip_group_check=True), r=r, w=w)

        def tr(sch, out, in_, ident, r, w):
            sch.op("pe", lambda e: e.transpose(out, in_, ident), r=r, w=w)

        def act(sch, out, in_, func, r, w, scale=1.0, bias=0.0):
            sch.op("act", lambda e: e.activation(out, in_, func, bias=bias, scale=scale), r=r, w=w)

        def dcopy(sch, out, in_, r, w):
            sch.op("dve", lambda e: e.tensor_copy(out, in_), r=r, w=w)

        def dtt(sch, out, a, b, op, r, w):
            sch.op("dve", lambda e: e.tensor_tensor(out, a, b, op), r=r, w=w)

        def dstt(sch, out, in0, scalar, in1, op0, op1, r, w):
            sch.op("dve", lambda e: e.scalar_tensor_tensor(out, in0, scalar, in1, op0, op1), r=r, w=w)

        def dts(sch, out, in0, s1, s2, op0, op1, r, w):
            sch.op("dve", lambda e: e.tensor_scalar(out, in0, s1, s2, op0, op1), r=r, w=w)

        def dmemset(sch, ap, val, w):
            sch.op("dve", lambda e: e.memset(ap, val), w=w)

        def dma(sch, q, out, in_, r, w):
            return sch.op(q, lambda e: e.dma_start(out=out, in_=in_), r=r, w=w, kind="d")

        psc = [0]

        def ps_next():
            i = psc[0] % 6
            psc[0] += 1
            return i

        def psbf(i):
            return psum[i][:, :].bitcast(BF16)

        def build(sch):
            planning = sch.plan
            WS.planning = planning
            WS.pos = 0
            WS.issued = 0
            psc[0] = 0

            if not planning:
                P = lambda fn, w=(), r=(): sch.op("pool", fn, r=r, w=w)
                P(lambda e: e.memset(ident_bf, 0.0), w=["c_identbf"])
                P(lambda e: e.affine_select(out=ident_bf, in_=ident_bf, pattern=[[-1, 128]], compare_op=ALU.not_equal,
                                            fill=1.0, base=0, channel_multiplier=1), r=["c_identbf"], w=["c_identbf"])
                P(lambda e: e.memset(ident_f, 0.0), w=["c_identf"])
                P(lambda e: e.affine_select(out=ident_f, in_=ident_f, pattern=[[-1, 128]], compare_op=ALU.not_equal,
                                            fill=1.0, base=0, channel_multiplier=1), r=["c_identf"], w=["c_identf"])
                P(lambda e: e.memset(ones_bf, 1.0), w=["c_ones"])
                P(lambda e: e.memset(ones_f, 1.0), w=["c_onesf"])
                P(lambda e: e.memset(negones_bf, -1.0), w=["c_negones"])
                P(lambda e: e.memset(negU_bf, -1.0), w=["c_negU"])
                P(lambda e: e.affine_select(out=negU_bf, in_=negU_bf, pattern=[[-1, 128]], compare_op=ALU.is_ge,
                                            fill=0.0, base=0, channel_multiplier=1), r=["c_negU"], w=["c_negU"])
                P(lambda e: e.memset(negtri_bf, 0.0), w=["c_negtri"])
                P(lambda e: e.affine_select(out=negtri_bf, in_=negtri_bf, pattern=[[1, 128]], compare_op=ALU.is_gt,
                                            fill=NEG, base=0, channel_multiplier=-1), r=["c_negtri"], w=["c_negtri"])
                RXT.reset()
                dbase_i = RXT.get([128], I32)
                P(lambda e: e.iota(dbase_i, [[-1, 128]], base=0, channel_multiplier=1), w=["s_dbasei"])
                P(lambda e: e.tensor_copy(dbase_f, dbase_i), r=["s_dbasei"], w=["c_dbase"])
                P(lambda e: e.tensor_scalar(dbase_f, dbase_f, 0.0, -16.0, ALU.max, ALU.mult), r=["c_dbase"], w=["c_dbase"])
                rows_i = RXT.get([2, 2048], I32)
                rows = RXT.get([2, 2048], F32)
                rows_bf = RXT.get([4, 2048])
                qrows_bf = RXT.get([4, 2048])
                P(lambda e: e.iota(rows_i[0:1, 0, :], [[512, 32], [0, 64]], base=0, channel_multiplier=0), w=["s_rowsi"])
                P(lambda e: e.iota(rows_i[0:1, 1, :], [[0, 32], [8, 64]], base=0, channel_multiplier=0), r=["s_rowsi"], w=["s_rowsi"])
                P(lambda e: e.tensor_copy(rows[0:1, :, :], rows_i[0:1, :, :]), r=["s_rowsi"], w=["s_rows"])
                P(lambda e: e.tensor_copy(rows_bf[0:1, 0:2, :], rows[0:1, :, :]), r=["s_rows"], w=["s_rowsbf"])
                P(lambda e: e.memset(rows_bf[0:1, 2:4, :], 1.0), r=["s_rowsbf"], w=["s_rowsbf"])
                P(lambda e: e.memset(qrows_bf[0:1, 0:2, :], 1.0), w=["s_qrows"])
                P(lambda e: e.tensor_scalar(qrows_bf[0:1, 2:4, :], rows[0:1, 0:2, :], -1.0, None, ALU.mult),
                  r=["s_rows", "s_qrows"], w=["s_qrows"])
                P(lambda e: e.memset(kaug, 0.0), w=["c_kaug"])
                P(lambda e: e.memset(qbase_hi, 0.0), w=["c_qbase"])
                for base in (0,):
                    for rr in range(4):
                        sch.op("sp", lambda e, base=base, rr=rr: e.dma_start(out=kaug[base + rr:base + rr + 1, :],
                                                                              in_=rows_bf[0:1, rr, :]),
                               r=["s_rowsbf", "c_kaug"], w=[("c_kaug_r", base, rr)], kind="d")
                        sch.op("sp", lambda e, base=base, rr=rr: e.dma_start(out=qbase_hi[base + rr:base + rr + 1, :],
                                                                              in_=qrows_bf[0:1, rr, :]),
                               r=["s_qrows", "c_qbase"], w=[("c_qbase_r", base, rr)], kind="d")
                dma(sch, "sp", gcols, gcols_d[:, :], r=[], w=["c_gcols"])
                dma(sch, "sp", dng, dng_d[:, :], r=[], w=["c_dng"])
                dma(sch, "sp", metaf, metaf_d[:, :], r=[], w=["c_metaf"])
                dma(sch, "sp", lamv, lamv_d[:, :], r=[], w=["s_lamv"])
                for j in range(4):
                    dts(sch, slope_my[:, j:j + 1], metaf, -0.9375, 1.0, ALU.mult, ALU.add, r=["c_metaf"], w=[("c_slope", j)])
                    dts(sch, slope_my[:, j:j + 1], slope_my[:, j:j + 1], 2.0 ** -(j + 1), None, ALU.mult, ALU.bypass,
                        r=[("c_slope", j)], w=[("c_slope", j)])
                lprod = R3.get([64], F32)
                for l in range(DEPTH):
                    for i in range(2):
                        a = lamv[:, (l * 4 + 2 * i) * 64:(l * 4 + 2 * i + 1) * 64]
                        b = lamv[:, (l * 4 + 2 * i + 1) * 64:(l * 4 + 2 * i + 2) * 64]
                        sch.op("dve", lambda e, a=a, b=b, l=l, i=i: e.scalar_tensor_tensor(
                            lprod, a, 1.0, b, ALU.mult, ALU.mult, accum_out=lamt[:, l * 2 + i:l * 2 + i + 1]),
                            r=["s_lamv"], w=["s_lprod", ("c_lamt", l, i)])
                    act(sch, lamt[:, 4 + l * 2:4 + l * 2 + 2], lamt[:, l * 2:l * 2 + 2], AF.Exp,
                        r=[("c_lamt", l, 0), ("c_lamt", l, 1)], w=[("c_lame", l)])
                    dstt(sch, neglam[:, l:l + 1], lamt[:, 4 + l * 2 + 1:4 + l * 2 + 2], -lam_init[l],
                         lamt[:, 4 + l * 2:4 + l * 2 + 1], ALU.add, ALU.subtract, r=[("c_lame", l)], w=[("c_neglam", l)])
                    dts(sch, gdn[:, l:l + 1], dng[:, l:l + 1], 1.0 - lam_init[l], None, ALU.mult, ALU.bypass,
                        r=["c_dng"], w=[("c_gdn", l)])
                sch.barrier()
                R3.reset()
            ctx["stop"]("setup", sch)

            CG = ["c_consts"]

            XT = sb(XT0, [NCH, TO], F32)

            def gXT(c, t):
                return ("XT", c, t)

            def norm_to(sch, l_g, dst, dstname, sq_bump):
                for t, (t0, tn) in enumerate(TT):
                    pi = ps_next()
                    ps = psum[pi][:, 0:tn]
                    for cg in range(4):
                        sq = sqbuf[cg % 2][:, :, 0:tn]
                        act(sch, sq, XT[:, 4 * cg:4 * cg + 4, t0:t0 + tn], AF.Square,
                            r=[gXT(c, t) for c in range(4 * cg, 4 * cg + 4)], w=[("sq", cg % 2)])
                        for k in range(4):
                            c = 4 * cg + k
                            mm(sch, ps, ones_bf, sq[:, k, :], start=(c == 0), stop=(c == NCH - 1),
                               r=[("sq", cg % 2)], w=[("ps", pi)])
                    rs = rstd_t[:, 0:tn]
                    act(sch, rs, ps, AF.Ln, r=[("ps", pi)], w=["rstd"], scale=1.0 / D, bias=EPS)
                    act(sch, rs, rs, AF.Exp, r=["rstd"], w=["rstd"], scale=-0.5)
                    for c in range(NCH):
                        dstt(sch, dst(c, t0, tn), XT[:, c, t0:t0 + tn], gcols[:, l_g * 16 + c:l_g * 16 + c + 1], rs,
                             ALU.mult, ALU.mult, r=[gXT(c, t), "rstd"], w=[(dstname, c, t)])

            for l in range(DEPTH):
                R2.reset(); R3.reset()
                hT = R2.get([NCH, TO])
                sqbuf = [R3.get([4, 512]) for _ in range(2)]
                rstd_t = R3.get([512], F32)
                if l == 0:
                    xt = [R3.get([D], F32) for _ in range(2)]
                    ntile = 9
                    for i in range(ntile):
                        n = 128 if i < 8 else TS
                        t = i // 4 if i < 8 else 2
                        xb = xt[i % 2]
                        dma(sch, "sp", xb[0:n, :], xin[i * 128:i * 128 + n, :], r=[], w=[("xt", i % 2)])
                        for b4 in range(4):
                            pi = ps_next()
                            for k in range(4):
                                c = 4 * b4 + k
                                tr(sch, psum[pi][:, k * 128:k * 128 + n], xb[0:n, c * 128:(c + 1) * 128], ident_f[0:n, 0:n],
                                   r=[("xt", i % 2)], w=[("ps", pi)])
                            src = psum[pi][:, :].rearrange("p (a b) -> p a b", b=128)[:, :, 0:n]
                            dstap = XT[:, 4 * b4:4 * b4 + 4, i * 128:i * 128 + n]
                            if b4 % 2 == 0:
                                act(sch, dstap, src, AF.Copy, r=[("ps", pi)], w=[("XTp", c, i) for c in range(4 * b4, 4 * b4 + 4)])
                            else:
                                dcopy(sch, dstap, src, r=[("ps", pi)], w=[("XTp", c, i) for c in range(4 * b4, 4 * b4 + 4)])
                    sch.barrier()
                norm_to(sch, l, lambda c, t0, tn: hT[:, c, t0:t0 + tn], "hT", None)
                dcopy(sch, hS, hT[:, :, TP:TO], r=[("hT", c, 2) for c in range(NCH)], w=["hS"])
                for c in range(NCH):
                    dma(sch, "sp", xsp[l].ap()[c, :, :], XT[:, c, :], r=[gXT(c, t) for t in range(3)], w=[("xsp", l, c)])
                for f in range(2):
                    dma(sch, "sp", hsrc[l][f].ap().rearrange("(c p) n -> p c n", p=128), hT[:, 8 * f:8 * f + 8, 0:TP],
                        r=[("hT", c, t) for c in range(8 * f, 8 * f + 8) for t in range(2)], w=[("hsrc", l, f)])
                for f in range(2):
                    sch.op("pool", lambda e, f=f, l=l: e.collective_compute(
                        "AllGather", ALU.bypass, replica_groups=GROUPS,
                        ins=[hsrc[l][f].ap().opt()], outs=[hall[l][f].ap().opt()]),
                        r=[("hsrc", l, f)], w=[("hall", l, f)], kind="x")
                sch.barrier()
                ctx["stop"](f"p1_{l}", sch)

                for ty in range(2):
                    RXT.reset(); R2.reset(); R3.reset()
                    QT = RXT.get([4, SEQ])
                    KT = RXT.get([4, SEQ])
                    V = RXT.get([16, 512])
                    QsT = RXT.get([8, TS])
                    KsT = RXT.get([8, 128])
                    Vs = RXT.get([1024])
                    if not planning:
                        dmemset(sch, KsT, 0.0, w=[("KsT", h_) for h_ in range(8)])
                        dmemset(sch, Vs, 0.0, w=[("Vs", 0), ("Vs", 1)])
                    hring = [R2.get([NCH, 512]) for _ in range(2)]
                    stage = [R3.get([512], F32) for _ in range(2)]
                    otile = [R3.get([SEQ]) for _ in range(2)]
                    sstage = R3.get([1024], F32)
                    pk_out, pv_out = pouts[2 * ty], pouts[2 * ty + 1]
                    sk_out, sv_out = souts[2 * ty], souts[2 * ty + 1]
                    qscale = (128.0 ** -0.5) if ty == 0 else 1.0

                    slabs = []
                    for s in range(3):
                        slabs.append(WS.get((wsrc(w_my[l], (3 * ty + s) * 512, 512), NCH, 512)))
                    if not planning:
                        (wq, sq_), (wk, sk_), (wv, sv_) = slabs
                        for t in range(4):
                            hb = hring[t % 2]
                            r_, half = t // 2, t % 2
                            for f in range(2):
                                dma(sch, "sp", hb[:, 8 * f:8 * f + 8, :],
                                    hall[l][f].ap()[r_ * 1024:(r_ + 1) * 1024, half * 512:(half + 1) * 512].rearrange("(c p) n -> p c n", p=128),
                                    r=[("hall", l, f)], w=[("hring", t % 2, f)])
                            rh = [("hring", t % 2, 0), ("hring", t % 2, 1)]
                            for j in range(4):
                                for which, (wslab, slot), dstT in ((0, (wq, sq_), QT), (1, (wk, sk_), KT)):
                                    pi = ps_next()
                                    for c in range(NCH):
                                        mm(sch, psum[pi][:, :], wslab[:, c, j * 128:(j + 1) * 128], hb[:, c, :],
                                           start=(c == 0), stop=(c == NCH - 1), r=rh + rg(slot, c), w=[("ps", pi)])
                                    if which == 0:
                                        act(sch, dstT[:, j, t * 512:(t + 1) * 512], psum[pi][:, :], AF.Copy, scale=qscale,
                                            r=[("ps", pi)], w=[("QT", j, t)])
                                    else:
                                        dcopy(sch, dstT[:, j, t * 512:(t + 1) * 512], psum[pi][:, :], r=[("ps", pi)], w=[("KT", j, t)])
                            if t == 0:
                                ctx["stop"]("p2a1", sch)
                            for b in range(4):
                                blk = 4 * t + b
                                for which, (wslab, slot) in ((1, (wk, sk_)), (2, (wv, sv_))):
                                    pi = ps_next()
                                    for c in range(NCH):
                                        mm(sch, psum[pi][:, :], hb[:, c, b * 128:(b + 1) * 128], wslab[:, c, :],
                                           start=(c == 0), stop=(c == NCH - 1), r=rh + rg(slot, c), w=[("ps", pi)])
                                    st = stage[which - 1]
                                    act(sch, st, psum[pi][:, :], AF.Copy, r=[("ps", pi)], w=[("stage", which)])
                                    if which == 2:
                                        dcopy(sch, V[:, blk, :], psum[pi][:, :], r=[("ps", pi)], w=[("V", blk)])
                                    dma(sch, "sp", (pk_out if which == 1 else pv_out)[l, blk * 128:(blk + 1) * 128, :], st,
                                        r=[("stage", which)], w=[("pout", ty, which, blk)])
                    WS.release(3)
                    ctx["stop"]("p2a3", sch)
                    for s in range(3):
                        for hf in range(2):
                            wslab, slot = WS.get((wsrc(w_in[l], (3 * ty + s) * 1024 + hf * 512, 512), NCH, 512))
                            if not planning:
                                if s < 2:
                                    for jj in range(4):
                                        h = hf * 4 + jj
                                        pi = ps_next()
                                        for c in range(NCH):
                                            mm(sch, psum[pi][:, 0:TS], wslab[:, c, jj * 128:(jj + 1) * 128], hS[:, c, :],
                                               start=(c == 0), stop=(c == NCH - 1), r=["hS"] + rg(slot, c), w=[("ps", pi)])
                                        if s == 0:
                                            act(sch, QsT[:, h, :], psum[pi][:, 0:TS], AF.Copy, scale=qscale, r=[("ps", pi)], w=[("QsT", h)])
                                        else:
                                            dcopy(sch, KsT[:, h, 0:TS], psum[pi][:, 0:TS], r=[("ps", pi)], w=[("KsT", h)])
                                if s >= 1:
                                    pi = ps_next()
                                    for c in range(NCH):
                                        mm(sch, psum[pi][0:TS, :], hS[:, c, :], wslab[:, c, :],
                                           start=(c == 0), stop=(c == NCH - 1), r=["hS"] + rg(slot, c), w=[("ps", pi)])
                                    act(sch, sstage[0:TS, hf * 512:(hf + 1) * 512], psum[pi][0:TS, :], AF.Copy,
                                        r=[("ps", pi)], w=[("sstage", hf)])
                                    if s == 2:
                                        dcopy(sch, Vs[0:TS, hf * 512:(hf + 1) * 512], psum[pi][0:TS, :], r=[("ps", pi)], w=[("Vs", hf)])
                                    if hf == 1:
                                        dma(sch, "sp", (sk_out if s == 1 else sv_out)[l, :, :], sstage[0:TS, :],
                                            r=[("sstage", 0), ("sstage", 1)], w=[("sout", ty, s)])
                            WS.release()
                    ctx["stop"](f"p2a_{l}_{ty}", sch)

                    if planning:
                        continue

                    sc = Bump(RXT.p, XT0 + XT_N)
                    if ty == 0:
                        e_t = [sc.get([512], F32) for _ in range(2)]
                        l1_t = [sc.get([512]) for _ in range(2)]
                        a_t = sc.get([512])
                        w_t = [sc.get([512]) for _ in range(2)]
                    else:
                        E_t = [[sc.get([512]) for _ in range(2)] for _ in range(2)]
                        z_t = [sc.get([512], F32) for _ in range(2)]
                        ep = [R3.get([512], F32) for _ in range(3)]
                        qpad = [R3.get([SEQ]) for _ in range(2)]
                    uid = [0]

                    def attend(qsrc, nq, blocks, out_ap, out_w, head_consts):
                        uid[0] += 1
                        u = uid[0]
                        if ty == 0:
                            po = 6 + (u % 2)
                            dmemset(sch, a_t[:, 0:nq], 0.0, w=["a_t"])
                            for bi, bk in enumerate(blocks):
                                nk, d0, diag = bk["nk"], bk["d0"], bk["diag"]
                                n = nq - d0
                                p1 = ps_next()
                                sps = psum[p1][0:nk, d0:nq]
                                rr = bk["r"] + qsrc["r"]
                                mm(sch, sps, bk["k"](slice(0, 128)), qsrc["ap"](slice(0, 128))[:, d0:nq], True, not diag, r=rr, w=[("ps", p1)])
                                if diag:
                                    mm(sch, psum[p1][0:nk, d0:d0 + min(nk, nq - d0)], ident_bf[0:nk, 0:nk], negtri_bf[0:nk, 0:min(nk, nq - d0)], False, True,
                                       r=[], w=[("ps", p1)])
                                eb = e_t[bi % 2][0:nk, d0:nq]
                                lb = l1_t[bi % 2][0:nk, d0:nq]
                                act(sch, eb, sps, AF.Exp, r=[("ps", p1)], w=[("e_t", bi % 2)])
                                act(sch, lb, eb, AF.Ln, r=[("e_t", bi % 2)], w=[("l1_t", bi % 2)], bias=1.0)
                                p2 = ps_next()
                                tps = psum[p2][0:nk, d0:nq]
                                mm(sch, tps, bk["k"](slice(0, 128)), qsrc["ap"](slice(0, 128))[:, d0:nq], True, False, r=rr, w=[("ps", p2)])
                                last = (bi == 0) and not diag
                                mm(sch, tps, negU_bf[0:nk, 0:nk], lb, False, last, r=[("l1_t", bi % 2)], w=[("ps", p2)])
                                if bi > 0:
                                    mm(sch, tps, negones_bf[:, 0:nk], a_t[:, d0:nq], False, not diag, r=["a_t"], w=[("ps", p2)])
                                if diag:
                                    mm(sch, psum[p2][0:nk, d0:d0 + min(nk, nq - d0)], ident_bf[0:nk, 0:nk], negtri_bf[0:nk, 0:min(nk, nq - d0)], False, True,
                                       r=[], w=[("ps", p2)])
                                wb = w_t[bi % 2][0:nk, d0:nq]
                                act(sch, wb, tps, AF.Exp, r=[("ps", p2)], w=[("w_t", bi % 2)])
                                mm(sch, psum[po][:, d0:nq], bk["v"], wb, bi == 0, bi == len(blocks) - 1,
                                   r=bk["rv"] + [("w_t", bi % 2)], w=[("ps", po)])
                                if bi < len(blocks) - 1:
                                    dtt(sch, a_t[0:nk, d0:nq], a_t[0:nk, d0:nq], lb, ALU.add, r=["a_t", ("l1_t", bi % 2)], w=["a_t"])
                            act(sch, out_ap, psum[po][:, 0:nq], AF.Copy, r=[("ps", po)], w=out_w)
                        else:
                            po = [6, 7]
                            qa, dg = head_consts
                            for m in range(2):
                                dmemset(sch, z_t[m][:, 0:nq], 0.0, w=[("z_t", m)])
                            for bi, bk in enumerate(blocks):
                                nk, d0, diag = bk["nk"], bk["d0"], bk["diag"]
                                for m in range(2):
                                    rows = slice(0, 128)
                                    p1 = ps_next()
                                    sps = psum[p1][0:nk, d0:nq]
                                    mm(sch, sps, bk["k"](rows), qsrc["pad"][m][:, d0:nq], True, False, r=bk["r"] + qsrc["rpad"], w=[("ps", p1)])
                                    mm(sch, sps, bk["ka"](rows), qa["ap"](rows)[:, qsrc["p0"] + d0:qsrc["p0"] + nq], False, not diag,
                                       r=qa["r"], w=[("ps", p1)])
                                    if diag:
                                        mm(sch, psum[p1][0:nk, d0:d0 + min(nk, nq - d0)], ident_bf[0:nk, 0:nk], dg["ap"][0:nk, 0:min(nk, nq - d0)], False, True,
                                           r=dg["r"], w=[("ps", p1)])
                                    Eb = E_t[m][bi % 2][0:nk, d0:nq]
                                    act(sch, Eb, sps, AF.Exp, scale=0.125, r=[("ps", p1)], w=[("E_t", m, bi % 2)])
                                    mm(sch, psum[po[m]][:, d0:nq], bk["v"], Eb, bi == 0, bi == len(blocks) - 1,
                                       r=bk["rv"] + [("E_t", m, bi % 2)], w=[("ps", po[m])])
                                    dtt(sch, z_t[m][0:nk, d0:nq], z_t[m][0:nk, d0:nq], Eb, ALU.add,
                                        r=[("z_t", m), ("E_t", m, bi % 2)], w=[("z_t", m)])
                            for m in range(2):
                                pz = ps_next()
                                mm(sch, psum[pz][:, 0:nq], ones_f, z_t[m][:, 0:nq], True, True, r=[("z_t", m)], w=[("ps", pz)])
                                rz = ep[m][:, 0:nq]
                                act(sch, rz, psum[pz][:, 0:nq], AF.Ln, r=[("ps", pz)], w=[("ep", m)])
                                act(sch, rz, rz, AF.Exp, scale=-1.0, r=[("ep", m)], w=[("ep", m)])
                                dtt(sch, rz, psum[po[m]][:, 0:nq], rz, ALU.mult, r=[("ps", po[m]), ("ep", m)], w=[("ep", m)])
                            o_ = ep[0][:, 0:nq]
                            dstt(sch, o_, ep[1][:, 0:nq], neglam[:, l:l + 1], o_, ALU.mult, ALU.add, r=[("ep", 0), ("ep", 1)], w=[("ep", 0)])
                            sqo = ep[2][:, 0:nq]
                            act(sch, sqo, o_, AF.Square, r=[("ep", 0)], w=[("ep", 2)])
                            pz = ps_next()
                            mm(sch, psum[pz][:, 0:nq], ones_f, sqo, True, True, r=[("ep", 2)], w=[("ps", pz)])
                            rs = ep[1][:, 0:nq]
                            act(sch, rs, psum[pz][:, 0:nq], AF.Ln, scale=1.0 / 128, bias=EPS, r=[("ps", pz)], w=[("ep", 1)])
                            act(sch, rs, rs, AF.Exp, scale=-0.5, r=[("ep", 1)], w=[("ep", 1)])
                            dstt(sch, out_ap, o_, gdn[:, l:l + 1], rs, ALU.mult, ALU.mult, r=[("ep", 0), ("ep", 1)], w=out_w)

                    def set_head_consts(slope_ap, slope_imm, buf):
                        if ty == 0:
                            return None
                        qa, dgc = qaug[buf], diag_cur[buf]
                        if slope_ap is not None:
                            dts(sch, qa, qbase_hi, slope_ap, None, ALU.mult, ALU.bypass, r=[], w=[("qaug", buf)])
                            dts(sch, dgc, dbase_f, slope_ap, None, ALU.mult, ALU.bypass, r=[], w=[("diagc", buf)])
                        else:
                            dts(sch, qa, qbase_hi, slope_imm, None, ALU.mult, ALU.bypass, r=[], w=[("qaug", buf)])
                            dts(sch, dgc, dbase_f, slope_imm, None, ALU.mult, ALU.bypass, r=[], w=[("diagc", buf)])
                        dmemset(sch, dgc[64:128, 0:64], 8 * NEG, w=[("diagc", buf)])
                        return (dict(ap=lambda rows, qa=qa: qa[rows, :], r=[("qaug", buf)]), dict(ap=dgc, r=[("diagc", buf)]))

                    def make_qpad(src_fn, n, rname):
                        if ty == 0:
                            return
                        dcopy(sch, qpad[0][0:64, 0:n], src_fn(slice(0, 64)), r=rname, w=["qpad0"])
                        dmemset(sch, qpad[0][64:128, 0:n], 0.0, w=["qpad0"])
                        dmemset(sch, qpad[1][0:64, 0:n], 0.0, w=["qpad1"])
                        dcopy(sch, qpad[1][64:128, 0:n], src_fn(slice(64, 128)), r=rname, w=["qpad1"])

                    for j in range(4):
                        hc = set_head_consts(slope_my[:, j:j + 1], None, j % 2)
                        ot = otile[j % 2]
                        make_qpad(lambda rows, j=j: QT[rows, j, :], SEQ, [("QT", j, t_) for t_ in range(4)])
                        for qt in range(4):
                            q0 = 512 * qt
                            blocks = []
                            for kb in range(4 * qt + 3, -1, -1):
                                d0 = max(0, (kb - 4 * qt) * 128)
                                blocks.append(dict(
                                    k=(lambda rows, kb=kb, j=j: KT[rows, j, kb * 128:(kb + 1) * 128]),
                                    ka=(lambda rows, kb=kb: kaug[rows, kb * 128:(kb + 1) * 128]),
                                    v=V[:, kb, j * 128:(j + 1) * 128], nk=128, d0=d0, diag=(kb >= 4 * qt),
                                    r=[("KT", j, kb // 4)], rv=[("V", kb)]))
                            qsrc = dict(ap=(lambda rows, j=j, q0=q0: QT[rows, j, q0:q0 + 512]), r=[("QT", j, qt)], p0=q0)
                            if ty == 1:
                                qsrc["pad"] = [qpad[m_][:, q0:q0 + 512] for m_ in range(2)]
                                qsrc["rpad"] = ["qpad0", "qpad1"]
                            attend(qsrc, 512, blocks, ot[:, q0:q0 + 512], [("otile", j % 2, qt)], hc)
                            if DEBUG and ty == 1 and j == 0 and qt == 0:
                                D_ = ctx["dbg"]
                                D_(sch, "kaug", kaug, [128, 2048], BF16)
                                D_(sch, "qaug", qaug[0], [128, 2048], BF16)
                                D_(sch, "diag", diag_cur[0], [128, 128], BF16)
                                D_(sch, "dbase", dbase_f, [128, 128])
                                D_(sch, "slope", slope_my, [128, 4])
                                D_(sch, "neglam", neglam, [128, 2])
                                D_(sch, "gdn", gdn, [128, 2])
                                D_(sch, "lamt", lamt, [128, 8])
                                D_(sch, "z0", z_t[0], [128, 512])
                                D_(sch, "z1", z_t[1], [128, 512])
                                D_(sch, "ep0", ep[0], [128, 512])
                                D_(sch, "ep1", ep[1], [128, 512])
                                D_(sch, "ep2", ep[2], [128, 512])
                                D_(sch, "E00", E_t[0][0], [128, 512], BF16)
                                D_(sch, "ot", ot, [128, 2048], BF16)
                                D_(sch, "QT", QT[:, 0, :], [128, 2048], BF16)
                                D_(sch, "KT", KT[:, 0, :], [128, 2048], BF16)
                                ctx["stop"]("dbg_df", sch)
                        for h2 in range(2):
                            dma(sch, "sp", osrc[l][ty].ap()[h2 * 512 + j * 128:h2 * 512 + (j + 1) * 128, :], ot[:, h2 * 1024:(h2 + 1) * 1024],
                                r=[("otile", j % 2, 2 * h2), ("otile", j % 2, 2 * h2 + 1)], w=[("osrc", l, ty, j, h2)])
                    sch.op("pool", lambda e, l=l, ty=ty: e.collective_compute(
                        "AllGather", ALU.bypass, replica_groups=GROUPS,
                        ins=[osrc[l][ty].ap().opt()], outs=[oall[l][ty].ap().opt()]),
                        r=[("osrc", l, ty, j, h2) for j in range(4) for h2 in range(2)], w=[("oall", l, ty)], kind="x")

                    sch.barrier()
                    ctx["stop"](f"p2b_{l}_{ty}", sch)
                    CB = Bump(XT0, XT0 + 24576)
                    Kc = CB.get([8, 1024])
                    Vc = CB.get([8, 1024])
                    KcT = CB.get([8, 1024])
                    for b in range(8):
                        sch.op("pool", lambda e, b=b, l=l, ty=ty, Kc=Kc: e.dma_start(out=Kc[:, b, :], in_=caches[2 * ty][l, b * 128:(b + 1) * 128, :]),
                               w=[("Kc", b)], kind="d")
                        sch.op("pool", lambda e, b=b, l=l, ty=ty, Vc=Vc: e.dma_start(out=Vc[:, b, :], in_=caches[2 * ty + 1][l, b * 128:(b + 1) * 128, :]),
                               w=[("Vc", b)], kind="d")
                    ctx["stop"]("c1", sch)
                    for h in range(8):
                        for b4 in range(2):
                            pi = ps_next()
                            pb = psbf(pi)
                            for k in range(4):
                                b = 4 * b4 + k
                                tr(sch, pb[:, k * 128:(k + 1) * 128], Kc[:, b, h * 128:(h + 1) * 128], ident_bf, r=[("Kc", b)], w=[("ps", pi)])
                            if b4 == 0:
                                dcopy(sch, KcT[:, h, 0:512], pb[:, 0:512], r=[("ps", pi)], w=[("KcT", h, 0)])
                            else:
                                act(sch, KcT[:, h, 512:1024], pb[:, 0:512], AF.Copy, r=[("ps", pi)], w=[("KcT", h, 1)])
                    ctx["stop"]("c2", sch)
                    for h in range(8):
                        hc = set_head_consts(None, 2.0 ** -(h + 1), h % 2)
                        make_qpad(lambda rows, h=h: QsT[rows, h, :], TS, [("QsT", h)])
                        blocks = [dict(k=(lambda rows, h=h: KsT[rows, h, :]), ka=(lambda rows: kaug[rows, PAST:PAST + 128]),
                                       v=Vs[:, h * 128:(h + 1) * 128], nk=128, d0=0, diag=True, r=[("KsT", h)], rv=[("Vs", h // 4)])]
                        for b in range(7, -1, -1):
                            blocks.append(dict(k=(lambda rows, h=h, b=b: KcT[rows, h, b * 128:(b + 1) * 128]),
                                               ka=(lambda rows, b=b: kaug[rows, b * 128:(b + 1) * 128]),
                                               v=Vc[:, b, h * 128:(h + 1) * 128], nk=128, d0=0, diag=False,
                                               r=[("KcT", h, b // 4)], rv=[("Vc", b)]))
                        qsrc = dict(ap=(lambda rows, h=h: QsT[rows, h, :]), r=[("QsT", h)], p0=PAST)
                        if ty == 1:
                            qsrc["pad"] = [qpad[m_][:, 0:TS] for m_ in range(2)]
                            qsrc["rpad"] = ["qpad0", "qpad1"]
                        attend(qsrc, TS, blocks, oS[:, ty * 8 + h, :], [("oS", ty, h)], hc)
                        if h == 0:
                            ctx["stop"]("c3", sch)
                        if h == 1:
                            ctx["stop"]("c4", sch)
                        if h == 7:
                            ctx["stop"]("c5", sch)
                    sch.barrier()
                    ctx["stop"](f"p2_{l}_{ty}", sch)

                RXT.reset(); R2.reset(); R3.reset()
                if not planning:
                    hTo = RXT.get([NCH, TO])
                    oT = RXT.get([NCH, TO])
                    mT = R2.get([NCH, TO])
                    gtmp = [[R3.get([512], F32) for _ in range(3)] for _ in range(2)]
                    for f in range(2):
                        dma(sch, "sp", hTo[:, 8 * f:8 * f + 8, 0:TP], hsrc[l][f].ap().rearrange("(c p) n -> p c n", p=128),
                            r=[("hsrc", l, f)], w=[("hTo", f)])
                    dcopy(sch, hTo[:, :, TP:TO], hS, r=["hS"], w=["hTo_s"])
                    dcopy(sch, oT[:, :, TP:TO], oS, r=[("oS", t_, h_) for t_ in range(2) for h_ in range(8)], w=["oT_s"])
                    for ty in range(2):
                        for r_ in range(2):
                            sch.op("pool", lambda e, ty=ty, r_=r_, l=l, oT=oT: e.dma_start(
                                out=oT[:, ty * 8 + 4 * r_:ty * 8 + 4 * r_ + 4, 0:TP],
                                in_=oall[l][ty].ap()[bass.ds(ctx["off"][r_], 512), :].rearrange("(j p) n -> p j n", p=128)),
                                r=[("oall", l, ty)], w=[("oT", ty, r_)], kind="d")
                    r_h = [("hTo", 0), ("hTo", 1), "hTo_s"]
                    r_o = [("oT", t_, r_) for t_ in range(2) for r_ in range(2)] + ["oT_s"]
                for cb4 in range(4):
                    sl_gs = WS.get((wsrc(w_in[l], 6144 + cb4 * 512, 512), NCH, 512))
                    sl_gd = WS.get((wsrc(w_in[l], 8192 + cb4 * 512, 512), NCH, 512))
                    sl_b = WS.get((lambda c0, c1, cb4=cb4, l=l: (w_bsb[l] if c0 < 8 else w_bdf[l])[(c0 % 8) * 128:((c1 - 1) % 8 + 1) * 128,
                                                                                              cb4 * 512:(cb4 + 1) * 512].rearrange("(c p) n -> p c n", p=128),
                                   NCH, 512))
                    if planning:
                        continue
                    for k in range(4):
                        cb = 4 * cb4 + k
                        cs = slice(k * 128, (k + 1) * 128)
                        for t, (t0, tn) in enumerate(TT):
                            gt = gtmp[(cb * 3 + t) % 2]
                            pg = [ps_next(), ps_next()]
                            for gi, (wslab, slot) in enumerate((sl_gs, sl_gd)):
                                for c in range(NCH):
                                    mm(sch, psum[pg[gi]][:, 0:tn], wslab[:, c, cs], hTo[:, c, t0:t0 + tn], c == 0, c == NCH - 1,
                                       r=r_h + rg(slot, c), w=[("ps", pg[gi])])
                                act(sch, gt[gi][:, 0:tn], psum[pg[gi]][:, 0:tn], AF.Sigmoid, r=[("ps", pg[gi])], w=[("gt", (cb * 3 + t) % 2, gi)])
                            pb_ = [ps_next(), ps_next()]
                            wslab, slot = sl_b
                            for bi in range(2):
                                for c in range(8):
                                    mm(sch, psum[pb_[bi]][:, 0:tn], wslab[:, 8 * bi + c, cs], oT[:, 8 * bi + c, t0:t0 + tn], c == 0, c == 7,
                                       r=r_o + rg(slot, 8 * bi + c), w=[("ps", pb_[bi])])
                            tmp = gt[2][:, 0:tn]
                            dtt(sch, tmp, gt[0][:, 0:tn], psum[pb_[0]][:, 0:tn], ALU.mult,
                                r=[("gt", (cb * 3 + t) % 2, 0), ("ps", pb_[0])], w=[("gt", (cb * 3 + t) % 2, 2)])
                            dtt(sch, gt[1][:, 0:tn], gt[1][:, 0:tn], psum[pb_[1]][:, 0:tn], ALU.mult,
                                r=[("gt", (cb * 3 + t) % 2, 1), ("ps", pb_[1])], w=[("gt", (cb * 3 + t) % 2, 1)])
                            dtt(sch, mT[:, cb, t0:t0 + tn], tmp, gt[1][:, 0:tn], ALU.add,
                                r=[("gt", (cb * 3 + t) % 2, 2), ("gt", (cb * 3 + t) % 2, 1)], w=[("mT", cb, t)])
                    WS.release(3)
                sch.barrier()
                ctx["stop"](f"p3a_{l}", sch)

                RXT.reset(); R3.reset()
                if not planning:
                    xr = [R3.get([TO], F32) for _ in range(2)]
                for cb4 in range(4):
                    sl = WS.get((wsrc(w_out[l], cb4 * 512, 512), NCH, 512))
                    if planning:
                        continue
                    wslab, slot = sl
                    for k in range(4):
                        cb = 4 * cb4 + k
                        xb = xr[cb % 2]
                        dma(sch, "sp", xb, xsp[l].ap()[cb, :, :], r=[("xsp", l, cb)], w=[("xr", cb % 2)])
                        for t, (t0, tn) in enumerate(TT):
                            pi = ps_next()
                            for c in range(NCH):
                                mm(sch, psum[pi][:, 0:tn], wslab[:, c, k * 128:(k + 1) * 128], mT[:, c, t0:t0 + tn], c == 0, c == NCH - 1,
                                   r=[("mT", c, t)] + rg(slot, c), w=[("ps", pi)])
                            dtt(sch, XT[:, cb, t0:t0 + tn], xb[:, t0:t0 + tn], psum[pi][:, 0:tn], ALU.add,
                                r=[("xr", cb % 2), ("ps", pi)], w=[gXT(cb, t)])
                    WS.release()
                sch.barrier()
                ctx["stop"](f"p3b_{l}", sch)

                R2.reset(); R3.reset()
                h2 = R2.get([NCH, TO])
                aT = [R3.get([4, TO]) for _ in range(2)]
                sqbuf = [R3.get([4, 512]) for _ in range(2)]
                rstd_t = R3.get([512], F32)
                rl = [R3.get([512], F32) for _ in range(2)]
                if not planning:
                    norm_to(sch, 2 + l, lambda c, t0, tn: h2[:, c, t0:t0 + tn], "h2", None)
                for g in range(16):
                    sl_u = WS.get((wsrc(w_up[l], g * 512, 512), NCH, 512))
                    sl_d = WS.get((wsrc(w_down[l], 0, 2048, row0=g * 512), 4, 2048))
                    if planning:
                        continue
                    ab = aT[g % 2]
                    wslab, slot = sl_u
                    for k in range(4):
                        for t, (t0, tn) in enumerate(TT):
                            pi = ps_next()
                            for c in range(NCH):
                                mm(sch, psum[pi][:, 0:tn], wslab[:, c, k * 128:(k + 1) * 128], h2[:, c, t0:t0 + tn], c == 0, c == NCH - 1,
                                   r=[("h2", c, t)] + rg(slot, c), w=[("ps", pi)])
                            rb = rl[(k * 3 + t) % 2]
                            dts(sch, rb[:, 0:tn], psum[pi][:, 0:tn], 0.0, None, ALU.max, ALU.bypass, r=[("ps", pi)], w=[("rl", (k * 3 + t) % 2)])
                            act(sch, ab[:, k, t0:t0 + tn], rb[:, 0:tn], AF.Square, r=[("rl", (k * 3 + t) % 2)], w=[("aT", g % 2, k, t)])
                    wslab, slot = sl_d
                    for cb in range(NCH):
                        for t, (t0, tn) in enumerate(TT):
                            pi = ps_next()
                            for k in range(4):
                                mm(sch, psum[pi][:, 0:tn], wslab[:, k, cb * 128:(cb + 1) * 128], ab[:, k, t0:t0 + tn], k == 0, k == 3,
                                   r=[("aT", g % 2, k, t)] + rg(slot, k, 2048), w=[("ps", pi)])
                            dtt(sch, XT[:, cb, t0:t0 + tn], XT[:, cb, t0:t0 + tn], psum[pi][:, 0:tn], ALU.add,
                                r=[gXT(cb, t), ("ps", pi)], w=[gXT(cb, t)])
                    WS.release(2)
                sch.barrier()
                ctx["stop"](f"p4_{l}", sch)

            if not planning:
                R2.reset(); R3.reset()
                sqbuf = [R3.get([4, 512]) for _ in range(2)]
                rstd_t = R3.get([512], F32)
                yt = [R3.get([D], F32) for _ in range(2)]
                YT = sb(R2_0, [8, TO], F32)
                for half in range(2):
                    for t, (t0, tn) in enumerate(TT):
                        pi = ps_next()
                        ps = psum[pi][:, 0:tn]
                        for cg in range(4):
                            sq = sqbuf[cg % 2][:, :, 0:tn]
                            act(sch, sq, XT[:, 4 * cg:4 * cg + 4, t0:t0 + tn], AF.Square,
                                r=[gXT(c, t) for c in range(4 * cg, 4 * cg + 4)], w=[("sq", cg % 2)])
                            for k in range(4):
                                c = 4 * cg + k
                                mm(sch, ps, ones_bf, sq[:, k, :], start=(c == 0), stop=(c == NCH - 1), r=[("sq", cg % 2)], w=[("ps", pi)])
                        rs = rstd_t[:, 0:tn]
                        act(sch, rs, ps, AF.Ln, r=[("ps", pi)], w=["rstd"], scale=1.0 / D, bias=EPS)
                        act(sch, rs, rs, AF.Exp, r=["rstd"], w=["rstd"], scale=-0.5)
                        for c8 in range(8):
                            c = 8 * half + c8
                            dstt(sch, YT[:, c8, t0:t0 + tn], XT[:, c, t0:t0 + tn], gcols[:, 64 + c:64 + c + 1], rs,
                                 ALU.mult, ALU.mult, r=[gXT(c, t), "rstd"], w=[("YT", c8, t)])
                    for i in range(9):
                        c0 = i * 128 if i < 8 else TO - 128
                        r0 = 0 if i < 8 else 64
                        rt = [("YT", c8_, t_) for c8_ in range(8) for t_ in ((i // 4,) if i < 8 else (1, 2))]
                        yb = yt[i % 2]
                        for b4 in range(2):
                            pi = ps_next()
                            for k in range(4):
                                c8 = 4 * b4 + k
                                tr(sch, psum[pi][:, k * 128:(k + 1) * 128], YT[:, c8, c0:c0 + 128], ident_f,
                                   r=[g for g in rt if g[1] == c8], w=[("ps", pi)])
                            if b4 == 0:
                                act(sch, yb[r0:128, 0:512], psum[pi][r0:128, :], AF.Copy, r=[("ps", pi)], w=[("yt", i % 2, b4)])
                            else:
                                dcopy(sch, yb[r0:128, 512:1024], psum[pi][r0:128, :], r=[("ps", pi)], w=[("yt", i % 2, b4)])
                        dma(sch, "sp", y_d[c0 + r0:c0 + 128, half * 1024:(half + 1) * 1024], yb[r0:128, 0:1024],
                            r=[("yt", i % 2, 0), ("yt", i % 2, 1)], w=[("y", i, half)])
                sch.barrier()

        dbg_t = {}

        def dbg(sch, name, ap, shape, dt=F32):
            if not DEBUG or sch.plan:
                return
            if name not in dbg_t:
                dbg_t[name] = nc.dram_tensor("dbg_" + name, list(shape), dt, kind="ExternalOutput").ap()
            sch.barrier()
            sch.op("sp", lambda e: e.dma_start(out=dbg_t[name], in_=ap), kind="d")
            sch.barrier()
        ctx["dbg"] = dbg

        def stop(tag, sch):
            if STOP_AFTER == tag:
                sch.barrier()
                raise _Stop()
        ctx["stop"] = stop
        try:
            build(PLAN)
        except _Stop:
            pass
        WS.planning = False
        try:
            build(S)
        except _Stop:
            pass

        run = S.emit(nc, None, esem, dsem, ccsem)

        @block.tensor
        def _(e):
            run("pe", e)

        @block.scalar
        def _(e):
            run("act", e)

        @block.vector
        def _(e):
            run("dve", e)

        @block.gpsimd
        def _(e):
            with e.register("r_off0") as r0, e.register("r_off1") as r1:
                e.reg_load(r0, metai_d[0:1, 0:1])
                e.reg_load(r1, metai_d[0:1, 1:2])
                ctx["off"] = [e.snap(r0), e.snap(r1)]
                run("pool", e)

        @block.sync
        def _(e):
            run("sp", e)

    return nc


_NC = None


def _make_in_maps(x_prompt, x_sample, cache_sb_k, cache_sb_v, cache_diff_k, cache_diff_v,
                  norm1_g, w_in, lambda_q1, lambda_k1, lambda_q2, lambda_k2, diff_norm_g,
                  w_branch_sb, w_branch_diff, w_out, norm2_g, w_up, w_down, final_norm_g):
    f = lambda a: np.ascontiguousarray(np.asarray(a, dtype=np.float32))
    x_prompt, x_sample = f(x_prompt), f(x_sample)
    w_in = f(w_in)
    w_my = []
    for hh in range(2):
        cols = []
        for sec in range(6):
            cols.append(w_in[:, :, sec * 1024 + hh * 512: sec * 1024 + (hh + 1) * 512])
        w_my.append(np.ascontiguousarray(np.concatenate(cols, axis=2)))
    gc = np.stack([f(norm1_g)[0], f(norm1_g)[1], f(norm2_g)[0], f(norm2_g)[1], f(final_norm_g)], 0)
    gcols = np.ascontiguousarray(gc.reshape(5, 16, 128).transpose(2, 0, 1).reshape(128, 80))
    dng = np.ascontiguousarray(f(diff_norm_g).T)
    lv = np.stack([f(lambda_q1), f(lambda_k1), f(lambda_q2), f(lambda_k2)], 1)
    lamv = np.ascontiguousarray(np.broadcast_to(lv.reshape(1, 512), (128, 512)))
    shared = dict(w_in=w_in, w_bsb=f(w_branch_sb), w_bdf=f(w_branch_diff), w_out=f(w_out), w_up=f(w_up), w_down=f(w_down),
                  gcols=gcols, dng=dng, lamv=lamv)
    cs = [f(cache_sb_k), f(cache_sb_v), f(cache_diff_k), f(cache_diff_v)]
    in_maps = []
    for c in range(8):
        b, hh = c // 2, c % 2
        m = dict(shared)
        m["xin"] = np.ascontiguousarray(np.concatenate([x_prompt[b, hh * 1024:(hh + 1) * 1024], x_sample[c]], 0))
        m["w_my"] = w_my[hh]
        for nme, arr in zip(("c_sbk", "c_sbv", "c_dfk", "c_dfv"), cs):
            m[nme] = np.ascontiguousarray(arr[:, c].reshape(DEPTH, PAST, 1024))
        m["meta_f"] = np.full((128, 1), float(hh), np.float32)
        m["meta_i"] = np.array([[hh * 512, hh * 512 + 1024]], np.int32)
        in_maps.append(m)
    return in_maps


def _assemble(R):
    y_prompt = np.empty((4, SEQ, D), np.float32)
    y_sample = np.empty((8, TS, D), np.float32)
    pk = [np.empty((DEPTH, 4, SEQ, 8, 128), np.float32) for _ in range(4)]
    sk = [np.empty((DEPTH, 8, TS, 8, 128), np.float32) for _ in range(4)]
    for c in range(8):
        b, hh = c // 2, c % 2
        y = R[c]["y"]
        y_prompt[b, hh * 1024:(hh + 1) * 1024] = y[:TP]
        y_sample[c] = y[TP:]
        for i, nme in enumerate(("p_sbk", "p_sbv", "p_dfk", "p_dfv")):
            pk[i][:, b, :, 4 * hh:4 * hh + 4, :] = R[c][nme].reshape(DEPTH, SEQ, 4, 128)
        for i, nme in enumerate(("s_sbk", "s_sbv", "s_dfk", "s_dfv")):
            sk[i][:, c] = R[c][nme].reshape(DEPTH, TS, 8, 128)
    return (y_prompt, y_sample, pk[0], pk[1], pk[2], pk[3], sk[0], sk[1], sk[2], sk[3])


def kernel(**inputs):
    global _NC
    if _NC is None:
        _NC = build_program()
    in_maps = _make_in_maps(**inputs)
    res = run_bass_kernel_spmd(_NC, in_maps, core_ids=list(range(8)))
    return _assemble(res.results)
```

```python
import math
import numpy as np
import ml_dtypes
import concourse.bass as bass
import concourse.mybir as mybir
from concourse.bass_utils import run_bass_kernel_spmd

F32 = mybir.dt.float32
BF16 = mybir.dt.bfloat16
I32 = mybir.dt.int32
AF = mybir.ActivationFunctionType
ALU = mybir.AluOpType

D = 2048
NCH = 16
TP = 1024
TS = 64
TO = TP + TS
SEQ = 2048
PAST = 1024
DEPTH = 2
EPS = 1e-6
TT = [(0, 512), (512, 512), (1024, 64)]
NEG = -30000.0
DEBUG = False
GROUPS = [[0, 1], [2, 3], [4, 5], [6, 7]]
SMALLW = False
STOP_AFTER = None


class _Stop(Exception):
    pass

ARENA = 106000


class Sched:
    ENG = ["pe", "act", "dve", "pool", "sp"]
    NDS = 30

    def __init__(self, plan):
        self.plan = plan
        self.ops = []
        self.lastw = {}
        self.rd = {}
        self.last_of = {}
        self.dmas_out = []
        self.last_cc = None

    def op(self, eng, fn, r=(), w=(), kind="c", extra=()):
        if self.plan:
            return -1
        i = len(self.ops)
        if eng in ("act", "dve"):
            pr = [g for g in r if isinstance(g, tuple) and g[0] == "ps"]
            if pr:
                w = list(w) + [g for g in pr if g not in w]
        deps = set(extra)
        for g in r:
            lw = self.lastw.get(g)
            if lw is not None:
                deps.add(lw)
        for g in w:
            lw = self.lastw.get(g)
            if lw is not None:
                deps.add(lw)
            deps.update(self.rd.get(g, ()))
        if kind == "x" and self.last_cc is not None:
            deps.add(self.last_cc)
        deps.discard(i)
        self.ops.append(dict(eng=eng, fn=fn, deps=deps, kind=kind))
        for g in r:
            lst = self.rd.setdefault(g, [])
            if kind == "c":
                for k in range(len(lst)):
                    if self.ops[lst[k]]["kind"] == "c" and self.ops[lst[k]]["eng"] == eng:
                        lst[k] = i
                        break
                else:
                    lst.append(i)
            else:
                lst.append(i)
        for g in w:
            self.lastw[g] = i
            self.rd[g] = []
        if kind == "c":
            self.last_of[eng] = i
        else:
            self.dmas_out.append(i)
            if kind == "x":
                self.last_cc = i
        return i

    def barrier(self):
        if self.plan:
            return
        deps = set(self.last_of.values()) | set(self.dmas_out)
        for e in self.ENG:
            i = len(self.ops)
            self.ops.append(dict(eng=e, fn=None, deps=set(deps), kind="c"))
            self.last_of[e] = i
        self.dmas_out = []
        self.lastw = {}
        self.rd = {}

    def emit(self, nc, handles, esem, dsem, ccsem):
        ops = self.ops
        for o in ops:
            o["tgt"] = False
        for i, o in enumerate(ops):
            for d in o["deps"]:
                od = ops[d]
                if od["kind"] != "c":
                    continue
                if od["eng"] == "pe" and o["eng"] == "pe" and o["kind"] == "c":
                    continue
                od["tgt"] = True
        cnt = {e: 0 for e in self.ENG}
        dcnt = {"sp": 0, "pool": 0}
        cccnt = 0
        for i, o in enumerate(ops):
            e = o["eng"]
            if o["kind"] == "c":
                if o["fn"] is None:
                    o["sig"] = ("E", e, None)
                elif o["tgt"]:
                    cnt[e] += 1
                    o["sig"] = (esem[e], cnt[e])
                else:
                    o["sig"] = None
            elif o["kind"] == "d":
                k = dcnt[e]
                dcnt[e] += 1
                s = dsem[e][k % self.NDS]
                o["sig"] = (s, 16 * (k // self.NDS + 1))
                o["prev"] = (s, 16 * (k // self.NDS)) if k >= self.NDS else None
            else:
                cccnt += 1
                o["sig"] = (ccsem, cccnt)
        per = {e: [] for e in self.ENG}
        for i, o in enumerate(ops):
            per[o["eng"]].append(i)
        waited = {e: {} for e in self.ENG}

        def run(e, h):
            wt = waited[e]
            for i in per[e]:
                o = ops[i]
                need = {}
                for d in o["deps"]:
                    od = ops[d]
                    if od["kind"] == "c":
                        if od["fn"] is None:
                            continue
                        if od["eng"] == "pe" and e == "pe" and o["kind"] == "c":
                            continue
                    sg = od["sig"]
                    if sg is None:
                        raise RuntimeError("dep on non-target op")
                    key = id(sg[0])
                    if key not in need or need[key][1] < sg[1]:
                        need[key] = sg
                if o["kind"] == "d" and o["prev"] is not None:
                    sg = o["prev"]
                    key = id(sg[0])
                    if key not in need or need[key][1] < sg[1]:
                        need[key] = sg
                for key, (s, v) in need.items():
                    if wt.get(key, 0) >= v:
                        continue
                    h.wait_ge(s, v)
                    wt[key] = v
                if o["fn"] is None:
                    continue
                ins = o["fn"](h)
                sg = o["sig"]
                if sg is not None:
                    if o["kind"] == "c":
                        ins.then_inc(sg[0], 1)
                    elif o["kind"] == "d":
                        ins.then_inc(sg[0], 16)
                    else:
                        ins.then_inc(sg[0])
        return run


def build_program():
    nc = bass.Bass("TRN2", target_bir_lowering=False)

    def din(name, shape, dt=F32):
        return nc.dram_tensor(name, list(shape), dt, kind="ExternalInput").ap()

    def dout(name, shape, dt=F32):
        return nc.dram_tensor(name, list(shape), dt, kind="ExternalOutput").ap()

    xin = din("xin", [TO, D])
    if SMALLW:
        w_my = w_in = w_bsb = w_bdf = w_out = w_up = w_down = din("w_small", [DEPTH, 8192, 128])
    else:
        w_my = din("w_my", [DEPTH, D, 3072])
        w_in = din("w_in", [DEPTH, D, 10240])
        w_bsb = din("w_bsb", [DEPTH, 1024, D])
        w_bdf = din("w_bdf", [DEPTH, 1024, D])
        w_out = din("w_out", [DEPTH, D, D])
        w_up = din("w_up", [DEPTH, D, 8192])
        w_down = din("w_down", [DEPTH, 8192, D])
    caches = [din(n, [DEPTH, PAST, 1024]) for n in ("c_sbk", "c_sbv", "c_dfk", "c_dfv")]
    gcols_d = din("gcols", [128, 80])
    dng_d = din("dng", [128, 2])
    lamv_d = din("lamv", [128, 512])
    metaf_d = din("meta_f", [128, 1])
    metai_d = din("meta_i", [1, 2], I32)

    y_d = dout("y", [TO, D])
    pouts = [dout(n, [DEPTH, SEQ, 512]) for n in ("p_sbk", "p_sbv", "p_dfk", "p_dfv")]
    souts = [dout(n, [DEPTH, TS, 1024]) for n in ("s_sbk", "s_sbv", "s_dfk", "s_dfv")]

    hsrc = [[nc.dram_tensor(f"hsrc{l}_{f}", [1024, 1024], BF16) for f in range(2)] for l in range(DEPTH)]
    hall = [[nc.dram_tensor(f"hall{l}_{f}", [2048, 1024], BF16) for f in range(2)] for l in range(DEPTH)]
    osrc = [[nc.dram_tensor(f"osrc{l}_{t}", [1024, 1024], BF16) for t in range(2)] for l in range(DEPTH)]
    oall = [[nc.dram_tensor(f"oall{l}_{t}", [2048, 1024], BF16) for t in range(2)] for l in range(DEPTH)]
    xsp = [nc.dram_tensor(f"xsp{l}", [NCH, 128, TO], F32) for l in range(DEPTH)]

    lam_init = [0.8 - 0.6 * math.exp(-0.3 * l) for l in range(DEPTH)]

    import contextlib
    es = contextlib.ExitStack()
    with es:
        arena_t = es.enter_context(nc.sbuf_tensor("arena", [128, ARENA], BF16))
        psum = [es.enter_context(nc.psum_tensor(f"ps{i}", [128, 512], F32)) for i in range(8)]
        esem = {e: es.enter_context(nc.semaphore(f"s_{e}")) for e in Sched.ENG}
        dsem = {q: [es.enter_context(nc.semaphore(f"d_{q}{i}")) for i in range(Sched.NDS)] for q in ("sp", "pool")}
        ccsem = es.enter_context(nc.semaphore("s_cc"))
        block = es.enter_context(nc.Block())

        def sb(off, shape, dt=BF16):
            n = int(np.prod(shape))
            if dt in (F32, I32):
                ap = arena_t[:, off:off + 2 * n].bitcast(dt)
            else:
                ap = arena_t[:, off:off + n]
            if len(shape) == 2:
                ap = ap.rearrange("p (a b) -> p a b", b=shape[1])
            elif len(shape) == 3:
                ap = ap.rearrange("p (a b c) -> p a b c", b=shape[1], c=shape[2])
            return ap

        class Bump:
            def __init__(self, lo, hi):
                self.lo, self.hi, self.p = lo, hi, lo

            def reset(self):
                self.p = self.lo

            def get(self, shape, dt=BF16):
                n = int(np.prod(shape)) * (2 if dt in (F32, I32) else 1)
                n = (n + 1) // 2 * 2
                off = self.p
                self.p += n
                assert self.p <= self.hi, ("arena overflow", self.p, self.hi)
                return sb(off, shape, dt)

        CONST = Bump(0, 12800)
        RING0 = 12800
        NRING = 3
        SLAB = 8192
        XT0 = RING0 + NRING * SLAB
        XT_N = NCH * TO * 2
        R2_0 = XT0 + XT_N
        R2_N = NCH * TO
        R3_0 = R2_0 + R2_N
        R3_N = ARENA - R3_0
        assert R3_N >= 15872
        RXT = Bump(XT0, XT0 + XT_N)
        R2 = Bump(R2_0, R2_0 + R2_N)
        R3 = Bump(R3_0, R3_0 + R3_N)

        ident_bf = CONST.get([128])
        ident_f = CONST.get([128], F32)
        ones_bf = CONST.get([128])
        ones_f = CONST.get([128], F32)
        negones_bf = CONST.get([128])
        negU_bf = CONST.get([128])
        negtri_bf = CONST.get([128])
        dbase_f = CONST.get([128], F32)
        diag_cur = [CONST.get([128]) for _ in range(2)]
        kaug = CONST.get([2048])
        qbase_hi = CONST.get([2048])
        qaug = [CONST.get([2048]) for _ in range(2)]
        gcols = CONST.get([80], F32)
        dng = CONST.get([2], F32)
        metaf = CONST.get([1], F32)
        slope_my = CONST.get([4], F32)
        nslope_my = CONST.get([4], F32)
        neglam = CONST.get([2], F32)
        gdn = CONST.get([2], F32)
        lamt = CONST.get([8], F32)
        hS = CONST.get([NCH, TS])
        oS = CONST.get([NCH, TS])
        lamv = R3.get([512], F32)

        S = Sched(plan=False)
        PLAN = Sched(plan=True)
        ctx = {}

        class WStream:
            def __init__(self):
                self.order = []
                self.pos = 0
                self.issued = 0
                self.released = 0
                self.planning = True

            def _issue(self, k):
                desc = self.order[k]
                slot = k % NRING
                src_fn, nck, ncol = desc
                dst = sb(RING0 + slot * SLAB, [nck, ncol])
                step = 4 if ncol <= 512 else 1
                for c0 in range(0, nck, step):
                    c1 = min(nck, c0 + step)
                    S.op("pool", (lambda e, c0=c0, c1=c1, dst=dst, src_fn=src_fn: e.dma_start(out=dst[:, c0:c1, :], in_=src_fn(c0, c1))),
                         w=[g for c in range(c0, c1) for g in rg(slot, c, ncol)], kind="d")

            def _pump(self):
                while self.issued < min(len(self.order), self.released + NRING):
                    self._issue(self.issued)
                    self.issued += 1

            def get(self, desc):
                if self.planning:
                    self.order.append(desc)
                    return None, None
                k = self.pos
                self.pos += 1
                self._pump()
                assert self.issued > k, "weight ring deadlock: slab requested before a slot was released"
                slot = k % NRING
                _, nck, ncol = desc
                return sb(RING0 + slot * SLAB, [nck, ncol]), slot

            def release(self, n=1):
                if self.planning:
                    return
                self.released += n
                self._pump()

        def rg(slot, c, ncol=512):
            per = ncol // 512
            return [("ring", slot, c * per + i) for i in range(per)]

        WS = WStream()

        def wsrc(w_ap_l, col0, ncol, row0=0):
            def f(c0, c1):
                return w_ap_l[row0 + c0 * 128: row0 + c1 * 128, col0:col0 + ncol].rearrange("(c p) n -> p c n", p=128)
            return f

        def mm(sch, out, lhsT, rhs, start, stop, r, w):
            sch.op("pe", lambda e: e.matmul(out, lhsT, rhs, start=start, stop=stop, skip_group_check=True), r=r, w=w)

        def tr(sch, out, in_, ident, r, w):
            sch.op("pe", lambda e: e.transpose(out, in_, ident), r=r, w=w)

        def act(sch, out, in_, func, r, w, scale=1.0, bias=0.0):
            sch.op("act", lambda e: e.activation(out, in_, func, bias=bias, scale=scale), r=r, w=w)

        def dcopy(sch, out, in_, r, w):
            sch.op("dve", lambda e: e.tensor_copy(out, in_), r=r, w=w)

        def dtt(sch, out, a, b, op, r, w):
            sch.op("dve", lambda e: e.tensor_tensor(out, a, b, op), r=r, w=w)

        def dstt(sch, out, in0, scalar, in1, op0, op1, r, w):
            sch.op("dve", lambda e: e.scalar_tensor_tensor(out, in0, scalar, in1, op0, op1), r=r, w=w)

        def dts(sch, out, in0, s1, s2, op0, op1, r, w):
            sch.op("dve", lambda e: e.tensor_scalar(out, in0, s1, s2, op0, op1), r=r, w=w)

        def dmemset(sch, ap, val, w):
            sch.op("dve", lambda e: e.memset(ap, val), w=w)

        def dma(sch, q, out, in_, r, w):
            return sch.op(q, lambda e: e.dma_start(out=out, in_=in_), r=r, w=w, kind="d")

        psc = [0]

        def ps_next():
            i = psc[0] % 6
            psc[0] += 1
            return i

        def psbf(i):
            return psum[i][:, :].bitcast(BF16)

        def build(sch):
            planning = sch.plan
            WS.planning = planning
            WS.pos = 0
            WS.issued = 0
            psc[0] = 0

            if not planning:
                P = lambda fn, w=(), r=(): sch.op("pool", fn, r=r, w=w)
                P(lambda e: e.memset(ident_bf, 0.0), w=["c_identbf"])
                P(lambda e: e.affine_select(out=ident_bf, in_=ident_bf, pattern=[[-1, 128]], compare_op=ALU.not_equal,
                                            fill=1.0, base=0, channel_multiplier=1), r=["c_identbf"], w=["c_identbf"])
                P(lambda e: e.memset(ident_f, 0.0), w=["c_identf"])
                P(lambda e: e.affine_select(out=ident_f, in_=ident_f, pattern=[[-1, 128]], compare_op=ALU.not_equal,
                                            fill=1.0, base=0, channel_multiplier=1), r=["c_identf"], w=["c_identf"])
                P(lambda e: e.memset(ones_bf, 1.0), w=["c_ones"])
                P(lambda e: e.memset(ones_f, 1.0), w=["c_onesf"])
                P(lambda e: e.memset(negones_bf, -1.0), w=["c_negones"])
                P(lambda e: e.memset(negU_bf, -1.0), w=["c_negU"])
                P(lambda e: e.affine_select(out=negU_bf, in_=negU_bf, pattern=[[-1, 128]], compare_op=ALU.is_ge,
                                            fill=0.0, base=0, channel_multiplier=1), r=["c_negU"], w=["c_negU"])
                P(lambda e: e.memset(negtri_bf, 0.0), w=["c_negtri"])
                P(lambda e: e.affine_select(out=negtri_bf, in_=negtri_bf, pattern=[[1, 128]], compare_op=ALU.is_gt,
                                            fill=NEG, base=0, channel_multiplier=-1), r=["c_negtri"], w=["c_negtri"])
                RXT.reset()
                dbase_i = RXT.get([128], I32)
                P(lambda e: e.iota(dbase_i, [[-1, 128]], base=0, channel_multiplier=1), w=["s_dbasei"])
                P(lambda e: e.tensor_copy(dbase_f, dbase_i), r=["s_dbasei"], w=["c_dbase"])
                P(lambda e: e.tensor_scalar(dbase_f, dbase_f, 0.0, -16.0, ALU.max, ALU.mult), r=["c_dbase"], w=["c_dbase"])
                rows_i = RXT.get([2, 2048], I32)
                rows = RXT.get([2, 2048], F32)
                rows_bf = RXT.get([4, 2048])
                qrows_bf = RXT.get([4, 2048])
                P(lambda e: e.iota(rows_i[0:1, 0, :], [[512, 32], [0, 64]], base=0, channel_multiplier=0), w=["s_rowsi"])
                P(lambda e: e.iota(rows_i[0:1, 1, :], [[0, 32], [8, 64]], base=0, channel_multiplier=0), r=["s_rowsi"], w=["s_rowsi"])
                P(lambda e: e.tensor_copy(rows[0:1, :, :], rows_i[0:1, :, :]), r=["s_rowsi"], w=["s_rows"])
                P(lambda e: e.tensor_copy(rows_bf[0:1, 0:2, :], rows[0:1, :, :]), r=["s_rows"], w=["s_rowsbf"])
                P(lambda e: e.memset(rows_bf[0:1, 2:4, :], 1.0), r=["s_rowsbf"], w=["s_rowsbf"])
                P(lambda e: e.memset(qrows_bf[0:1, 0:2, :], 1.0), w=["s_qrows"])
                P(lambda e: e.tensor_scalar(qrows_bf[0:1, 2:4, :], rows[0:1, 0:2, :], -1.0, None, ALU.mult),
                  r=["s_rows", "s_qrows"], w=["s_qrows"])
                P(lambda e: e.memset(kaug, 0.0), w=["c_kaug"])
                P(lambda e: e.memset(qbase_hi, 0.0), w=["c_qbase"])
                for base in (0,):
                    for rr in range(4):
                        sch.op("sp", lambda e, base=base, rr=rr: e.dma_start(out=kaug[base + rr:base + rr + 1, :],
                                                                              in_=rows_bf[0:1, rr, :]),
                               r=["s_rowsbf", "c_kaug"], w=[("c_kaug_r", base, rr)], kind="d")
                        sch.op("sp", lambda e, base=base, rr=rr: e.dma_start(out=qbase_hi[base + rr:base + rr + 1, :],
                                                                              in_=qrows_bf[0:1, rr, :]),
                               r=["s_qrows", "c_qbase"], w=[("c_qbase_r", base, rr)], kind="d")
                dma(sch, "sp", gcols, gcols_d[:, :], r=[], w=["c_gcols"])
                dma(sch, "sp", dng, dng_d[:, :], r=[], w=["c_dng"])
                dma(sch, "sp", metaf, metaf_d[:, :], r=[], w=["c_metaf"])
                dma(sch, "sp", lamv, lamv_d[:, :], r=[], w=["s_lamv"])
                for j in range(4):
                    dts(sch, slope_my[:, j:j + 1], metaf, -0.9375, 1.0, ALU.mult, ALU.add, r=["c_metaf"], w=[("c_slope", j)])
                    dts(sch, slope_my[:, j:j + 1], slope_my[:, j:j + 1], 2.0 ** -(j + 1), None, ALU.mult, ALU.bypass,
                        r=[("c_slope", j)], w=[("c_slope", j)])
                lprod = R3.get([64], F32)
                for l in range(DEPTH):
                    for i in range(2):
                        a = lamv[:, (l * 4 + 2 * i) * 64:(l * 4 + 2 * i + 1) * 64]
                        b = lamv[:, (l * 4 + 2 * i + 1) * 64:(l * 4 + 2 * i + 2) * 64]
                        sch.op("dve", lambda e, a=a, b=b, l=l, i=i: e.scalar_tensor_tensor(
                            lprod, a, 1.0, b, ALU.mult, ALU.mult, accum_out=lamt[:, l * 2 + i:l * 2 + i + 1]),
                            r=["s_lamv"], w=["s_lprod", ("c_lamt", l, i)])
                    act(sch, lamt[:, 4 + l * 2:4 + l * 2 + 2], lamt[:, l * 2:l * 2 + 2], AF.Exp,
                        r=[("c_lamt", l, 0), ("c_lamt", l, 1)], w=[("c_lame", l)])
                    dstt(sch, neglam[:, l:l + 1], lamt[:, 4 + l * 2 + 1:4 + l * 2 + 2], -lam_init[l],
                         lamt[:, 4 + l * 2:4 + l * 2 + 1], ALU.add, ALU.subtract, r=[("c_lame", l)], w=[("c_neglam", l)])
                    dts(sch, gdn[:, l:l + 1], dng[:, l:l + 1], 1.0 - lam_init[l], None, ALU.mult, ALU.bypass,
                        r=["c_dng"], w=[("c_gdn", l)])
                sch.barrier()
                R3.reset()
            ctx["stop"]("setup", sch)

            CG = ["c_consts"]

            XT = sb(XT0, [NCH, TO], F32)

            def gXT(c, t):
                return ("XT", c, t)

            def norm_to(sch, l_g, dst, dstname, sq_bump):
                for t, (t0, tn) in enumerate(TT):
                    pi = ps_next()
                    ps = psum[pi][:, 0:tn]
                    for cg in range(4):
                        sq = sqbuf[cg % 2][:, :, 0:tn]
                        act(sch, sq, XT[:, 4 * cg:4 * cg + 4, t0:t0 + tn], AF.Square,
                            r=[gXT(c, t) for c in range(4 * cg, 4 * cg + 4)], w=[("sq", cg % 2)])
                        for k in range(4):
                            c = 4 * cg + k
                            mm(sch, ps, ones_bf, sq[:, k, :], start=(c == 0), stop=(c == NCH - 1),
                               r=[("sq", cg % 2)], w=[("ps", pi)])
                    rs = rstd_t[:, 0:tn]
                    act(sch, rs, ps, AF.Ln, r=[("ps", pi)], w=["rstd"], scale=1.0 / D, bias=EPS)
                    act(sch, rs, rs, AF.Exp, r=["rstd"], w=["rstd"], scale=-0.5)
                    for c in range(NCH):
                        dstt(sch, dst(c, t0, tn), XT[:, c, t0:t0 + tn], gcols[:, l_g * 16 + c:l_g * 16 + c + 1], rs,
                             ALU.mult, ALU.mult, r=[gXT(c, t), "rstd"], w=[(dstname, c, t)])

            for l in range(DEPTH):
                R2.reset(); R3.reset()
                hT = R2.get([NCH, TO])
                sqbuf = [R3.get([4, 512]) for _ in range(2)]
                rstd_t = R3.get([512], F32)
                if l == 0:
                    xt = [R3.get([D], F32) for _ in range(2)]
                    ntile = 9
                    for i in range(ntile):
                        n = 128 if i < 8 else TS
                        t = i // 4 if i < 8 else 2
                        xb = xt[i % 2]
                        dma(sch, "sp", xb[0:n, :], xin[i * 128:i * 128 + n, :], r=[], w=[("xt", i % 2)])
                        for b4 in range(4):
                            pi = ps_next()
                            for k in range(4):
                                c = 4 * b4 + k
                                tr(sch, psum[pi][:, k * 128:k * 128 + n], xb[0:n, c * 128:(c + 1) * 128], ident_f[0:n, 0:n],
                                   r=[("xt", i % 2)], w=[("ps", pi)])
                            src = psum[pi][:, :].rearrange("p (a b) -> p a b", b=128)[:, :, 0:n]
                            dstap = XT[:, 4 * b4:4 * b4 + 4, i * 128:i * 128 + n]
                            if b4 % 2 == 0:
                                act(sch, dstap, src, AF.Copy, r=[("ps", pi)], w=[("XTp", c, i) for c in range(4 * b4, 4 * b4 + 4)])
                            else:
                                dcopy(sch, dstap, src, r=[("ps", pi)], w=[("XTp", c, i) for c in range(4 * b4, 4 * b4 + 4)])
                    sch.barrier()
                norm_to(sch, l, lambda c, t0, tn: hT[:, c, t0:t0 + tn], "hT", None)
                dcopy(sch, hS, hT[:, :, TP:TO], r=[("hT", c, 2) for c in range(NCH)], w=["hS"])
                for c in range(NCH):
                    dma(sch, "sp", xsp[l].ap()[c, :, :], XT[:, c, :], r=[gXT(c, t) for t in range(3)], w=[("xsp", l, c)])
                for f in range(2):
                    dma(sch, "sp", hsrc[l][f].ap().rearrange("(c p) n -> p c n", p=128), hT[:, 8 * f:8 * f + 8, 0:TP],
                        r=[("hT", c, t) for c in range(8 * f, 8 * f + 8) for t in range(2)], w=[("hsrc", l, f)])
                for f in range(2):
                    sch.op("pool", lambda e, f=f, l=l: e.collective_compute(
                        "AllGather", ALU.bypass, replica_groups=GROUPS,
                        ins=[hsrc[l][f].ap().opt()], outs=[hall[l][f].ap().opt()]),
                        r=[("hsrc", l, f)], w=[("hall", l, f)], kind="x")
                sch.barrier()
                ctx["stop"](f"p1_{l}", sch)

                for ty in range(2):
                    RXT.reset(); R2.reset(); R3.reset()
                    QT = RXT.get([4, SEQ])
                    KT = RXT.get([4, SEQ])
                    V = RXT.get([16, 512])
                    QsT = RXT.get([8, TS])
                    KsT = RXT.get([8, 128])
                    Vs = RXT.get([1024])
                    if not planning:
                        dmemset(sch, KsT, 0.0, w=[("KsT", h_) for h_ in range(8)])
                        dmemset(sch, Vs, 0.0, w=[("Vs", 0), ("Vs", 1)])
                    hring = [R2.get([NCH, 512]) for _ in range(2)]
                    stage = [R3.get([512], F32) for _ in range(2)]
                    otile = [R3.get([SEQ]) for _ in range(2)]
                    sstage = R3.get([1024], F32)
                    pk_out, pv_out = pouts[2 * ty], pouts[2 * ty + 1]
                    sk_out, sv_out = souts[2 * ty], souts[2 * ty + 1]
                    qscale = (128.0 ** -0.5) if ty == 0 else 1.0

                    slabs = []
                    for s in range(3):
                        slabs.append(WS.get((wsrc(w_my[l], (3 * ty + s) * 512, 512), NCH, 512)))
                    if not planning:
                        (wq, sq_), (wk, sk_), (wv, sv_) = slabs
                        for t in range(4):
                            hb = hring[t % 2]
                            r_, half = t // 2, t % 2
                            for f in range(2):
                                dma(sch, "sp", hb[:, 8 * f:8 * f + 8, :],
                                    hall[l][f].ap()[r_ * 1024:(r_ + 1) * 1024, half * 512:(half + 1) * 512].rearrange("(c p) n -> p c n", p=128),
                                    r=[("hall", l, f)], w=[("hring", t % 2, f)])
                            rh = [("hring", t % 2, 0), ("hring", t % 2, 1)]
                            for j in range(4):
                                for which, (wslab, slot), dstT in ((0, (wq, sq_), QT), (1, (wk, sk_), KT)):
                                    pi = ps_next()
                                    for c in range(NCH):
                                        mm(sch, psum[pi][:, :], wslab[:, c, j * 128:(j + 1) * 128], hb[:, c, :],
                                           start=(c == 0), stop=(c == NCH - 1), r=rh + rg(slot, c), w=[("ps", pi)])
                                    if which == 0:
                                        act(sch, dstT[:, j, t * 512:(t + 1) * 512], psum[pi][:, :], AF.Copy, scale=qscale,
                                            r=[("ps", pi)], w=[("QT", j, t)])
                                    else:
                                        dcopy(sch, dstT[:, j, t * 512:(t + 1) * 512], psum[pi][:, :], r=[("ps", pi)], w=[("KT", j, t)])
                            if t == 0:
                                ctx["stop"]("p2a1", sch)
                            for b in range(4):
                                blk = 4 * t + b
                                for which, (wslab, slot) in ((1, (wk, sk_)), (2, (wv, sv_))):
                                    pi = ps_next()
                                    for c in range(NCH):
                                        mm(sch, psum[pi][:, :], hb[:, c, b * 128:(b + 1) * 128], wslab[:, c, :],
                                           start=(c == 0), stop=(c == NCH - 1), r=rh + rg(slot, c), w=[("ps", pi)])
                                    st = stage[which - 1]
                                    act(sch, st, psum[pi][:, :], AF.Copy, r=[("ps", pi)], w=[("stage", which)])
                                    if which == 2:
                                        dcopy(sch, V[:, blk, :], psum[pi][:, :], r=[("ps", pi)], w=[("V", blk)])
                                    dma(sch, "sp", (pk_out if which == 1 else pv_out)[l, blk * 128:(blk + 1) * 128, :], st,
                                        r=[("stage", which)], w=[("pout", ty, which, blk)])
                    WS.release(3)
                    ctx["stop"]("p2a3", sch)
                    for s in range(3):
                        for hf in range(2):
                            wslab, slot = WS.get((wsrc(w_in[l], (3 * ty + s) * 1024 + hf * 512, 512), NCH, 512))
                            if not planning:
                                if s < 2:
                                    for jj in range(4):
                                        h = hf * 4 + jj
                                        pi = ps_next()
                                        for c in range(NCH):
                                            mm(sch, psum[pi][:, 0:TS], wslab[:, c, jj * 128:(jj + 1) * 128], hS[:, c, :],
                                               start=(c == 0), stop=(c == NCH - 1), r=["hS"] + rg(slot, c), w=[("ps", pi)])
                                        if s == 0:
                                            act(sch, QsT[:, h, :], psum[pi][:, 0:TS], AF.Copy, scale=qscale, r=[("ps", pi)], w=[("QsT", h)])
                                        else:
                                            dcopy(sch, KsT[:, h, 0:TS], psum[pi][:, 0:TS], r=[("ps", pi)], w=[("KsT", h)])
                                if s >= 1:
                                    pi = ps_next()
                                    for c in range(NCH):
                                        mm(sch, psum[pi][0:TS, :], hS[:, c, :], wslab[:, c, :],
                                           start=(c == 0), stop=(c == NCH - 1), r=["hS"] + rg(slot, c), w=[("ps", pi)])
                                    act(sch, sstage[0:TS, hf * 512:(hf + 1) * 512], psum[pi][0:TS, :], AF.Copy,
                                        r=[("ps", pi)], w=[("sstage", hf)])
                                    if s == 2:
                                        dcopy(sch, Vs[0:TS, hf * 512:(hf + 1) * 512], psum[pi][0:TS, :], r=[("ps", pi)], w=[("Vs", hf)])
                                    if hf == 1:
                                        dma(sch, "sp", (sk_out if s == 1 else sv_out)[l, :, :], sstage[0:TS, :],
                                            r=[("sstage", 0), ("sstage", 1)], w=[("sout", ty, s)])
                            WS.release()
                    ctx["stop"](f"p2a_{l}_{ty}", sch)

                    if planning:
                        continue

                    sc = Bump(RXT.p, XT0 + XT_N)
                    if ty == 0:
                        e_t = [sc.get([512], F32) for _ in range(2)]
                        l1_t = [sc.get([512]) for _ in range(3)]
                        a_t = [sc.get([512]) for _ in range(3)]
                        w_t = [sc.get([512]) for _ in range(3)]
                    else:
                        E_t = [[sc.get([512]) for _ in range(3)] for _ in range(2)]
                        z_t = [sc.get([512], F32) for _ in range(2)]
                        ep = [R3.get([512], F32) for _ in range(3)]
                        qpad = [R3.get([SEQ]) for _ in range(2)]
                    uid = [0]

                    def attend(qsrc, nq, blocks, out_ap, out_w, head_consts):
                        uid[0] += 1
                        u = uid[0]
                        nb = len(blocks)
                        full = slice(0, 128)
                        if ty == 0:
                            po = 6 + (u % 2)
                            for i3 in range(3):
                                dmemset(sch, a_t[i3][:, 0:nq], 0.0, w=[("a_t", i3)])
                            st = {}

                            def A(bi):
                                bk = blocks[bi]
                                nk, d0, diag = bk["nk"], bk["d0"], bk["diag"]
                                nd = min(nk, nq - d0)
                                p1 = ps_next()
                                st[bi] = dict(p1=p1)
                                sps = psum[p1][0:nk, d0:nq]
                                mm(sch, sps, bk["k"](full), qsrc["ap"](full)[:, d0:nq], True, not diag, r=bk["r"] + qsrc["r"], w=[("ps", p1)])
                                if diag:
                                    mm(sch, psum[p1][0:nk, d0:d0 + nd], ident_bf[0:nk, 0:nk], negtri_bf[0:nk, 0:nd], False, True, r=[], w=[("ps", p1)])

                            def B(bi):
                                bk = blocks[bi]
                                nk, d0 = bk["nk"], bk["d0"]
                                p1 = st[bi]["p1"]
                                eb = e_t[bi % 2][0:nk, d0:nq]
                                lb = l1_t[bi % 3][0:nk, d0:nq]
                                act(sch, eb, psum[p1][0:nk, d0:nq], AF.Exp, r=[("ps", p1)], w=[("e_t", bi % 2)])
                                act(sch, lb, eb, AF.Ln, r=[("e_t", bi % 2)], w=[("l1_t", bi % 3)], bias=1.0)
                                if bi < nb - 1:
                                    dtt(sch, a_t[(bi + 1) % 3][0:nk, d0:nq], a_t[bi % 3][0:nk, d0:nq], lb, ALU.add,
                                        r=[("a_t", bi % 3), ("l1_t", bi % 3)], w=[("a_t", (bi + 1) % 3)])

                            def C(bi):
                                bk = blocks[bi]
                                nk, d0, diag = bk["nk"], bk["d0"], bk["diag"]
                                nd = min(nk, nq - d0)
                                p2 = ps_next()
                                st[bi]["p2"] = p2
                                tps = psum[p2][0:nk, d0:nq]
                                lb = l1_t[bi % 3][0:nk, d0:nq]
                                mm(sch, tps, bk["k"](full), qsrc["ap"](full)[:, d0:nq], True, False, r=bk["r"] + qsrc["r"], w=[("ps", p2)])
                                mm(sch, tps, negU_bf[0:nk, 0:nk], lb, False, (bi == 0) and not diag, r=[("l1_t", bi % 3)], w=[("ps", p2)])
                                if bi > 0:
                                    mm(sch, tps, negones_bf[:, 0:nk], a_t[bi % 3][:, d0:nq], False, not diag, r=[("a_t", bi % 3)], w=[("ps", p2)])
                                if diag:
                                    mm(sch, psum[p2][0:nk, d0:d0 + nd], ident_bf[0:nk, 0:nk], negtri_bf[0:nk, 0:nd], False, True, r=[], w=[("ps", p2)])

                            def Dw(bi):
                                bk = blocks[bi]
                                nk, d0 = bk["nk"], bk["d0"]
                                p2 = st[bi]["p2"]
                                act(sch, w_t[bi % 3][0:nk, d0:nq], psum[p2][0:nk, d0:nq], AF.Exp, r=[("ps", p2)], w=[("w_t", bi % 3)])

                            def E(bi):
                                bk = blocks[bi]
                                nk, d0 = bk["nk"], bk["d0"]
                                mm(sch, psum[po][:, d0:nq], bk["v"], w_t[bi % 3][0:nk, d0:nq], bi == 0, bi == nb - 1,
                                   r=bk["rv"] + [("w_t", bi % 3)], w=[("ps", po)])

                            for s_ in range(nb + 2):
                                if s_ < nb:
                                    A(s_)
                                    B(s_)
                                if 0 <= s_ - 1 < nb:
                                    C(s_ - 1)
                                    Dw(s_ - 1)
                                if 0 <= s_ - 2 < nb:
                                    E(s_ - 2)
                            act(sch, out_ap, psum[po][:, 0:nq], AF.Copy, r=[("ps", po)], w=out_w)
                        else:
                            po = [6, 7]
                            qa, dg = head_consts
                            for m in range(2):
                                dmemset(sch, z_t[m][:, 0:nq], 0.0, w=[("z_t", m)])
                            st = {}

                            def A(bi):
                                bk = blocks[bi]
                                nk, d0, diag = bk["nk"], bk["d0"], bk["diag"]
                                nd = min(nk, nq - d0)
                                st[bi] = []
                                for m in range(2):
                                    p1 = ps_next()
                                    st[bi].append(p1)
                                    sps = psum[p1][0:nk, d0:nq]
                                    mm(sch, sps, bk["k"](full), qsrc["pad"][m][:, d0:nq], True, False, r=bk["r"] + qsrc["rpad"], w=[("ps", p1)])
                                    mm(sch, sps, bk["ka"](full), qa["ap"](full)[:, qsrc["p0"] + d0:qsrc["p0"] + nq], False, not diag,
                                       r=qa["r"], w=[("ps", p1)])
                                    if diag:
                                        mm(sch, psum[p1][0:nk, d0:d0 + nd], ident_bf[0:nk, 0:nk], dg["ap"][0:nk, 0:nd], False, True,
                                           r=dg["r"], w=[("ps", p1)])

                            def B(bi):
                                bk = blocks[bi]
                                nk, d0 = bk["nk"], bk["d0"]
                                for m in range(2):
                                    p1 = st[bi][m]
                                    act(sch, E_t[m][bi % 3][0:nk, d0:nq], psum[p1][0:nk, d0:nq], AF.Exp, scale=0.125,
                                        r=[("ps", p1)], w=[("E_t", m, bi % 3)])

                            def C(bi):
                                bk = blocks[bi]
                                nk, d0 = bk["nk"], bk["d0"]
                                for m in range(2):
                                    Eb = E_t[m][bi % 3][0:nk, d0:nq]
                                    mm(sch, psum[po[m]][:, d0:nq], bk["v"], Eb, bi == 0, bi == nb - 1,
                                       r=bk["rv"] + [("E_t", m, bi % 3)], w=[("ps", po[m])])
                                    dtt(sch, z_t[m][0:nk, d0:nq], z_t[m][0:nk, d0:nq], Eb, ALU.add,
                                        r=[("z_t", m), ("E_t", m, bi % 3)], w=[("z_t", m)])

                            for s_ in range(nb + 1):
                                if s_ < nb:
                                    A(s_)
                                    B(s_)
                                if 0 <= s_ - 1 < nb:
                                    C(s_ - 1)
                            for m in range(2):
                                pz = ps_next()
                                mm(sch, psum[pz][:, 0:nq], ones_f, z_t[m][:, 0:nq], True, True, r=[("z_t", m)], w=[("ps", pz)])
                                rz = ep[m][:, 0:nq]
                                act(sch, rz, psum[pz][:, 0:nq], AF.Ln, r=[("ps", pz)], w=[("ep", m)])
                                act(sch, rz, rz, AF.Exp, scale=-1.0, r=[("ep", m)], w=[("ep", m)])
                                dtt(sch, rz, psum[po[m]][:, 0:nq], rz, ALU.mult, r=[("ps", po[m]), ("ep", m)], w=[("ep", m)])
                            o_ = ep[0][:, 0:nq]
                            dstt(sch, o_, ep[1][:, 0:nq], neglam[:, l:l + 1], o_, ALU.mult, ALU.add, r=[("ep", 0), ("ep", 1)], w=[("ep", 0)])
                            sqo = ep[2][:, 0:nq]
                            act(sch, sqo, o_, AF.Square, r=[("ep", 0)], w=[("ep", 2)])
                            pz = ps_next()
                            mm(sch, psum[pz][:, 0:nq], ones_f, sqo, True, True, r=[("ep", 2)], w=[("ps", pz)])
                            rs = ep[1][:, 0:nq]
                            act(sch, rs, psum[pz][:, 0:nq], AF.Ln, scale=1.0 / 128, bias=EPS, r=[("ps", pz)], w=[("ep", 1)])
                            act(sch, rs, rs, AF.Exp, scale=-0.5, r=[("ep", 1)], w=[("ep", 1)])
                            dstt(sch, out_ap, o_, gdn[:, l:l + 1], rs, ALU.mult, ALU.mult, r=[("ep", 0), ("ep", 1)], w=out_w)

                    def set_head_consts(slope_ap, slope_imm, buf):
                        if ty == 0:
                            return None
                        qa, dgc = qaug[buf], diag_cur[buf]
                        if slope_ap is not None:
                            dts(sch, qa, qbase_hi, slope_ap, None, ALU.mult, ALU.bypass, r=[], w=[("qaug", buf)])
                            dts(sch, dgc, dbase_f, slope_ap, None, ALU.mult, ALU.bypass, r=[], w=[("diagc", buf)])
                        else:
                            dts(sch, qa, qbase_hi, slope_imm, None, ALU.mult, ALU.bypass, r=[], w=[("qaug", buf)])
                            dts(sch, dgc, dbase_f, slope_imm, None, ALU.mult, ALU.bypass, r=[], w=[("diagc", buf)])
                        dmemset(sch, dgc[64:128, 0:64], 8 * NEG, w=[("diagc", buf)])
                        return (dict(ap=lambda rows, qa=qa: qa[rows, :], r=[("qaug", buf)]), dict(ap=dgc, r=[("diagc", buf)]))

                    def make_qpad(src_fn, n, rname):
                        if ty == 0:
                            return
                        dcopy(sch, qpad[0][0:64, 0:n], src_fn(slice(0, 64)), r=rname, w=["qpad0"])
                        dmemset(sch, qpad[0][64:128, 0:n], 0.0, w=["qpad0"])
                        dmemset(sch, qpad[1][0:64, 0:n], 0.0, w=["qpad1"])
                        dcopy(sch, qpad[1][64:128, 0:n], src_fn(slice(64, 128)), r=rname, w=["qpad1"])

                    for j in range(4):
                        hc = set_head_consts(slope_my[:, j:j + 1], None, j % 2)
                        ot = otile[j % 2]
                        make_qpad(lambda rows, j=j: QT[rows, j, :], SEQ, [("QT", j, t_) for t_ in range(4)])
                        for qt in range(4):
                            q0 = 512 * qt
                            blocks = []
                            for kb in range(4 * qt + 3, -1, -1):
                                d0 = max(0, (kb - 4 * qt) * 128)
                                blocks.append(dict(
                                    k=(lambda rows, kb=kb, j=j: KT[rows, j, kb * 128:(kb + 1) * 128]),
                                    ka=(lambda rows, kb=kb: kaug[rows, kb * 128:(kb + 1) * 128]),
                                    v=V[:, kb, j * 128:(j + 1) * 128], nk=128, d0=d0, diag=(kb >= 4 * qt),
                                    r=[("KT", j, kb // 4)], rv=[("V", kb)]))
                            qsrc = dict(ap=(lambda rows, j=j, q0=q0: QT[rows, j, q0:q0 + 512]), r=[("QT", j, qt)], p0=q0)
                            if ty == 1:
                                qsrc["pad"] = [qpad[m_][:, q0:q0 + 512] for m_ in range(2)]
                                qsrc["rpad"] = ["qpad0", "qpad1"]
                            attend(qsrc, 512, blocks, ot[:, q0:q0 + 512], [("otile", j % 2, qt)], hc)
                            if DEBUG and ty == 1 and j == 0 and qt == 0:
                                D_ = ctx["dbg"]
                                D_(sch, "kaug", kaug, [128, 2048], BF16)
                                D_(sch, "qaug", qaug[0], [128, 2048], BF16)
                                D_(sch, "diag", diag_cur[0], [128, 128], BF16)
                                D_(sch, "dbase", dbase_f, [128, 128])
                                D_(sch, "slope", slope_my, [128, 4])
                                D_(sch, "neglam", neglam, [128, 2])
                                D_(sch, "gdn", gdn, [128, 2])
                                D_(sch, "lamt", lamt, [128, 8])
                                D_(sch, "z0", z_t[0], [128, 512])
                                D_(sch, "z1", z_t[1], [128, 512])
                                D_(sch, "ep0", ep[0], [128, 512])
                                D_(sch, "ep1", ep[1], [128, 512])
                                D_(sch, "ep2", ep[2], [128, 512])
                                D_(sch, "E00", E_t[0][0], [128, 512], BF16)
                                D_(sch, "ot", ot, [128, 2048], BF16)
                                D_(sch, "QT", QT[:, 0, :], [128, 2048], BF16)
                                D_(sch, "KT", KT[:, 0, :], [128, 2048], BF16)
                                ctx["stop"]("dbg_df", sch)
                        for h2 in range(2):
                            dma(sch, "sp", osrc[l][ty].ap()[h2 * 512 + j * 128:h2 * 512 + (j + 1) * 128, :], ot[:, h2 * 1024:(h2 + 1) * 1024],
                                r=[("otile", j % 2, 2 * h2), ("otile", j % 2, 2 * h2 + 1)], w=[("osrc", l, ty, j, h2)])
                    sch.op("pool", lambda e, l=l, ty=ty: e.collective_compute(
                        "AllGather", ALU.bypass, replica_groups=GROUPS,
                        ins=[osrc[l][ty].ap().opt()], outs=[oall[l][ty].ap().opt()]),
                        r=[("osrc", l, ty, j, h2) for j in range(4) for h2 in range(2)], w=[("oall", l, ty)], kind="x")

                    sch.barrier()
                    ctx["stop"](f"p2b_{l}_{ty}", sch)
                    CB = Bump(XT0, XT0 + 24576)
                    Kc = CB.get([8, 1024])
                    Vc = CB.get([8, 1024])
                    KcT = CB.get([8, 1024])
                    for b in range(8):
                        sch.op("pool", lambda e, b=b, l=l, ty=ty, Kc=Kc: e.dma_start(out=Kc[:, b, :], in_=caches[2 * ty][l, b * 128:(b + 1) * 128, :]),
                               w=[("Kc", b)], kind="d")
                        sch.op("pool", lambda e, b=b, l=l, ty=ty, Vc=Vc: e.dma_start(out=Vc[:, b, :], in_=caches[2 * ty + 1][l, b * 128:(b + 1) * 128, :]),
                               w=[("Vc", b)], kind="d")
                    ctx["stop"]("c1", sch)
                    for h in range(8):
                        for b4 in range(2):
                            pi = ps_next()
                            pb = psbf(pi)
                            for k in range(4):
                                b = 4 * b4 + k
                                tr(sch, pb[:, k * 128:(k + 1) * 128], Kc[:, b, h * 128:(h + 1) * 128], ident_bf, r=[("Kc", b)], w=[("ps", pi)])
                            if b4 == 0:
                                dcopy(sch, KcT[:, h, 0:512], pb[:, 0:512], r=[("ps", pi)], w=[("KcT", h, 0)])
                            else:
                                act(sch, KcT[:, h, 512:1024], pb[:, 0:512], AF.Copy, r=[("ps", pi)], w=[("KcT", h, 1)])
                    ctx["stop"]("c2", sch)
                    for h in range(8):
                        hc = set_head_consts(None, 2.0 ** -(h + 1), h % 2)
                        make_qpad(lambda rows, h=h: QsT[rows, h, :], TS, [("QsT", h)])
                        blocks = [dict(k=(lambda rows, h=h: KsT[rows, h, :]), ka=(lambda rows: kaug[rows, PAST:PAST + 128]),
                                       v=Vs[:, h * 128:(h + 1) * 128], nk=128, d0=0, diag=True, r=[("KsT", h)], rv=[("Vs", h // 4)])]
                        for b in range(7, -1, -1):
                            blocks.append(dict(k=(lambda rows, h=h, b=b: KcT[rows, h, b * 128:(b + 1) * 128]),
                                               ka=(lambda rows, b=b: kaug[rows, b * 128:(b + 1) * 128]),
                                               v=Vc[:, b, h * 128:(h + 1) * 128], nk=128, d0=0, diag=False,
                                               r=[("KcT", h, b // 4)], rv=[("Vc", b)]))
                        qsrc = dict(ap=(lambda rows, h=h: QsT[rows, h, :]), r=[("QsT", h)], p0=PAST)
                        if ty == 1:
                            qsrc["pad"] = [qpad[m_][:, 0:TS] for m_ in range(2)]
                            qsrc["rpad"] = ["qpad0", "qpad1"]
                        attend(qsrc, TS, blocks, oS[:, ty * 8 + h, :], [("oS", ty, h)], hc)
                        if h == 0:
                            ctx["stop"]("c3", sch)
                        if h == 1:
                            ctx["stop"]("c4", sch)
                        if h == 7:
                            ctx["stop"]("c5", sch)
                    sch.barrier()
                    ctx["stop"](f"p2_{l}_{ty}", sch)

                RXT.reset(); R2.reset(); R3.reset()
                if not planning:
                    hTo = RXT.get([NCH, TO])
                    oT = RXT.get([NCH, TO])
                    mT = R2.get([NCH, TO])
                    gtmp = [[R3.get([512], F32) for _ in range(3)] for _ in range(2)]
                    for f in range(2):
                        dma(sch, "sp", hTo[:, 8 * f:8 * f + 8, 0:TP], hsrc[l][f].ap().rearrange("(c p) n -> p c n", p=128),
                            r=[("hsrc", l, f)], w=[("hTo", f)])
                    dcopy(sch, hTo[:, :, TP:TO], hS, r=["hS"], w=["hTo_s"])
                    dcopy(sch, oT[:, :, TP:TO], oS, r=[("oS", t_, h_) for t_ in range(2) for h_ in range(8)], w=["oT_s"])
                    for ty in range(2):
                        for r_ in range(2):
                            sch.op("pool", lambda e, ty=ty, r_=r_, l=l, oT=oT: e.dma_start(
                                out=oT[:, ty * 8 + 4 * r_:ty * 8 + 4 * r_ + 4, 0:TP],
                                in_=oall[l][ty].ap()[bass.ds(ctx["off"][r_], 512), :].rearrange("(j p) n -> p j n", p=128)),
                                r=[("oall", l, ty)], w=[("oT", ty, r_)], kind="d")
                    r_h = [("hTo", 0), ("hTo", 1), "hTo_s"]
                    r_o = [("oT", t_, r_) for t_ in range(2) for r_ in range(2)] + ["oT_s"]
                for cb4 in range(4):
                    sl_gs = WS.get((wsrc(w_in[l], 6144 + cb4 * 512, 512), NCH, 512))
                    sl_gd = WS.get((wsrc(w_in[l], 8192 + cb4 * 512, 512), NCH, 512))
                    sl_b = WS.get((lambda c0, c1, cb4=cb4, l=l: (w_bsb[l] if c0 < 8 else w_bdf[l])[(c0 % 8) * 128:((c1 - 1) % 8 + 1) * 128,
                                                                                              cb4 * 512:(cb4 + 1) * 512].rearrange("(c p) n -> p c n", p=128),
                                   NCH, 512))
                    if planning:
                        continue
                    for k in range(4):
                        cb = 4 * cb4 + k
                        cs = slice(k * 128, (k + 1) * 128)
                        for t, (t0, tn) in enumerate(TT):
                            gt = gtmp[(cb * 3 + t) % 2]
                            pg = [ps_next(), ps_next()]
                            for gi, (wslab, slot) in enumerate((sl_gs, sl_gd)):
                                for c in range(NCH):
                                    mm(sch, psum[pg[gi]][:, 0:tn], wslab[:, c, cs], hTo[:, c, t0:t0 + tn], c == 0, c == NCH - 1,
                                       r=r_h + rg(slot, c), w=[("ps", pg[gi])])
                                act(sch, gt[gi][:, 0:tn], psum[pg[gi]][:, 0:tn], AF.Sigmoid, r=[("ps", pg[gi])], w=[("gt", (cb * 3 + t) % 2, gi)])
                            pb_ = [ps_next(), ps_next()]
                            wslab, slot = sl_b
                            for bi in range(2):
                                for c in range(8):
                                    mm(sch, psum[pb_[bi]][:, 0:tn], wslab[:, 8 * bi + c, cs], oT[:, 8 * bi + c, t0:t0 + tn], c == 0, c == 7,
                                       r=r_o + rg(slot, 8 * bi + c), w=[("ps", pb_[bi])])
                            tmp = gt[2][:, 0:tn]
                            dtt(sch, tmp, gt[0][:, 0:tn], psum[pb_[0]][:, 0:tn], ALU.mult,
                                r=[("gt", (cb * 3 + t) % 2, 0), ("ps", pb_[0])], w=[("gt", (cb * 3 + t) % 2, 2)])
                            dtt(sch, gt[1][:, 0:tn], gt[1][:, 0:tn], psum[pb_[1]][:, 0:tn], ALU.mult,
                                r=[("gt", (cb * 3 + t) % 2, 1), ("ps", pb_[1])], w=[("gt", (cb * 3 + t) % 2, 1)])
                            dtt(sch, mT[:, cb, t0:t0 + tn], tmp, gt[1][:, 0:tn], ALU.add,
                                r=[("gt", (cb * 3 + t) % 2, 2), ("gt", (cb * 3 + t) % 2, 1)], w=[("mT", cb, t)])
                    WS.release(3)
                sch.barrier()
                ctx["stop"](f"p3a_{l}", sch)

                RXT.reset(); R3.reset()
                if not planning:
                    xr = [R3.get([TO], F32) for _ in range(2)]
                for cb4 in range(4):
                    sl = WS.get((wsrc(w_out[l], cb4 * 512, 512), NCH, 512))
                    if planning:
                        continue
                    wslab, slot = sl
                    for k in range(4):
                        cb = 4 * cb4 + k
                        xb = xr[cb % 2]
                        dma(sch, "sp", xb, xsp[l].ap()[cb, :, :], r=[("xsp", l, cb)], w=[("xr", cb % 2)])
                        for t, (t0, tn) in enumerate(TT):
                            pi = ps_next()
                            for c in range(NCH):
                                mm(sch, psum[pi][:, 0:tn], wslab[:, c, k * 128:(k + 1) * 128], mT[:, c, t0:t0 + tn], c == 0, c == NCH - 1,
                                   r=[("mT", c, t)] + rg(slot, c), w=[("ps", pi)])
                            dtt(sch, XT[:, cb, t0:t0 + tn], xb[:, t0:t0 + tn], psum[pi][:, 0:tn], ALU.add,
                                r=[("xr", cb % 2), ("ps", pi)], w=[gXT(cb, t)])
                    WS.release()
                sch.barrier()
                ctx["stop"](f"p3b_{l}", sch)

                R2.reset(); R3.reset()
                h2 = R2.get([NCH, TO])
                aT = [R3.get([4, TO]) for _ in range(2)]
                sqbuf = [R3.get([4, 512]) for _ in range(2)]
                rstd_t = R3.get([512], F32)
                rl = [R3.get([512], F32) for _ in range(2)]
                if not planning:
                    norm_to(sch, 2 + l, lambda c, t0, tn: h2[:, c, t0:t0 + tn], "h2", None)
                for g in range(16):
                    sl_u = WS.get((wsrc(w_up[l], g * 512, 512), NCH, 512))
                    sl_d = WS.get((wsrc(w_down[l], 0, 2048, row0=g * 512), 4, 2048))
                    if planning:
                        continue
                    ab = aT[g % 2]
                    wslab, slot = sl_u
                    for k in range(4):
                        for t, (t0, tn) in enumerate(TT):
                            pi = ps_next()
                            for c in range(NCH):
                                mm(sch, psum[pi][:, 0:tn], wslab[:, c, k * 128:(k + 1) * 128], h2[:, c, t0:t0 + tn], c == 0, c == NCH - 1,
                                   r=[("h2", c, t)] + rg(slot, c), w=[("ps", pi)])
                            rb = rl[(k * 3 + t) % 2]
                            dts(sch, rb[:, 0:tn], psum[pi][:, 0:tn], 0.0, None, ALU.max, ALU.bypass, r=[("ps", pi)], w=[("rl", (k * 3 + t) % 2)])
                            act(sch, ab[:, k, t0:t0 + tn], rb[:, 0:tn], AF.Square, r=[("rl", (k * 3 + t) % 2)], w=[("aT", g % 2, k, t)])
                    wslab, slot = sl_d
                    for cb in range(NCH):
                        for t, (t0, tn) in enumerate(TT):
                            pi = ps_next()
                            for k in range(4):
                                mm(sch, psum[pi][:, 0:tn], wslab[:, k, cb * 128:(cb + 1) * 128], ab[:, k, t0:t0 + tn], k == 0, k == 3,
                                   r=[("aT", g % 2, k, t)] + rg(slot, k, 2048), w=[("ps", pi)])
                            dtt(sch, XT[:, cb, t0:t0 + tn], XT[:, cb, t0:t0 + tn], psum[pi][:, 0:tn], ALU.add,
                                r=[gXT(cb, t), ("ps", pi)], w=[gXT(cb, t)])
                    WS.release(2)
                sch.barrier()
                ctx["stop"](f"p4_{l}", sch)

            if not planning:
                R2.reset(); R3.reset()
                sqbuf = [R3.get([4, 512]) for _ in range(2)]
                rstd_t = R3.get([512], F32)
                yt = [R3.get([D], F32) for _ in range(2)]
                YT = sb(R2_0, [8, TO], F32)
                for half in range(2):
                    for t, (t0, tn) in enumerate(TT):
                        pi = ps_next()
                        ps = psum[pi][:, 0:tn]
                        for cg in range(4):
                            sq = sqbuf[cg % 2][:, :, 0:tn]
                            act(sch, sq, XT[:, 4 * cg:4 * cg + 4, t0:t0 + tn], AF.Square,
                                r=[gXT(c, t) for c in range(4 * cg, 4 * cg + 4)], w=[("sq", cg % 2)])
                            for k in range(4):
                                c = 4 * cg + k
                                mm(sch, ps, ones_bf, sq[:, k, :], start=(c == 0), stop=(c == NCH - 1), r=[("sq", cg % 2)], w=[("ps", pi)])
                        rs = rstd_t[:, 0:tn]
                        act(sch, rs, ps, AF.Ln, r=[("ps", pi)], w=["rstd"], scale=1.0 / D, bias=EPS)
                        act(sch, rs, rs, AF.Exp, r=["rstd"], w=["rstd"], scale=-0.5)
                        for c8 in range(8):
                            c = 8 * half + c8
                            dstt(sch, YT[:, c8, t0:t0 + tn], XT[:, c, t0:t0 + tn], gcols[:, 64 + c:64 + c + 1], rs,
                                 ALU.mult, ALU.mult, r=[gXT(c, t), "rstd"], w=[("YT", c8, t)])
                    for i in range(9):
                        c0 = i * 128 if i < 8 else TO - 128
                        r0 = 0 if i < 8 else 64
                        rt = [("YT", c8_, t_) for c8_ in range(8) for t_ in ((i // 4,) if i < 8 else (1, 2))]
                        yb = yt[i % 2]
                        for b4 in range(2):
                            pi = ps_next()
                            for k in range(4):
                                c8 = 4 * b4 + k
                                tr(sch, psum[pi][:, k * 128:(k + 1) * 128], YT[:, c8, c0:c0 + 128], ident_f,
                                   r=[g for g in rt if g[1] == c8], w=[("ps", pi)])
                            if b4 == 0:
                                act(sch, yb[r0:128, 0:512], psum[pi][r0:128, :], AF.Copy, r=[("ps", pi)], w=[("yt", i % 2, b4)])
                            else:
                                dcopy(sch, yb[r0:128, 512:1024], psum[pi][r0:128, :], r=[("ps", pi)], w=[("yt", i % 2, b4)])
                        dma(sch, "sp", y_d[c0 + r0:c0 + 128, half * 1024:(half + 1) * 1024], yb[r0:128, 0:1024],
                            r=[("yt", i % 2, 0), ("yt", i % 2, 1)], w=[("y", i, half)])
                sch.barrier()

        dbg_t = {}

        def dbg(sch, name, ap, shape, dt=F32):
            if not DEBUG or sch.plan:
                return
            if name not in dbg_t:
                dbg_t[name] = nc.dram_tensor("dbg_" + name, list(shape), dt, kind="ExternalOutput").ap()
            sch.barrier()
            sch.op("sp", lambda e: e.dma_start(out=dbg_t[name], in_=ap), kind="d")
            sch.barrier()
        ctx["dbg"] = dbg

        def stop(tag, sch):
            if STOP_AFTER == tag:
                sch.barrier()
                raise _Stop()
        ctx["stop"] = stop
        try:
            build(PLAN)
        except _Stop:
            pass
        WS.planning = False
        try:
            build(S)
        except _Stop:
            pass

        run = S.emit(nc, None, esem, dsem, ccsem)

        @block.tensor
        def _(e):
            run("pe", e)

        @block.scalar
        def _(e):
            run("act", e)

        @block.vector
        def _(e):
            run("dve", e)

        @block.gpsimd
        def _(e):
            with e.register("r_off0") as r0, e.register("r_off1") as r1:
                e.reg_load(r0, metai_d[0:1, 0:1])
                e.reg_load(r1, metai_d[0:1, 1:2])
                ctx["off"] = [e.snap(r0), e.snap(r1)]
                run("pool", e)

        @block.sync
        def _(e):
            run("sp", e)

    return nc


_NC = None


def _make_in_maps(x_prompt, x_sample, cache_sb_k, cache_sb_v, cache_diff_k, cache_diff_v,
                  norm1_g, w_in, lambda_q1, lambda_k1, lambda_q2, lambda_k2, diff_norm_g,
                  w_branch_sb, w_branch_diff, w_out, norm2_g, w_up, w_down, final_norm_g):
    f = lambda a: np.ascontiguousarray(np.asarray(a, dtype=np.float32))
    x_prompt, x_sample = f(x_prompt), f(x_sample)
    w_in = f(w_in)
    w_my = []
    for hh in range(2):
        cols = []
        for sec in range(6):
            cols.append(w_in[:, :, sec * 1024 + hh * 512: sec * 1024 + (hh + 1) * 512])
        w_my.append(np.ascontiguousarray(np.concatenate(cols, axis=2)))
    gc = np.stack([f(norm1_g)[0], f(norm1_g)[1], f(norm2_g)[0], f(norm2_g)[1], f(final_norm_g)], 0)
    gcols = np.ascontiguousarray(gc.reshape(5, 16, 128).transpose(2, 0, 1).reshape(128, 80))
    dng = np.ascontiguousarray(f(diff_norm_g).T)
    lv = np.stack([f(lambda_q1), f(lambda_k1), f(lambda_q2), f(lambda_k2)], 1)
    lamv = np.ascontiguousarray(np.broadcast_to(lv.reshape(1, 512), (128, 512)))
    shared = dict(w_in=w_in, w_bsb=f(w_branch_sb), w_bdf=f(w_branch_diff), w_out=f(w_out), w_up=f(w_up), w_down=f(w_down),
                  gcols=gcols, dng=dng, lamv=lamv)
    cs = [f(cache_sb_k), f(cache_sb_v), f(cache_diff_k), f(cache_diff_v)]
    in_maps = []
    for c in range(8):
        b, hh = c // 2, c % 2
        m = dict(shared)
        m["xin"] = np.ascontiguousarray(np.concatenate([x_prompt[b, hh * 1024:(hh + 1) * 1024], x_sample[c]], 0))
        m["w_my"] = w_my[hh]
        for nme, arr in zip(("c_sbk", "c_sbv", "c_dfk", "c_dfv"), cs):
            m[nme] = np.ascontiguousarray(arr[:, c].reshape(DEPTH, PAST, 1024))
        m["meta_f"] = np.full((128, 1), float(hh), np.float32)
        m["meta_i"] = np.array([[hh * 512, hh * 512 + 1024]], np.int32)
        in_maps.append(m)
    return in_maps


def _assemble(R):
    y_prompt = np.empty((4, SEQ, D), np.float32)
    y_sample = np.empty((8, TS, D), np.float32)
    pk = [np.empty((DEPTH, 4, SEQ, 8, 128), np.float32) for _ in range(4)]
    sk = [np.empty((DEPTH, 8, TS, 8, 128), np.float32) for _ in range(4)]
    for c in range(8):
        b, hh = c // 2, c % 2
        y = R[c]["y"]
        y_prompt[b, hh * 1024:(hh + 1) * 1024] = y[:TP]
        y_sample[c] = y[TP:]
        for i, nme in enumerate(("p_sbk", "p_sbv", "p_dfk", "p_dfv")):
            pk[i][:, b, :, 4 * hh:4 * hh + 4, :] = R[c][nme].reshape(DEPTH, SEQ, 4, 128)
        for i, nme in enumerate(("s_sbk", "s_sbv", "s_dfk", "s_dfv")):
            sk[i][:, c] = R[c][nme].reshape(DEPTH, TS, 8, 128)
    return (y_prompt, y_sample, pk[0], pk[1], pk[2], pk[3], sk[0], sk[1], sk[2], sk[3])


def kernel(**inputs):
    global _NC
    if _NC is None:
        _NC = build_program()
    in_maps = _make_in_maps(**inputs)
    res = run_bass_kernel_spmd(_NC, in_maps, core_ids=list(range(8)))
    return _assemble(res.results)
```

```python
import math
import numpy as np
import ml_dtypes
import concourse.bass as bass
import concourse.mybir as mybir
from concourse.bass_utils import run_bass_kernel_spmd

F32 = mybir.dt.float32
BF16 = mybir.dt.bfloat16
I32 = mybir.dt.int32
AF = mybir.ActivationFunctionType
ALU = mybir.AluOpType

D = 2048
NCH = 16
TP = 1024
TS = 64
TO = TP + TS
SEQ = 2048
PAST = 1024
DEPTH = 2
EPS = 1e-6
TT = [(0, 512), (512, 512), (1024, 64)]
NEG = -30000.0
DEBUG = False
GROUPS = [[0, 1], [2, 3], [4, 5], [6, 7]]
SMALLW = False
STOP_AFTER = None


class _Stop(Exception):
    pass

ARENA = 106000


class Sched:
    ENG = ["pe", "act", "dve", "pool", "sp"]
    NDS = 30

    def __init__(self, plan):
        self.plan = plan
        self.ops = []
        self.lastw = {}
        self.rd = {}
        self.last_of = {}
        self.dmas_out = []
        self.last_cc = None

    def op(self, eng, fn, r=(), w=(), kind="c", extra=()):
        if self.plan:
            return -1
        i = len(self.ops)
        if eng in ("act", "dve"):
            pr = [g for g in r if isinstance(g, tuple) and g[0] == "ps"]
            if pr:
                w = list(w) + [g for g in pr if g not in w]
        deps = set(extra)
        for g in r:
            lw = self.lastw.get(g)
            if lw is not None:
                deps.add(lw)
        for g in w:
            lw = self.lastw.get(g)
            if lw is not None:
                deps.add(lw)
            deps.update(self.rd.get(g, ()))
        if kind == "x" and self.last_cc is not None:
            deps.add(self.last_cc)
        deps.discard(i)
        self.ops.append(dict(eng=eng, fn=fn, deps=deps, kind=kind))
        for g in r:
            lst = self.rd.setdefault(g, [])
            if kind == "c":
                for k in range(len(lst)):
                    if self.ops[lst[k]]["kind"] == "c" and self.ops[lst[k]]["eng"] == eng:
                        lst[k] = i
                        break
                else:
                    lst.append(i)
            else:
                lst.append(i)
        for g in w:
            self.lastw[g] = i
            self.rd[g] = []
        if kind == "c":
            self.last_of[eng] = i
        else:
            self.dmas_out.append(i)
            if kind == "x":
                self.last_cc = i
        return i

    def barrier(self):
        if self.plan:
            return
        deps = set(self.last_of.values()) | set(d for d in self.dmas_out if self.ops[d]["kind"] != "x")
        for e in self.ENG:
            i = len(self.ops)
            self.ops.append(dict(eng=e, fn=None, deps=set(deps), kind="c"))
            self.last_of[e] = i
        self.dmas_out = []
        self.lastw = {g: i for g, i in self.lastw.items() if self.ops[i]["kind"] == "x"}
        self.rd = {}

    def emit(self, nc, handles, esem, dsem, ccsem):
        ops = self.ops
        for o in ops:
            o["tgt"] = False
        for i, o in enumerate(ops):
            for d in o["deps"]:
                od = ops[d]
                if od["kind"] != "c":
                    continue
                if od["eng"] == "pe" and o["eng"] == "pe" and o["kind"] == "c":
                    continue
                od["tgt"] = True
        cnt = {e: 0 for e in self.ENG}
        dcnt = {"sp": 0, "pool": 0}
        cccnt = 0
        for i, o in enumerate(ops):
            e = o["eng"]
            if o["kind"] == "c":
                if o["fn"] is None:
                    o["sig"] = ("E", e, None)
                elif o["tgt"]:
                    cnt[e] += 1
                    o["sig"] = (esem[e], cnt[e])
                else:
                    o["sig"] = None
            elif o["kind"] == "d":
                k = dcnt[e]
                dcnt[e] += 1
                s = dsem[e][k % self.NDS]
                o["sig"] = (s, 16 * (k // self.NDS + 1))
                o["prev"] = (s, 16 * (k // self.NDS)) if k >= self.NDS else None
            else:
                cccnt += 1
                o["sig"] = (ccsem, cccnt)
        per = {e: [] for e in self.ENG}
        for i, o in enumerate(ops):
            per[o["eng"]].append(i)
        waited = {e: {} for e in self.ENG}

        def run(e, h):
            wt = waited[e]
            for i in per[e]:
                o = ops[i]
                need = {}
                for d in o["deps"]:
                    od = ops[d]
                    if od["kind"] == "c":
                        if od["fn"] is None:
                            continue
                        if od["eng"] == "pe" and e == "pe" and o["kind"] == "c":
                            continue
                    sg = od["sig"]
                    if sg is None:
                        raise RuntimeError("dep on non-target op")
                    key = id(sg[0])
                    if key not in need or need[key][1] < sg[1]:
                        need[key] = sg
                if o["kind"] == "d" and o["prev"] is not None:
                    sg = o["prev"]
                    key = id(sg[0])
                    if key not in need or need[key][1] < sg[1]:
                        need[key] = sg
                for key, (s, v) in need.items():
                    if wt.get(key, 0) >= v:
                        continue
                    h.wait_ge(s, v)
                    wt[key] = v
                if o["fn"] is None:
                    continue
                ins = o["fn"](h)
                sg = o["sig"]
                if sg is not None:
                    if o["kind"] == "c":
                        ins.then_inc(sg[0], 1)
                    elif o["kind"] == "d":
                        ins.then_inc(sg[0], 16)
                    else:
                        ins.then_inc(sg[0])
        return run


def build_program():
    nc = bass.Bass("TRN2", target_bir_lowering=False)

    def din(name, shape, dt=F32):
        return nc.dram_tensor(name, list(shape), dt, kind="ExternalInput").ap()

    def dout(name, shape, dt=F32):
        return nc.dram_tensor(name, list(shape), dt, kind="ExternalOutput").ap()

    xin = din("xin", [TO, D])
    if SMALLW:
        w_my = w_in = w_bsb = w_bdf = w_out = w_up = w_down = din("w_small", [DEPTH, 8192, 128])
    else:
        w_my = din("w_my", [DEPTH, D, 3072])
        w_in = din("w_in", [DEPTH, D, 10240])
        w_bsb = din("w_bsb", [DEPTH, 1024, D])
        w_bdf = din("w_bdf", [DEPTH, 1024, D])
        w_out = din("w_out", [DEPTH, D, D])
        w_up = din("w_up", [DEPTH, D, 8192])
        w_down = din("w_down", [DEPTH, 8192, D])
    caches = [din(n, [DEPTH, PAST, 1024]) for n in ("c_sbk", "c_sbv", "c_dfk", "c_dfv")]
    gcols_d = din("gcols", [128, 80])
    dng_d = din("dng", [128, 2])
    lamv_d = din("lamv", [128, 512])
    metaf_d = din("meta_f", [128, 1])
    metai_d = din("meta_i", [1, 2], I32)

    y_d = dout("y", [TO, D])
    pouts = [dout(n, [DEPTH, SEQ, 512]) for n in ("p_sbk", "p_sbv", "p_dfk", "p_dfv")]
    souts = [dout(n, [DEPTH, TS, 1024]) for n in ("s_sbk", "s_sbv", "s_dfk", "s_dfv")]

    hsrc = [[nc.dram_tensor(f"hsrc{l}_{f}", [1024, 1024], BF16) for f in range(2)] for l in range(DEPTH)]
    hall = [[nc.dram_tensor(f"hall{l}_{f}", [2048, 1024], BF16) for f in range(2)] for l in range(DEPTH)]
    osrc = [[nc.dram_tensor(f"osrc{l}_{t}", [1024, 1024], BF16) for t in range(2)] for l in range(DEPTH)]
    oall = [[nc.dram_tensor(f"oall{l}_{t}", [2048, 1024], BF16) for t in range(2)] for l in range(DEPTH)]
    xsp = [nc.dram_tensor(f"xsp{l}", [NCH, 128, TO], F32) for l in range(DEPTH)]

    lam_init = [0.8 - 0.6 * math.exp(-0.3 * l) for l in range(DEPTH)]

    import contextlib
    es = contextlib.ExitStack()
    with es:
        arena_t = es.enter_context(nc.sbuf_tensor("arena", [128, ARENA], BF16))
        psum = [es.enter_context(nc.psum_tensor(f"ps{i}", [128, 512], F32)) for i in range(8)]
        esem = {e: es.enter_context(nc.semaphore(f"s_{e}")) for e in Sched.ENG}
        dsem = {q: [es.enter_context(nc.semaphore(f"d_{q}{i}")) for i in range(Sched.NDS)] for q in ("sp", "pool")}
        ccsem = es.enter_context(nc.semaphore("s_cc"))
        block = es.enter_context(nc.Block())

        def sb(off, shape, dt=BF16):
            n = int(np.prod(shape))
            if dt in (F32, I32):
                ap = arena_t[:, off:off + 2 * n].bitcast(dt)
            else:
                ap = arena_t[:, off:off + n]
            if len(shape) == 2:
                ap = ap.rearrange("p (a b) -> p a b", b=shape[1])
            elif len(shape) == 3:
                ap = ap.rearrange("p (a b c) -> p a b c", b=shape[1], c=shape[2])
            return ap

        class Bump:
            def __init__(self, lo, hi):
                self.lo, self.hi, self.p = lo, hi, lo

            def reset(self):
                self.p = self.lo

            def get(self, shape, dt=BF16):
                n = int(np.prod(shape)) * (2 if dt in (F32, I32) else 1)
                n = (n + 1) // 2 * 2
                off = self.p
                self.p += n
                assert self.p <= self.hi, ("arena overflow", self.p, self.hi)
                return sb(off, shape, dt)

        CONST = Bump(0, 12800)
        RING0 = 12800
        NRING = 3
        SLAB = 8192
        XT0 = RING0 + NRING * SLAB
        XT_N = NCH * TO * 2
        R2_0 = XT0 + XT_N
        R2_N = NCH * TO
        R3_0 = R2_0 + R2_N
        R3_N = ARENA - R3_0
        assert R3_N >= 15872
        RXT = Bump(XT0, XT0 + XT_N)
        R2 = Bump(R2_0, R2_0 + R2_N)
        R3 = Bump(R3_0, R3_0 + R3_N)

        ident_bf = CONST.get([128])
        ident_f = CONST.get([128], F32)
        ones_bf = CONST.get([128])
        ones_f = CONST.get([128], F32)
        negones_bf = CONST.get([128])
        negU_bf = CONST.get([128])
        negtri_bf = CONST.get([128])
        dbase_f = CONST.get([128], F32)
        diag_cur = [CONST.get([128]) for _ in range(2)]
        kaug = CONST.get([2048])
        qbase_hi = CONST.get([2048])
        qaug = [CONST.get([2048]) for _ in range(2)]
        gcols = CONST.get([80], F32)
        dng = CONST.get([2], F32)
        metaf = CONST.get([1], F32)
        slope_my = CONST.get([4], F32)
        nslope_my = CONST.get([4], F32)
        neglam = CONST.get([2], F32)
        gdn = CONST.get([2], F32)
        lamt = CONST.get([8], F32)
        hS = CONST.get([NCH, TS])
        oS = CONST.get([NCH, TS])
        lamv = R3.get([512], F32)

        S = Sched(plan=False)
        PLAN = Sched(plan=True)
        ctx = {}

        class WStream:
            def __init__(self):
                self.order = []
                self.pos = 0
                self.issued = 0
                self.released = 0
                self.planning = True

            def _issue(self, k):
                desc = self.order[k]
                slot = k % NRING
                src_fn, nck, ncol = desc
                dst = sb(RING0 + slot * SLAB, [nck, ncol])
                step = 4 if ncol <= 512 else 1
                for c0 in range(0, nck, step):
                    c1 = min(nck, c0 + step)
                    S.op("pool", (lambda e, c0=c0, c1=c1, dst=dst, src_fn=src_fn: e.dma_start(out=dst[:, c0:c1, :], in_=src_fn(c0, c1))),
                         w=[g for c in range(c0, c1) for g in rg(slot, c, ncol)], kind="d")

            def _pump(self):
                while self.issued < min(len(self.order), self.released + NRING):
                    self._issue(self.issued)
                    self.issued += 1

            def get(self, desc):
                if self.planning:
                    self.order.append(desc)
                    return None, None
                k = self.pos
                self.pos += 1
                self._pump()
                assert self.issued > k, "weight ring deadlock: slab requested before a slot was released"
                slot = k % NRING
                _, nck, ncol = desc
                return sb(RING0 + slot * SLAB, [nck, ncol]), slot

            def release(self, n=1):
                if self.planning:
                    return
                self.released += n
                self._pump()

        def rg(slot, c, ncol=512):
            per = ncol // 512
            return [("ring", slot, c * per + i) for i in range(per)]

        WS = WStream()

        def wsrc(w_ap_l, col0, ncol, row0=0):
            def f(c0, c1):
                return w_ap_l[row0 + c0 * 128: row0 + c1 * 128, col0:col0 + ncol].rearrange("(c p) n -> p c n", p=128)
            return f

        def mm(sch, out, lhsT, rhs, start, stop, r, w):
            sch.op("pe", lambda e: e.matmul(out, lhsT, rhs, start=start, stop=stop, skip_group_check=True), r=r, w=w)

        def tr(sch, out, in_, ident, r, w):
            sch.op("pe", lambda e: e.transpose(out, in_, ident), r=r, w=w)

        def act(sch, out, in_, func, r, w, scale=1.0, bias=0.0):
            sch.op("act", lambda e: e.activation(out, in_, func, bias=bias, scale=scale), r=r, w=w)

        def dcopy(sch, out, in_, r, w):
            sch.op("dve", lambda e: e.tensor_copy(out, in_), r=r, w=w)

        def dtt(sch, out, a, b, op, r, w):
            sch.op("dve", lambda e: e.tensor_tensor(out, a, b, op), r=r, w=w)

        def dstt(sch, out, in0, scalar, in1, op0, op1, r, w):
            sch.op("dve", lambda e: e.scalar_tensor_tensor(out, in0, scalar, in1, op0, op1), r=r, w=w)

        def dts(sch, out, in0, s1, s2, op0, op1, r, w):
            sch.op("dve", lambda e: e.tensor_scalar(out, in0, s1, s2, op0, op1), r=r, w=w)

        def dmemset(sch, ap, val, w):
            sch.op("dve", lambda e: e.memset(ap, val), w=w)

        def dma(sch, q, out, in_, r, w):
            return sch.op(q, lambda e: e.dma_start(out=out, in_=in_), r=r, w=w, kind="d")

        psc = [0]

        def ps_next():
            i = psc[0] % 6
            psc[0] += 1
            return i

        def psbf(i):
            return psum[i][:, :].bitcast(BF16)

        def build(sch):
            planning = sch.plan
            WS.planning = planning
            WS.pos = 0
            WS.issued = 0
            psc[0] = 0

            if not planning:
                P = lambda fn, w=(), r=(): sch.op("pool", fn, r=r, w=w)
                P(lambda e: e.memset(ident_bf, 0.0), w=["c_identbf"])
                P(lambda e: e.affine_select(out=ident_bf, in_=ident_bf, pattern=[[-1, 128]], compare_op=ALU.not_equal,
                                            fill=1.0, base=0, channel_multiplier=1), r=["c_identbf"], w=["c_identbf"])
                P(lambda e: e.memset(ident_f, 0.0), w=["c_identf"])
                P(lambda e: e.affine_select(out=ident_f, in_=ident_f, pattern=[[-1, 128]], compare_op=ALU.not_equal,
                                            fill=1.0, base=0, channel_multiplier=1), r=["c_identf"], w=["c_identf"])
                P(lambda e: e.memset(ones_bf, 1.0), w=["c_ones"])
                P(lambda e: e.memset(ones_f, 1.0), w=["c_onesf"])
                P(lambda e: e.memset(negones_bf, -1.0), w=["c_negones"])
                P(lambda e: e.memset(negU_bf, -1.0), w=["c_negU"])
                P(lambda e: e.affine_select(out=negU_bf, in_=negU_bf, pattern=[[-1, 128]], compare_op=ALU.is_ge,
                                            fill=0.0, base=0, channel_multiplier=1), r=["c_negU"], w=["c_negU"])
                P(lambda e: e.memset(negtri_bf, 0.0), w=["c_negtri"])
                P(lambda e: e.affine_select(out=negtri_bf, in_=negtri_bf, pattern=[[1, 128]], compare_op=ALU.is_gt,
                                            fill=NEG, base=0, channel_multiplier=-1), r=["c_negtri"], w=["c_negtri"])
                RXT.reset()
                dbase_i = RXT.get([128], I32)
                P(lambda e: e.iota(dbase_i, [[-1, 128]], base=0, channel_multiplier=1), w=["s_dbasei"])
                P(lambda e: e.tensor_copy(dbase_f, dbase_i), r=["s_dbasei"], w=["c_dbase"])
                P(lambda e: e.tensor_scalar(dbase_f, dbase_f, 0.0, -16.0, ALU.max, ALU.mult), r=["c_dbase"], w=["c_dbase"])
                rows_i = RXT.get([2, 2048], I32)
                rows = RXT.get([2, 2048], F32)
                rows_bf = RXT.get([4, 2048])
                qrows_bf = RXT.get([4, 2048])
                P(lambda e: e.iota(rows_i[0:1, 0, :], [[512, 32], [0, 64]], base=0, channel_multiplier=0), w=["s_rowsi"])
                P(lambda e: e.iota(rows_i[0:1, 1, :], [[0, 32], [8, 64]], base=0, channel_multiplier=0), r=["s_rowsi"], w=["s_rowsi"])
                P(lambda e: e.tensor_copy(rows[0:1, :, :], rows_i[0:1, :, :]), r=["s_rowsi"], w=["s_rows"])
                P(lambda e: e.tensor_copy(rows_bf[0:1, 0:2, :], rows[0:1, :, :]), r=["s_rows"], w=["s_rowsbf"])
                P(lambda e: e.memset(rows_bf[0:1, 2:4, :], 1.0), r=["s_rowsbf"], w=["s_rowsbf"])
                P(lambda e: e.memset(qrows_bf[0:1, 0:2, :], 1.0), w=["s_qrows"])
                P(lambda e: e.tensor_scalar(qrows_bf[0:1, 2:4, :], rows[0:1, 0:2, :], -1.0, None, ALU.mult),
                  r=["s_rows", "s_qrows"], w=["s_qrows"])
                P(lambda e: e.memset(kaug, 0.0), w=["c_kaug"])
                P(lambda e: e.memset(qbase_hi, 0.0), w=["c_qbase"])
                for base in (0,):
                    for rr in range(4):
                        sch.op("sp", lambda e, base=base, rr=rr: e.dma_start(out=kaug[base + rr:base + rr + 1, :],
                                                                              in_=rows_bf[0:1, rr, :]),
                               r=["s_rowsbf", "c_kaug"], w=[("c_kaug_r", base, rr)], kind="d")
                        sch.op("sp", lambda e, base=base, rr=rr: e.dma_start(out=qbase_hi[base + rr:base + rr + 1, :],
                                                                              in_=qrows_bf[0:1, rr, :]),
                               r=["s_qrows", "c_qbase"], w=[("c_qbase_r", base, rr)], kind="d")
                dma(sch, "sp", gcols, gcols_d[:, :], r=[], w=["c_gcols"])
                dma(sch, "sp", dng, dng_d[:, :], r=[], w=["c_dng"])
                dma(sch, "sp", metaf, metaf_d[:, :], r=[], w=["c_metaf"])
                dma(sch, "sp", lamv, lamv_d[:, :], r=[], w=["s_lamv"])
                for j in range(4):
                    dts(sch, slope_my[:, j:j + 1], metaf, -0.9375, 1.0, ALU.mult, ALU.add, r=["c_metaf"], w=[("c_slope", j)])
                    dts(sch, slope_my[:, j:j + 1], slope_my[:, j:j + 1], 2.0 ** -(j + 1), None, ALU.mult, ALU.bypass,
                        r=[("c_slope", j)], w=[("c_slope", j)])
                lprod = R3.get([64], F32)
                for l in range(DEPTH):
                    for i in range(2):
                        a = lamv[:, (l * 4 + 2 * i) * 64:(l * 4 + 2 * i + 1) * 64]
                        b = lamv[:, (l * 4 + 2 * i + 1) * 64:(l * 4 + 2 * i + 2) * 64]
                        sch.op("dve", lambda e, a=a, b=b, l=l, i=i: e.scalar_tensor_tensor(
                            lprod, a, 1.0, b, ALU.mult, ALU.mult, accum_out=lamt[:, l * 2 + i:l * 2 + i + 1]),
                            r=["s_lamv"], w=["s_lprod", ("c_lamt", l, i)])
                    act(sch, lamt[:, 4 + l * 2:4 + l * 2 + 2], lamt[:, l * 2:l * 2 + 2], AF.Exp,
                        r=[("c_lamt", l, 0), ("c_lamt", l, 1)], w=[("c_lame", l)])
                    dstt(sch, neglam[:, l:l + 1], lamt[:, 4 + l * 2 + 1:4 + l * 2 + 2], -lam_init[l],
                         lamt[:, 4 + l * 2:4 + l * 2 + 1], ALU.add, ALU.subtract, r=[("c_lame", l)], w=[("c_neglam", l)])
                    dts(sch, gdn[:, l:l + 1], dng[:, l:l + 1], 1.0 - lam_init[l], None, ALU.mult, ALU.bypass,
                        r=["c_dng"], w=[("c_gdn", l)])
                sch.barrier()
                R3.reset()
            ctx["stop"]("setup", sch)

            CG = ["c_consts"]

            XT = sb(XT0, [NCH, TO], F32)

            def gXT(c, t):
                return ("XT", c, t)

            def norm_to(sch, l_g, dst, dstname, sq_bump):
                for t, (t0, tn) in enumerate(TT):
                    pi = ps_next()
                    ps = psum[pi][:, 0:tn]
                    for cg in range(4):
                        sq = sqbuf[cg % 2][:, :, 0:tn]
                        act(sch, sq, XT[:, 4 * cg:4 * cg + 4, t0:t0 + tn], AF.Square,
                            r=[gXT(c, t) for c in range(4 * cg, 4 * cg + 4)], w=[("sq", cg % 2)])
                        for k in range(4):
                            c = 4 * cg + k
                            mm(sch, ps, ones_bf, sq[:, k, :], start=(c == 0), stop=(c == NCH - 1),
                               r=[("sq", cg % 2)], w=[("ps", pi)])
                    rs = rstd_t[:, 0:tn]
                    act(sch, rs, ps, AF.Ln, r=[("ps", pi)], w=["rstd"], scale=1.0 / D, bias=EPS)
                    act(sch, rs, rs, AF.Exp, r=["rstd"], w=["rstd"], scale=-0.5)
                    for c in range(NCH):
                        dstt(sch, dst(c, t0, tn), XT[:, c, t0:t0 + tn], gcols[:, l_g * 16 + c:l_g * 16 + c + 1], rs,
                             ALU.mult, ALU.mult, r=[gXT(c, t), "rstd"], w=[(dstname, c, t)])

            for l in range(DEPTH):
                R2.reset(); R3.reset()
                hT = R2.get([NCH, TO])
                sqbuf = [R3.get([4, 512]) for _ in range(2)]
                rstd_t = R3.get([512], F32)
                if l == 0:
                    xt = [R3.get([D], F32) for _ in range(2)]
                    ntile = 9
                    for i in range(ntile):
                        n = 128 if i < 8 else TS
                        t = i // 4 if i < 8 else 2
                        xb = xt[i % 2]
                        dma(sch, "sp", xb[0:n, :], xin[i * 128:i * 128 + n, :], r=[], w=[("xt", i % 2)])
                        for b4 in range(4):
                            pi = ps_next()
                            for k in range(4):
                                c = 4 * b4 + k
                                tr(sch, psum[pi][:, k * 128:k * 128 + n], xb[0:n, c * 128:(c + 1) * 128], ident_f[0:n, 0:n],
                                   r=[("xt", i % 2)], w=[("ps", pi)])
                            src = psum[pi][:, :].rearrange("p (a b) -> p a b", b=128)[:, :, 0:n]
                            dstap = XT[:, 4 * b4:4 * b4 + 4, i * 128:i * 128 + n]
                            if b4 % 2 == 0:
                                act(sch, dstap, src, AF.Copy, r=[("ps", pi)], w=[("XTp", c, i) for c in range(4 * b4, 4 * b4 + 4)])
                            else:
                                dcopy(sch, dstap, src, r=[("ps", pi)], w=[("XTp", c, i) for c in range(4 * b4, 4 * b4 + 4)])
                    sch.barrier()
                norm_to(sch, l, lambda c, t0, tn: hT[:, c, t0:t0 + tn], "hT", None)
                dcopy(sch, hS, hT[:, :, TP:TO], r=[("hT", c, 2) for c in range(NCH)], w=["hS"])
                for c in range(NCH):
                    dma(sch, "sp", xsp[l].ap()[c, :, :], XT[:, c, :], r=[gXT(c, t) for t in range(3)], w=[("xsp", l, c)])
                for f in range(2):
                    dma(sch, "sp", hsrc[l][f].ap().rearrange("(c p) n -> p c n", p=128), hT[:, 8 * f:8 * f + 8, 0:TP],
                        r=[("hT", c, t) for c in range(8 * f, 8 * f + 8) for t in range(2)], w=[("hsrc", l, f)])
                for f in range(2):
                    sch.op("pool", lambda e, f=f, l=l: e.collective_compute(
                        "AllGather", ALU.bypass, replica_groups=GROUPS,
                        ins=[hsrc[l][f].ap().opt()], outs=[hall[l][f].ap().opt()]),
                        r=[("hsrc", l, f)], w=[("hall", l, f)], kind="x")
                sch.barrier()
                ctx["stop"](f"p1_{l}", sch)

                for ty in range(2):
                    RXT.reset(); R2.reset(); R3.reset()
                    QT = RXT.get([4, SEQ])
                    KT = RXT.get([4, SEQ])
                    V = RXT.get([16, 512])
                    QsT = RXT.get([8, TS])
                    KsT = RXT.get([8, 128])
                    Vs = RXT.get([1024])
                    if not planning:
                        dmemset(sch, KsT, 0.0, w=[("KsT", h_) for h_ in range(8)])
                        dmemset(sch, Vs, 0.0, w=[("Vs", 0), ("Vs", 1)])
                    hring = [R2.get([NCH, 512]) for _ in range(2)]
                    stage = [R3.get([512], F32) for _ in range(2)]
                    otile = [R3.get([SEQ]) for _ in range(2)]
                    sstage = R3.get([1024], F32)
                    pk_out, pv_out = pouts[2 * ty], pouts[2 * ty + 1]
                    sk_out, sv_out = souts[2 * ty], souts[2 * ty + 1]
                    qscale = (128.0 ** -0.5) if ty == 0 else 1.0

                    slabs = []
                    for s in range(3):
                        slabs.append(WS.get((wsrc(w_my[l], (3 * ty + s) * 512, 512), NCH, 512)))
                    if not planning:
                        (wq, sq_), (wk, sk_), (wv, sv_) = slabs
                        for t in range(4):
                            hb = hring[t % 2]
                            r_, half = t // 2, t % 2
                            for f in range(2):
                                dma(sch, "sp", hb[:, 8 * f:8 * f + 8, :],
                                    hall[l][f].ap()[r_ * 1024:(r_ + 1) * 1024, half * 512:(half + 1) * 512].rearrange("(c p) n -> p c n", p=128),
                                    r=[("hall", l, f)], w=[("hring", t % 2, f)])
                            rh = [("hring", t % 2, 0), ("hring", t % 2, 1)]
                            for j in range(4):
                                for which, (wslab, slot), dstT in ((0, (wq, sq_), QT), (1, (wk, sk_), KT)):
                                    pi = ps_next()
                                    for c in range(NCH):
                                        mm(sch, psum[pi][:, :], wslab[:, c, j * 128:(j + 1) * 128], hb[:, c, :],
                                           start=(c == 0), stop=(c == NCH - 1), r=rh + rg(slot, c), w=[("ps", pi)])
                                    if which == 0:
                                        act(sch, dstT[:, j, t * 512:(t + 1) * 512], psum[pi][:, :], AF.Copy, scale=qscale,
                                            r=[("ps", pi)], w=[("QT", j, t)])
                                    else:
                                        dcopy(sch, dstT[:, j, t * 512:(t + 1) * 512], psum[pi][:, :], r=[("ps", pi)], w=[("KT", j, t)])
                            if t == 0:
                                ctx["stop"]("p2a1", sch)
                            for b in range(4):
                                blk = 4 * t + b
                                for which, (wslab, slot) in ((1, (wk, sk_)), (2, (wv, sv_))):
                                    pi = ps_next()
                                    for c in range(NCH):
                                        mm(sch, psum[pi][:, :], hb[:, c, b * 128:(b + 1) * 128], wslab[:, c, :],
                                           start=(c == 0), stop=(c == NCH - 1), r=rh + rg(slot, c), w=[("ps", pi)])
                                    st = stage[which - 1]
                                    act(sch, st, psum[pi][:, :], AF.Copy, r=[("ps", pi)], w=[("stage", which)])
                                    if which == 2:
                                        dcopy(sch, V[:, blk, :], psum[pi][:, :], r=[("ps", pi)], w=[("V", blk)])
                                    dma(sch, "sp", (pk_out if which == 1 else pv_out)[l, blk * 128:(blk + 1) * 128, :], st,
                                        r=[("stage", which)], w=[("pout", ty, which, blk)])
                    WS.release(3)
                    ctx["stop"]("p2a3", sch)
                    for s in range(3):
                        for hf in range(2):
                            wslab, slot = WS.get((wsrc(w_in[l], (3 * ty + s) * 1024 + hf * 512, 512), NCH, 512))
                            if not planning:
                                if s < 2:
                                    for jj in range(4):
                                        h = hf * 4 + jj
                                        pi = ps_next()
                                        for c in range(NCH):
                                            mm(sch, psum[pi][:, 0:TS], wslab[:, c, jj * 128:(jj + 1) * 128], hS[:, c, :],
                                               start=(c == 0), stop=(c == NCH - 1), r=["hS"] + rg(slot, c), w=[("ps", pi)])
                                        if s == 0:
                                            act(sch, QsT[:, h, :], psum[pi][:, 0:TS], AF.Copy, scale=qscale, r=[("ps", pi)], w=[("QsT", h)])
                                        else:
                                            dcopy(sch, KsT[:, h, 0:TS], psum[pi][:, 0:TS], r=[("ps", pi)], w=[("KsT", h)])
                                if s >= 1:
                                    pi = ps_next()
                                    for c in range(NCH):
                                        mm(sch, psum[pi][0:TS, :], hS[:, c, :], wslab[:, c, :],
                                           start=(c == 0), stop=(c == NCH - 1), r=["hS"] + rg(slot, c), w=[("ps", pi)])
                                    act(sch, sstage[0:TS, hf * 512:(hf + 1) * 512], psum[pi][0:TS, :], AF.Copy,
                                        r=[("ps", pi)], w=[("sstage", hf)])
                                    if s == 2:
                                        dcopy(sch, Vs[0:TS, hf * 512:(hf + 1) * 512], psum[pi][0:TS, :], r=[("ps", pi)], w=[("Vs", hf)])
                                    if hf == 1:
                                        dma(sch, "sp", (sk_out if s == 1 else sv_out)[l, :, :], sstage[0:TS, :],
                                            r=[("sstage", 0), ("sstage", 1)], w=[("sout", ty, s)])
                            WS.release()
                    ctx["stop"](f"p2a_{l}_{ty}", sch)

                    if planning:
                        continue

                    sc = Bump(RXT.p, XT0 + XT_N)
                    if ty == 0:
                        e_t = [sc.get([512], F32) for _ in range(2)]
                        l1_t = [sc.get([512]) for _ in range(3)]
                        a_t = [sc.get([512]) for _ in range(3)]
                        w_t = [sc.get([512]) for _ in range(3)]
                    else:
                        E_t = [[sc.get([512]) for _ in range(3)] for _ in range(2)]
                        z_t = [sc.get([512], F32) for _ in range(2)]
                        ep = [R3.get([512], F32) for _ in range(3)]
                        qpad = [R3.get([SEQ]) for _ in range(2)]
                    uid = [0]

                    def attend(qsrc, nq, blocks, out_ap, out_w, head_consts):
                        uid[0] += 1
                        u = uid[0]
                        nb = len(blocks)
                        full = slice(0, 128)
                        if ty == 0:
                            po = 6 + (u % 2)
                            for i3 in range(3):
                                dmemset(sch, a_t[i3][:, 0:nq], 0.0, w=[("a_t", i3)])
                            st = {}

                            def A(bi):
                                bk = blocks[bi]
                                nk, d0, diag = bk["nk"], bk["d0"], bk["diag"]
                                nd = min(nk, nq - d0)
                                p1 = ps_next()
                                st[bi] = dict(p1=p1)
                                sps = psum[p1][0:nk, d0:nq]
                                mm(sch, sps, bk["k"](full), qsrc["ap"](full)[:, d0:nq], True, not diag, r=bk["r"] + qsrc["r"], w=[("ps", p1)])
                                if diag:
                                    mm(sch, psum[p1][0:nk, d0:d0 + nd], ident_bf[0:nk, 0:nk], negtri_bf[0:nk, 0:nd], False, True, r=[], w=[("ps", p1)])

                            def B(bi):
                                bk = blocks[bi]
                                nk, d0 = bk["nk"], bk["d0"]
                                p1 = st[bi]["p1"]
                                eb = e_t[bi % 2][0:nk, d0:nq]
                                lb = l1_t[bi % 3][0:nk, d0:nq]
                                act(sch, eb, psum[p1][0:nk, d0:nq], AF.Exp, r=[("ps", p1)], w=[("e_t", bi % 2)])
                                act(sch, lb, eb, AF.Ln, r=[("e_t", bi % 2)], w=[("l1_t", bi % 3)], bias=1.0)
                                if bi < nb - 1:
                                    dtt(sch, a_t[(bi + 1) % 3][0:nk, d0:nq], a_t[bi % 3][0:nk, d0:nq], lb, ALU.add,
                                        r=[("a_t", bi % 3), ("l1_t", bi % 3)], w=[("a_t", (bi + 1) % 3)])

                            def C(bi):
                                bk = blocks[bi]
                                nk, d0, diag = bk["nk"], bk["d0"], bk["diag"]
                                nd = min(nk, nq - d0)
                                p2 = ps_next()
                                st[bi]["p2"] = p2
                                tps = psum[p2][0:nk, d0:nq]
                                lb = l1_t[bi % 3][0:nk, d0:nq]
                                mm(sch, tps, bk["k"](full), qsrc["ap"](full)[:, d0:nq], True, False, r=bk["r"] + qsrc["r"], w=[("ps", p2)])
                                mm(sch, tps, negU_bf[0:nk, 0:nk], lb, False, (bi == 0) and not diag, r=[("l1_t", bi % 3)], w=[("ps", p2)])
                                if bi > 0:
                                    mm(sch, tps, negones_bf[:, 0:nk], a_t[bi % 3][:, d0:nq], False, not diag, r=[("a_t", bi % 3)], w=[("ps", p2)])
                                if diag:
                                    mm(sch, psum[p2][0:nk, d0:d0 + nd], ident_bf[0:nk, 0:nk], negtri_bf[0:nk, 0:nd], False, True, r=[], w=[("ps", p2)])

                            def Dw(bi):
                                bk = blocks[bi]
                                nk, d0 = bk["nk"], bk["d0"]
                                p2 = st[bi]["p2"]
                                act(sch, w_t[bi % 3][0:nk, d0:nq], psum[p2][0:nk, d0:nq], AF.Exp, r=[("ps", p2)], w=[("w_t", bi % 3)])

                            def E(bi):
                                bk = blocks[bi]
                                nk, d0 = bk["nk"], bk["d0"]
                                mm(sch, psum[po][:, d0:nq], bk["v"], w_t[bi % 3][0:nk, d0:nq], bi == 0, bi == nb - 1,
                                   r=bk["rv"] + [("w_t", bi % 3)], w=[("ps", po)])

                            for s_ in range(nb + 2):
                                if s_ < nb:
                                    A(s_)
                                    B(s_)
                                if 0 <= s_ - 1 < nb:
                                    C(s_ - 1)
                                    Dw(s_ - 1)
                                if 0 <= s_ - 2 < nb:
                                    E(s_ - 2)
                            act(sch, out_ap, psum[po][:, 0:nq], AF.Copy, r=[("ps", po)], w=out_w)
                        else:
                            po = [6, 7]
                            qa, dg = head_consts
                            for m in range(2):
                                dmemset(sch, z_t[m][:, 0:nq], 0.0, w=[("z_t", m)])
                            st = {}

                            def A(bi):
                                bk = blocks[bi]
                                nk, d0, diag = bk["nk"], bk["d0"], bk["diag"]
                                nd = min(nk, nq - d0)
                                st[bi] = []
                                for m in range(2):
                                    p1 = ps_next()
                                    st[bi].append(p1)
                                    sps = psum[p1][0:nk, d0:nq]
                                    mm(sch, sps, bk["k"](full), qsrc["pad"][m][:, d0:nq], True, False, r=bk["r"] + qsrc["rpad"], w=[("ps", p1)])
                                    mm(sch, sps, bk["ka"](full), qa["ap"](full)[:, qsrc["p0"] + d0:qsrc["p0"] + nq], False, not diag,
                                       r=qa["r"], w=[("ps", p1)])
                                    if diag:
                                        mm(sch, psum[p1][0:nk, d0:d0 + nd], ident_bf[0:nk, 0:nk], dg["ap"][0:nk, 0:nd], False, True,
                                           r=dg["r"], w=[("ps", p1)])

                            def B(bi):
                                bk = blocks[bi]
                                nk, d0 = bk["nk"], bk["d0"]
                                for m in range(2):
                                    p1 = st[bi][m]
                                    act(sch, E_t[m][bi % 3][0:nk, d0:nq], psum[p1][0:nk, d0:nq], AF.Exp, scale=0.125,
                                        r=[("ps", p1)], w=[("E_t", m, bi % 3)])

                            def C(bi):
                                bk = blocks[bi]
                                nk, d0 = bk["nk"], bk["d0"]
                                for m in range(2):
                                    Eb = E_t[m][bi % 3][0:nk, d0:nq]
                                    mm(sch, psum[po[m]][:, d0:nq], bk["v"], Eb, bi == 0, bi == nb - 1,
                                       r=bk["rv"] + [("E_t", m, bi % 3)], w=[("ps", po[m])])
                                    dtt(sch, z_t[m][0:nk, d0:nq], z_t[m][0:nk, d0:nq], Eb, ALU.add,
                                        r=[("z_t", m), ("E_t", m, bi % 3)], w=[("z_t", m)])

                            for s_ in range(nb + 1):
                                if s_ < nb:
                                    A(s_)
                                    B(s_)
                                if 0 <= s_ - 1 < nb:
                                    C(s_ - 1)
                            for m in range(2):
                                pz = ps_next()
                                mm(sch, psum[pz][:, 0:nq], ones_f, z_t[m][:, 0:nq], True, True, r=[("z_t", m)], w=[("ps", pz)])
                                rz = ep[m][:, 0:nq]
                                act(sch, rz, psum[pz][:, 0:nq], AF.Ln, r=[("ps", pz)], w=[("ep", m)])
                                act(sch, rz, rz, AF.Exp, scale=-1.0, r=[("ep", m)], w=[("ep", m)])
                                dtt(sch, rz, psum[po[m]][:, 0:nq], rz, ALU.mult, r=[("ps", po[m]), ("ep", m)], w=[("ep", m)])
                            o_ = ep[0][:, 0:nq]
                            dstt(sch, o_, ep[1][:, 0:nq], neglam[:, l:l + 1], o_, ALU.mult, ALU.add, r=[("ep", 0), ("ep", 1)], w=[("ep", 0)])
                            sqo = ep[2][:, 0:nq]
                            act(sch, sqo, o_, AF.Square, r=[("ep", 0)], w=[("ep", 2)])
                            pz = ps_next()
                            mm(sch, psum[pz][:, 0:nq], ones_f, sqo, True, True, r=[("ep", 2)], w=[("ps", pz)])
                            rs = ep[1][:, 0:nq]
                            act(sch, rs, psum[pz][:, 0:nq], AF.Ln, scale=1.0 / 128, bias=EPS, r=[("ps", pz)], w=[("ep", 1)])
                            act(sch, rs, rs, AF.Exp, scale=-0.5, r=[("ep", 1)], w=[("ep", 1)])
                            dstt(sch, out_ap, o_, gdn[:, l:l + 1], rs, ALU.mult, ALU.mult, r=[("ep", 0), ("ep", 1)], w=out_w)

                    def set_head_consts(slope_ap, slope_imm, buf):
                        if ty == 0:
                            return None
                        qa, dgc = qaug[buf], diag_cur[buf]
                        if slope_ap is not None:
                            dts(sch, qa, qbase_hi, slope_ap, None, ALU.mult, ALU.bypass, r=[], w=[("qaug", buf)])
                            dts(sch, dgc, dbase_f, slope_ap, None, ALU.mult, ALU.bypass, r=[], w=[("diagc", buf)])
                        else:
                            dts(sch, qa, qbase_hi, slope_imm, None, ALU.mult, ALU.bypass, r=[], w=[("qaug", buf)])
                            dts(sch, dgc, dbase_f, slope_imm, None, ALU.mult, ALU.bypass, r=[], w=[("diagc", buf)])
                        dmemset(sch, dgc[64:128, 0:64], 8 * NEG, w=[("diagc", buf)])
                        return (dict(ap=lambda rows, qa=qa: qa[rows, :], r=[("qaug", buf)]), dict(ap=dgc, r=[("diagc", buf)]))

                    def make_qpad(src_fn, n, rname):
                        if ty == 0:
                            return
                        dcopy(sch, qpad[0][0:64, 0:n], src_fn(slice(0, 64)), r=rname, w=["qpad0"])
                        dmemset(sch, qpad[0][64:128, 0:n], 0.0, w=["qpad0"])
                        dmemset(sch, qpad[1][0:64, 0:n], 0.0, w=["qpad1"])
                        dcopy(sch, qpad[1][64:128, 0:n], src_fn(slice(64, 128)), r=rname, w=["qpad1"])

                    for j in range(4):
                        hc = set_head_consts(slope_my[:, j:j + 1], None, j % 2)
                        ot = otile[j % 2]
                        make_qpad(lambda rows, j=j: QT[rows, j, :], SEQ, [("QT", j, t_) for t_ in range(4)])
                        for qt in range(4):
                            q0 = 512 * qt
                            blocks = []
                            for kb in range(4 * qt + 3, -1, -1):
                                d0 = max(0, (kb - 4 * qt) * 128)
                                blocks.append(dict(
                                    k=(lambda rows, kb=kb, j=j: KT[rows, j, kb * 128:(kb + 1) * 128]),
                                    ka=(lambda rows, kb=kb: kaug[rows, kb * 128:(kb + 1) * 128]),
                                    v=V[:, kb, j * 128:(j + 1) * 128], nk=128, d0=d0, diag=(kb >= 4 * qt),
                                    r=[("KT", j, kb // 4)], rv=[("V", kb)]))
                            qsrc = dict(ap=(lambda rows, j=j, q0=q0: QT[rows, j, q0:q0 + 512]), r=[("QT", j, qt)], p0=q0)
                            if ty == 1:
                                qsrc["pad"] = [qpad[m_][:, q0:q0 + 512] for m_ in range(2)]
                                qsrc["rpad"] = ["qpad0", "qpad1"]
                            attend(qsrc, 512, blocks, ot[:, q0:q0 + 512], [("otile", j % 2, qt)], hc)
                            if DEBUG and ty == 1 and j == 0 and qt == 0:
                                D_ = ctx["dbg"]
                                D_(sch, "kaug", kaug, [128, 2048], BF16)
                                D_(sch, "qaug", qaug[0], [128, 2048], BF16)
                                D_(sch, "diag", diag_cur[0], [128, 128], BF16)
                                D_(sch, "dbase", dbase_f, [128, 128])
                                D_(sch, "slope", slope_my, [128, 4])
                                D_(sch, "neglam", neglam, [128, 2])
                                D_(sch, "gdn", gdn, [128, 2])
                                D_(sch, "lamt", lamt, [128, 8])
                                D_(sch, "z0", z_t[0], [128, 512])
                                D_(sch, "z1", z_t[1], [128, 512])
                                D_(sch, "ep0", ep[0], [128, 512])
                                D_(sch, "ep1", ep[1], [128, 512])
                                D_(sch, "ep2", ep[2], [128, 512])
                                D_(sch, "E00", E_t[0][0], [128, 512], BF16)
                                D_(sch, "ot", ot, [128, 2048], BF16)
                                D_(sch, "QT", QT[:, 0, :], [128, 2048], BF16)
                                D_(sch, "KT", KT[:, 0, :], [128, 2048], BF16)
                                ctx["stop"]("dbg_df", sch)
                        for h2 in range(2):
                            dma(sch, "sp", osrc[l][ty].ap()[h2 * 512 + j * 128:h2 * 512 + (j + 1) * 128, :], ot[:, h2 * 1024:(h2 + 1) * 1024],
                                r=[("otile", j % 2, 2 * h2), ("otile", j % 2, 2 * h2 + 1)], w=[("osrc", l, ty, j, h2)])
                    sch.op("pool", lambda e, l=l, ty=ty: e.collective_compute(
                        "AllGather", ALU.bypass, replica_groups=GROUPS,
                        ins=[osrc[l][ty].ap().opt()], outs=[oall[l][ty].ap().opt()]),
                        r=[("osrc", l, ty, j, h2) for j in range(4) for h2 in range(2)], w=[("oall", l, ty)], kind="x")

                    sch.barrier()
                    ctx["stop"](f"p2b_{l}_{ty}", sch)
                    CB = Bump(XT0, XT0 + 24576)
                    Kc = CB.get([8, 1024])
                    Vc = CB.get([8, 1024])
                    KcT = CB.get([8, 1024])
                    for b in range(8):
                        sch.op("pool", lambda e, b=b, l=l, ty=ty, Kc=Kc: e.dma_start(out=Kc[:, b, :], in_=caches[2 * ty][l, b * 128:(b + 1) * 128, :]),
                               w=[("Kc", b)], kind="d")
                        sch.op("pool", lambda e, b=b, l=l, ty=ty, Vc=Vc: e.dma_start(out=Vc[:, b, :], in_=caches[2 * ty + 1][l, b * 128:(b + 1) * 128, :]),
                               w=[("Vc", b)], kind="d")
                    ctx["stop"]("c1", sch)
                    for h in range(8):
                        for b4 in range(2):
                            pi = ps_next()
                            pb = psbf(pi)
                            for k in range(4):
                                b = 4 * b4 + k
                                tr(sch, pb[:, k * 128:(k + 1) * 128], Kc[:, b, h * 128:(h + 1) * 128], ident_bf, r=[("Kc", b)], w=[("ps", pi)])
                            if b4 == 0:
                                dcopy(sch, KcT[:, h, 0:512], pb[:, 0:512], r=[("ps", pi)], w=[("KcT", h, 0)])
                            else:
                                act(sch, KcT[:, h, 512:1024], pb[:, 0:512], AF.Copy, r=[("ps", pi)], w=[("KcT", h, 1)])
                    ctx["stop"]("c2", sch)
                    for h in range(8):
                        hc = set_head_consts(None, 2.0 ** -(h + 1), h % 2)
                        make_qpad(lambda rows, h=h: QsT[rows, h, :], TS, [("QsT", h)])
                        blocks = [dict(k=(lambda rows, h=h: KsT[rows, h, :]), ka=(lambda rows: kaug[rows, PAST:PAST + 128]),
                                       v=Vs[:, h * 128:(h + 1) * 128], nk=128, d0=0, diag=True, r=[("KsT", h)], rv=[("Vs", h // 4)])]
                        for b in range(7, -1, -1):
                            blocks.append(dict(k=(lambda rows, h=h, b=b: KcT[rows, h, b * 128:(b + 1) * 128]),
                                               ka=(lambda rows, b=b: kaug[rows, b * 128:(b + 1) * 128]),
                                               v=Vc[:, b, h * 128:(h + 1) * 128], nk=128, d0=0, diag=False,
                                               r=[("KcT", h, b // 4)], rv=[("Vc", b)]))
                        qsrc = dict(ap=(lambda rows, h=h: QsT[rows, h, :]), r=[("QsT", h)], p0=PAST)
                        if ty == 1:
                            qsrc["pad"] = [qpad[m_][:, 0:TS] for m_ in range(2)]
                            qsrc["rpad"] = ["qpad0", "qpad1"]
                        attend(qsrc, TS, blocks, oS[:, ty * 8 + h, :], [("oS", ty, h)], hc)
                        if h == 0:
                            ctx["stop"]("c3", sch)
                        if h == 1:
                            ctx["stop"]("c4", sch)
                        if h == 7:
                            ctx["stop"]("c5", sch)
                    sch.barrier()
                    ctx["stop"](f"p2_{l}_{ty}", sch)

                RXT.reset(); R2.reset(); R3.reset()
                if not planning:
                    hTo = RXT.get([NCH, TO])
                    oT = RXT.get([NCH, TO])
                    mT = R2.get([NCH, TO])
                    gtmp = [[R3.get([512], F32) for _ in range(3)] for _ in range(2)]
                    for f in range(2):
                        dma(sch, "sp", hTo[:, 8 * f:8 * f + 8, 0:TP], hsrc[l][f].ap().rearrange("(c p) n -> p c n", p=128),
                            r=[("hsrc", l, f)], w=[("hTo", f)])
                    dcopy(sch, hTo[:, :, TP:TO], hS, r=["hS"], w=["hTo_s"])
                    dcopy(sch, oT[:, :, TP:TO], oS, r=[("oS", t_, h_) for t_ in range(2) for h_ in range(8)], w=["oT_s"])
                    for ty in range(2):
                        for r_ in range(2):
                            sch.op("pool", lambda e, ty=ty, r_=r_, l=l, oT=oT: e.dma_start(
                                out=oT[:, ty * 8 + 4 * r_:ty * 8 + 4 * r_ + 4, 0:TP],
                                in_=oall[l][ty].ap()[bass.ds(ctx["off"][r_], 512), :].rearrange("(j p) n -> p j n", p=128)),
                                r=[("oall", l, ty)], w=[("oT", ty, r_)], kind="d")
                    r_h = [("hTo", 0), ("hTo", 1), "hTo_s"]
                    r_o = [("oT", t_, r_) for t_ in range(2) for r_ in range(2)] + ["oT_s"]
                for cb4 in range(4):
                    sl_gs = WS.get((wsrc(w_in[l], 6144 + cb4 * 512, 512), NCH, 512))
                    sl_gd = WS.get((wsrc(w_in[l], 8192 + cb4 * 512, 512), NCH, 512))
                    sl_b = WS.get((lambda c0, c1, cb4=cb4, l=l: (w_bsb[l] if c0 < 8 else w_bdf[l])[(c0 % 8) * 128:((c1 - 1) % 8 + 1) * 128,
                                                                                              cb4 * 512:(cb4 + 1) * 512].rearrange("(c p) n -> p c n", p=128),
                                   NCH, 512))
                    if planning:
                        continue
                    for k in range(4):
                        cb = 4 * cb4 + k
                        cs = slice(k * 128, (k + 1) * 128)
                        for t, (t0, tn) in enumerate(TT):
                            gt = gtmp[(cb * 3 + t) % 2]
                            pg = [ps_next(), ps_next()]
                            for gi, (wslab, slot) in enumerate((sl_gs, sl_gd)):
                                for c in range(NCH):
                                    mm(sch, psum[pg[gi]][:, 0:tn], wslab[:, c, cs], hTo[:, c, t0:t0 + tn], c == 0, c == NCH - 1,
                                       r=r_h + rg(slot, c), w=[("ps", pg[gi])])
                                act(sch, gt[gi][:, 0:tn], psum[pg[gi]][:, 0:tn], AF.Sigmoid, r=[("ps", pg[gi])], w=[("gt", (cb * 3 + t) % 2, gi)])
                            pb_ = [ps_next(), ps_next()]
                            wslab, slot = sl_b
                            for bi in range(2):
                                for c in range(8):
                                    mm(sch, psum[pb_[bi]][:, 0:tn], wslab[:, 8 * bi + c, cs], oT[:, 8 * bi + c, t0:t0 + tn], c == 0, c == 7,
                                       r=r_o + rg(slot, 8 * bi + c), w=[("ps", pb_[bi])])
                            tmp = gt[2][:, 0:tn]
                            dtt(sch, tmp, gt[0][:, 0:tn], psum[pb_[0]][:, 0:tn], ALU.mult,
                                r=[("gt", (cb * 3 + t) % 2, 0), ("ps", pb_[0])], w=[("gt", (cb * 3 + t) % 2, 2)])
                            dtt(sch, gt[1][:, 0:tn], gt[1][:, 0:tn], psum[pb_[1]][:, 0:tn], ALU.mult,
                                r=[("gt", (cb * 3 + t) % 2, 1), ("ps", pb_[1])], w=[("gt", (cb * 3 + t) % 2, 1)])
                            dtt(sch, mT[:, cb, t0:t0 + tn], tmp, gt[1][:, 0:tn], ALU.add,
                                r=[("gt", (cb * 3 + t) % 2, 2), ("gt", (cb * 3 + t) % 2, 1)], w=[("mT", cb, t)])
                    WS.release(3)
                sch.barrier()
                ctx["stop"](f"p3a_{l}", sch)

                RXT.reset(); R3.reset()
                if not planning:
                    xr = [R3.get([TO], F32) for _ in range(2)]
                for cb4 in range(4):
                    sl = WS.get((wsrc(w_out[l], cb4 * 512, 512), NCH, 512))
                    if planning:
                        continue
                    wslab, slot = sl
                    for k in range(4):
                        cb = 4 * cb4 + k
                        xb = xr[cb % 2]
                        dma(sch, "sp", xb, xsp[l].ap()[cb, :, :], r=[("xsp", l, cb)], w=[("xr", cb % 2)])
                        for t, (t0, tn) in enumerate(TT):
                            pi = ps_next()
                            for c in range(NCH):
                                mm(sch, psum[pi][:, 0:tn], wslab[:, c, k * 128:(k + 1) * 128], mT[:, c, t0:t0 + tn], c == 0, c == NCH - 1,
                                   r=[("mT", c, t)] + rg(slot, c), w=[("ps", pi)])
                            dtt(sch, XT[:, cb, t0:t0 + tn], xb[:, t0:t0 + tn], psum[pi][:, 0:tn], ALU.add,
                                r=[("xr", cb % 2), ("ps", pi)], w=[gXT(cb, t)])
                    WS.release()
                sch.barrier()
                ctx["stop"](f"p3b_{l}", sch)

                R2.reset(); R3.reset()
                h2 = R2.get([NCH, TO])
                aT = [R3.get([4, TO]) for _ in range(2)]
                sqbuf = [R3.get([4, 512]) for _ in range(2)]
                rstd_t = R3.get([512], F32)
                rl = [R3.get([512], F32) for _ in range(2)]
                if not planning:
                    norm_to(sch, 2 + l, lambda c, t0, tn: h2[:, c, t0:t0 + tn], "h2", None)
                for g in range(16):
                    sl_u = WS.get((wsrc(w_up[l], g * 512, 512), NCH, 512))
                    sl_d = WS.get((wsrc(w_down[l], 0, 2048, row0=g * 512), 4, 2048))
                    if planning:
                        continue
                    ab = aT[g % 2]
                    wslab, slot = sl_u
                    for k in range(4):
                        for t, (t0, tn) in enumerate(TT):
                            pi = ps_next()
                            for c in range(NCH):
                                mm(sch, psum[pi][:, 0:tn], wslab[:, c, k * 128:(k + 1) * 128], h2[:, c, t0:t0 + tn], c == 0, c == NCH - 1,
                                   r=[("h2", c, t)] + rg(slot, c), w=[("ps", pi)])
                            rb = rl[(k * 3 + t) % 2]
                            dts(sch, rb[:, 0:tn], psum[pi][:, 0:tn], 0.0, None, ALU.max, ALU.bypass, r=[("ps", pi)], w=[("rl", (k * 3 + t) % 2)])
                            act(sch, ab[:, k, t0:t0 + tn], rb[:, 0:tn], AF.Square, r=[("rl", (k * 3 + t) % 2)], w=[("aT", g % 2, k, t)])
                    wslab, slot = sl_d
                    for cb in range(NCH):
                        for t, (t0, tn) in enumerate(TT):
                            pi = ps_next()
                            for k in range(4):
                                mm(sch, psum[pi][:, 0:tn], wslab[:, k, cb * 128:(cb + 1) * 128], ab[:, k, t0:t0 + tn], k == 0, k == 3,
                                   r=[("aT", g % 2, k, t)] + rg(slot, k, 2048), w=[("ps", pi)])
                            dtt(sch, XT[:, cb, t0:t0 + tn], XT[:, cb, t0:t0 + tn], psum[pi][:, 0:tn], ALU.add,
                                r=[gXT(cb, t), ("ps", pi)], w=[gXT(cb, t)])
                    WS.release(2)
                sch.barrier()
                ctx["stop"](f"p4_{l}", sch)

            if not planning:
                R2.reset(); R3.reset()
                sqbuf = [R3.get([4, 512]) for _ in range(2)]
                rstd_t = R3.get([512], F32)
                yt = [R3.get([D], F32) for _ in range(2)]
                YT = sb(R2_0, [8, TO], F32)
                for half in range(2):
                    for t, (t0, tn) in enumerate(TT):
                        pi = ps_next()
                        ps = psum[pi][:, 0:tn]
                        for cg in range(4):
                            sq = sqbuf[cg % 2][:, :, 0:tn]
                            act(sch, sq, XT[:, 4 * cg:4 * cg + 4, t0:t0 + tn], AF.Square,
                                r=[gXT(c, t) for c in range(4 * cg, 4 * cg + 4)], w=[("sq", cg % 2)])
                            for k in range(4):
                                c = 4 * cg + k
                                mm(sch, ps, ones_bf, sq[:, k, :], start=(c == 0), stop=(c == NCH - 1), r=[("sq", cg % 2)], w=[("ps", pi)])
                        rs = rstd_t[:, 0:tn]
                        act(sch, rs, ps, AF.Ln, r=[("ps", pi)], w=["rstd"], scale=1.0 / D, bias=EPS)
                        act(sch, rs, rs, AF.Exp, r=["rstd"], w=["rstd"], scale=-0.5)
                        for c8 in range(8):
                            c = 8 * half + c8
                            dstt(sch, YT[:, c8, t0:t0 + tn], XT[:, c, t0:t0 + tn], gcols[:, 64 + c:64 + c + 1], rs,
                                 ALU.mult, ALU.mult, r=[gXT(c, t), "rstd"], w=[("YT", c8, t)])
                    for i in range(9):
                        c0 = i * 128 if i < 8 else TO - 128
                        r0 = 0 if i < 8 else 64
                        rt = [("YT", c8_, t_) for c8_ in range(8) for t_ in ((i // 4,) if i < 8 else (1, 2))]
                        yb = yt[i % 2]
                        for b4 in range(2):
                            pi = ps_next()
                            for k in range(4):
                                c8 = 4 * b4 + k
                                tr(sch, psum[pi][:, k * 128:(k + 1) * 128], YT[:, c8, c0:c0 + 128], ident_f,
                                   r=[g for g in rt if g[1] == c8], w=[("ps", pi)])
                            if b4 == 0:
                                act(sch, yb[r0:128, 0:512], psum[pi][r0:128, :], AF.Copy, r=[("ps", pi)], w=[("yt", i % 2, b4)])
                            else:
                                dcopy(sch, yb[r0:128, 512:1024], psum[pi][r0:128, :], r=[("ps", pi)], w=[("yt", i % 2, b4)])
                        dma(sch, "sp", y_d[c0 + r0:c0 + 128, half * 1024:(half + 1) * 1024], yb[r0:128, 0:1024],
                            r=[("yt", i % 2, 0), ("yt", i % 2, 1)], w=[("y", i, half)])
                sch.barrier()

        dbg_t = {}

        def dbg(sch, name, ap, shape, dt=F32):
            if not DEBUG or sch.plan:
                return
            if name not in dbg_t:
                dbg_t[name] = nc.dram_tensor("dbg_" + name, list(shape), dt, kind="ExternalOutput").ap()
            sch.barrier()
            sch.op("sp", lambda e: e.dma_start(out=dbg_t[name], in_=ap), kind="d")
            sch.barrier()
        ctx["dbg"] = dbg

        def stop(tag, sch):
            if STOP_AFTER == tag:
                sch.barrier()
                raise _Stop()
        ctx["stop"] = stop
        try:
            build(PLAN)
        except _Stop:
            pass
        WS.planning = False
        try:
            build(S)
        except _Stop:
            pass

        run = S.emit(nc, None, esem, dsem, ccsem)

        @block.tensor
        def _(e):
            run("pe", e)

        @block.scalar
        def _(e):
            run("act", e)

        @block.vector
        def _(e):
            run("dve", e)

        @block.gpsimd
        def _(e):
            with e.register("r_off0") as r0, e.register("r_off1") as r1:
                e.reg_load(r0, metai_d[0:1, 0:1])
                e.reg_load(r1, metai_d[0:1, 1:2])
                ctx["off"] = [e.snap(r0), e.snap(r1)]
                run("pool", e)

        @block.sync
        def _(e):
            run("sp", e)

    return nc


_NC = None


def _make_in_maps(x_prompt, x_sample, cache_sb_k, cache_sb_v, cache_diff_k, cache_diff_v,
                  norm1_g, w_in, lambda_q1, lambda_k1, lambda_q2, lambda_k2, diff_norm_g,
                  w_branch_sb, w_branch_diff, w_out, norm2_g, w_up, w_down, final_norm_g):
    f = lambda a: np.ascontiguousarray(np.asarray(a, dtype=np.float32))
    x_prompt, x_sample = f(x_prompt), f(x_sample)
    w_in = f(w_in)
    w_my = []
    for hh in range(2):
        cols = []
        for sec in range(6):
            cols.append(w_in[:, :, sec * 1024 + hh * 512: sec * 1024 + (hh + 1) * 512])
        w_my.append(np.ascontiguousarray(np.concatenate(cols, axis=2)))
    gc = np.stack([f(norm1_g)[0], f(norm1_g)[1], f(norm2_g)[0], f(norm2_g)[1], f(final_norm_g)], 0)
    gcols = np.ascontiguousarray(gc.reshape(5, 16, 128).transpose(2, 0, 1).reshape(128, 80))
    dng = np.ascontiguousarray(f(diff_norm_g).T)
    lv = np.stack([f(lambda_q1), f(lambda_k1), f(lambda_q2), f(lambda_k2)], 1)
    lamv = np.ascontiguousarray(np.broadcast_to(lv.reshape(1, 512), (128, 512)))
    shared = dict(w_in=w_in, w_bsb=f(w_branch_sb), w_bdf=f(w_branch_diff), w_out=f(w_out), w_up=f(w_up), w_down=f(w_down),
                  gcols=gcols, dng=dng, lamv=lamv)
    cs = [f(cache_sb_k), f(cache_sb_v), f(cache_diff_k), f(cache_diff_v)]
    in_maps = []
    for c in range(8):
        b, hh = c // 2, c % 2
        m = dict(shared)
        m["xin"] = np.ascontiguousarray(np.concatenate([x_prompt[b, hh * 1024:(hh + 1) * 1024], x_sample[c]], 0))
        m["w_my"] = w_my[hh]
        for nme, arr in zip(("c_sbk", "c_sbv", "c_dfk", "c_dfv"), cs):
            m[nme] = np.ascontiguousarray(arr[:, c].reshape(DEPTH, PAST, 1024))
        m["meta_f"] = np.full((128, 1), float(hh), np.float32)
        m["meta_i"] = np.array([[hh * 512, hh * 512 + 1024]], np.int32)
        in_maps.append(m)
    return in_maps


def _assemble(R):
    y_prompt = np.empty((4, SEQ, D), np.float32)
    y_sample = np.empty((8, TS, D), np.float32)
    pk = [np.empty((DEPTH, 4, SEQ, 8, 128), np.float32) for _ in range(4)]
    sk = [np.empty((DEPTH, 8, TS, 8, 128), np.float32) for _ in range(4)]
    for c in range(8):
        b, hh = c // 2, c % 2
        y = R[c]["y"]
        y_prompt[b, hh * 1024:(hh + 1) * 1024] = y[:TP]
        y_sample[c] = y[TP:]
        for i, nme in enumerate(("p_sbk", "p_sbv", "p_dfk", "p_dfv")):
            pk[i][:, b, :, 4 * hh:4 * hh + 4, :] = R[c][nme].reshape(DEPTH, SEQ, 4, 128)
        for i, nme in enumerate(("s_sbk", "s_sbv", "s_dfk", "s_dfv")):
            sk[i][:, c] = R[c][nme].reshape(DEPTH, TS, 8, 128)
    return (y_prompt, y_sample, pk[0], pk[1], pk[2], pk[3], sk[0], sk[1], sk[2], sk[3])


def kernel(**inputs):
    global _NC
    if _NC is None:
        _NC = build_program()
    in_maps = _make_in_maps(**inputs)
    res = run_bass_kernel_spmd(_NC, in_maps, core_ids=list(range(8)))
    return _assemble(res.results)
```
